# Optimizing a Trainium2 kernel written in Bass

```python
import math
import jax
import jax.numpy as jnp
from jax import lax
import numpy as np

D_MODEL = 2048
BATCH = 4
SEQ = 4096
DEPTH = 1
DEC_BATCH = 8
DEC_SEQ = 32
PAST_LEN = 4096

CHUNK = 64
H_A = 8
DK_A = 128
DV_A = 128
W_A = H_A * DV_A
CONV_W = 4
CONV_CH = 2 * H_A * DK_A + W_A
H_B = 16
N_B = 64
W_B = H_B * N_B
LORA_W = 96
LORA_A = 96
LORA_G = 256
SHIFT_W = 3 * W_B + LORA_W + LORA_A + LORA_G
D_FF = 4 * D_MODEL
D_PLE = 256
N_IN = CONV_CH + 2 * H_A + W_A + SHIFT_W + 2 * D_MODEL
EPS = 1e-6
LNX_EPS = 64e-5

kernel_name = 'hybrid_gdn_rwkv7_streaming_step'

F32 = jnp.float32


def _split(x, sizes):
    idx = [int(s) for s in np.cumsum(sizes)[:-1]]
    return jnp.split(x, idx, axis=-1)


def rmsnorm(x, g, eps=EPS):
    xf = x.astype(F32)
    y = xf * lax.rsqrt(jnp.mean(xf * xf, axis=-1, keepdims=True) + eps)
    return (y * g.astype(F32)).astype(x.dtype)


def l2norm(x):
    xf = x.astype(F32)
    return xf * lax.rsqrt(jnp.sum(xf * xf, axis=-1, keepdims=True) + EPS)


def causal_conv_silu(u, buf, w):
    T = u.shape[1]
    full = jnp.concatenate([buf.astype(u.dtype), u], axis=1)
    out = full[:, 0:T] * w[0]
    for j in range(1, CONV_W):
        out = out + full[:, j:j + T] * w[j]
    return jax.nn.silu(out), full[:, T:]


def token_shift(u, prev):
    shifted = jnp.concatenate([prev[:, None].astype(u.dtype), u[:, :-1]], axis=1)
    return shifted, u[:, -1]


def gated_delta_rule(q, k, v, g, beta, s0):
    B, T, H, _ = q.shape
    c = min(CHUNK, T)
    n = -(-T // c)
    pad = n * c - T

    def blocks(a):
        a = a.astype(F32)
        a = jnp.pad(a, [(0, 0), (0, pad)] + [(0, 0)] * (a.ndim - 2))
        a = a.reshape((B, n, c) + a.shape[2:])
        return jnp.moveaxis(a, 3, 1)

    qb, kb, vb, gb, bb = blocks(q), blocks(k), blocks(v), blocks(g), blocks(beta)
    G = jnp.cumsum(gb, axis=-1)
    idx = jnp.arange(c)
    incl = idx[:, None] >= idx[None, :]
    strict = idx[:, None] > idx[None, :]
    decay = jnp.exp(jnp.where(incl, G[..., :, None] - G[..., None, :], -jnp.inf))
    kbeta = kb * bb[..., None]
    a_mat = jnp.where(strict, jnp.einsum('bhnik,bhnjk->bhnij', kbeta, kb) * decay, 0.0)
    eye = jnp.eye(c, dtype=F32)
    t_inv = lax.linalg.triangular_solve(eye + a_mat, jnp.broadcast_to(eye, a_mat.shape),
                                        left_side=True, lower=True)
    eG = jnp.exp(G)[..., None]
    u = jnp.einsum('bhnij,bhnjd->bhnid', t_inv, vb * bb[..., None])
    w = jnp.einsum('bhnij,bhnjk->bhnik', t_inv, kbeta * eG)
    qk = jnp.where(incl, jnp.einsum('bhnik,bhnjk->bhnij', qb, kb) * decay, 0.0)
    q_dec = qb * eG
    g_last = G[..., -1]
    k_dec = kb * jnp.exp(g_last[..., None] - G)[..., None]

    def step(s, xs):
        u_c, w_c, qk_c, q_c, k_c, gl = xs
        v_new = u_c - jnp.einsum('bhik,bhkd->bhid', w_c, s)
        o = jnp.einsum('bhik,bhkd->bhid', q_c, s) + jnp.einsum('bhij,bhjd->bhid', qk_c, v_new)
        s = s * jnp.exp(gl)[..., None, None] + jnp.einsum('bhik,bhid->bhkd', k_c, v_new)
        return s, o

    xs = tuple(jnp.moveaxis(a, 2, 0) for a in (u, w, qk, q_dec, k_dec, g_last))
    s, o = lax.scan(step, s0.astype(F32), xs)
    o = jnp.transpose(o, (1, 0, 3, 2, 4)).reshape(B, n * c, H, -1)[:, :T]
    return o, s


def rwkv7_recurrence(r, decay, k, v, kk, a, s0):
    def step(s, xs):
        r_t, w_t, k_t, v_t, kk_t, a_t = xs
        sa = jnp.einsum('bhvk,bhk->bhv', s, -kk_t)
        s = (s * w_t[:, :, None, :] + sa[..., None] * (kk_t * a_t)[:, :, None, :]
             + v_t[..., None] * k_t[:, :, None, :])
        return s, jnp.einsum('bhvk,bhk->bhv', s, r_t)

    xs = tuple(jnp.moveaxis(t.astype(F32), 1, 0) for t in (r, decay, k, v, kk, a))
    s, y = lax.scan(step, s0.astype(F32), xs)
    return jnp.moveaxis(y, 0, 1), s


def gdn_branch(conv_in, alpha_logit, beta_logit, gate, buf, s0, conv_w, a_log, dt_bias, norm_g):
    B, T, _ = conv_in.shape
    c_out, new_buf = causal_conv_silu(conv_in, buf, conv_w)
    q, k, v = _split(c_out, [H_A * DK_A, H_A * DK_A, W_A])
    q = l2norm(q.reshape(B, T, H_A, DK_A)) * (DK_A ** -0.5)
    k = l2norm(k.reshape(B, T, H_A, DK_A))
    v = v.reshape(B, T, H_A, DV_A)
    beta = jax.nn.sigmoid(beta_logit.astype(F32))
    g = -jnp.exp(a_log.astype(F32)) * jax.nn.softplus(alpha_logit.astype(F32) + dt_bias.astype(F32))
    o, s = gated_delta_rule(q, k, v, g, beta, s0)
    o = rmsnorm(o, norm_g) * jax.nn.silu(gate.astype(F32)).reshape(B, T, H_A, DV_A)
    return o.reshape(B, T, W_A).astype(conv_in.dtype), s, new_buf


def rwkv_branch(xb, prev, s0, mu, w0, w2, a0, a2, g2, k_k, k_a, r_k, lnx_g, lnx_b):
    B, T, _ = xb.shape
    shifted, last = token_shift(xb, prev)
    xm = xb + (shifted - xb) * mu
    r, k, v, xw, xa, xg = _split(xm, [W_B, W_B, W_B, LORA_W, LORA_A, LORA_G])
    w_log = -jax.nn.softplus(-(w0 + jnp.tanh(xw) @ w2).astype(F32)) - 0.5
    decay = jnp.exp(-jnp.exp(w_log))
    a = jax.nn.sigmoid((a0 + xa @ a2).astype(F32))
    gate = (jax.nn.sigmoid(xg) @ g2).astype(F32)

    def heads(t):
        return t.astype(F32).reshape(B, T, H_B, N_B)

    rh, vh, ah = heads(r), heads(v), heads(a)
    kk = l2norm(heads(k * k_k))
    kh = heads(k.astype(F32) * (1.0 + (a - 1.0) * k_a.astype(F32)))
    y, s = rwkv7_recurrence(rh, heads(decay), kh, vh, kk, ah, s0)
    mean = jnp.mean(y, axis=-1, keepdims=True)
    yc = y - mean
    y = yc * lax.rsqrt(jnp.mean(yc * yc, axis=-1, keepdims=True) + LNX_EPS)
    y = y.reshape(B, T, W_B) * lnx_g.astype(F32) + lnx_b.astype(F32)
    bonus = jnp.sum(rh * kh * r_k.astype(F32), axis=-1, keepdims=True) * vh
    y = (y + bonus.reshape(B, T, W_B)) * gate
    return y.astype(xb.dtype), s, last


def hybrid_layer(x, ple, s_gdn, buf_gdn, s_rwkv, shift_rwkv, lw):
    (g_mix, w_in, conv_w, a_log, dt_bias, gdn_norm_g, mu_shift, w0, w2, a0, a2, g2,
     k_k, k_a, r_k, lnx_g, lnx_b, w_up_a, w_up_b, w_o, g_mlp, w_ff1, w_ff2,
     g_ple, w_ple_gate, w_ple) = lw
    h = rmsnorm(x, g_mix)
    proj = h @ w_in
    conv_in, alpha_logit, beta_logit, gate_a, xb, gate_ma, gate_mb = _split(
        proj, [CONV_CH, H_A, H_A, W_A, SHIFT_W, D_MODEL, D_MODEL])
    o_a, s_gdn, buf_gdn = gdn_branch(conv_in, alpha_logit, beta_logit, gate_a, buf_gdn, s_gdn,
                                     conv_w, a_log, dt_bias, gdn_norm_g)
    o_b, s_rwkv, shift_rwkv = rwkv_branch(xb, shift_rwkv, s_rwkv, mu_shift, w0, w2, a0, a2, g2,
                                          k_k, k_a, r_k, lnx_g, lnx_b)
    merged = jax.nn.sigmoid(gate_ma) * (o_a @ w_up_a) + jax.nn.sigmoid(gate_mb) * (o_b @ w_up_b)
    x = x + merged @ w_o
    h = rmsnorm(x, g_mlp)
    x = x + jnp.square(jax.nn.relu(h @ w_ff1)) @ w_ff2
    x = x + jax.nn.sigmoid(rmsnorm(x, g_ple) @ w_ple_gate) * (ple @ w_ple)
    return x, s_gdn, buf_gdn, s_rwkv, shift_rwkv


def run_group(x, p, s_gdn, buf_gdn, s_rwkv, shift_rwkv, params, g_final):
    out_sg, out_buf, out_sr, out_sh = [], [], [], []
    for i in range(DEPTH):
        lw = tuple(w[i] for w in params)
        x, sg, bf, sr, sh = hybrid_layer(x, p[i], s_gdn[i], buf_gdn[i], s_rwkv[i], shift_rwkv[i], lw)
        out_sg.append(sg.astype(s_gdn.dtype))
        out_buf.append(bf.astype(buf_gdn.dtype))
        out_sr.append(sr.astype(s_rwkv.dtype))
        out_sh.append(sh.astype(shift_rwkv.dtype))
    y = rmsnorm(x, g_final)
    return y, jnp.stack(out_sg), jnp.stack(out_buf), jnp.stack(out_sr), jnp.stack(out_sh)


def setup_inputs(seed: int = 0) -> dict:
    key = jax.random.key(seed)
    ks = jax.random.split(key, 40)
    L = DEPTH

    def nrm(k, shape, scale):
        return jax.random.normal(k, shape, F32) * scale

    dt = jnp.exp(jax.random.uniform(ks[10], (L, H_A), F32, math.log(1e-3), math.log(1e-1)))
    return {
        'x_prompt': nrm(ks[0], (BATCH, SEQ, D_MODEL), 1.0),
        'x_sample': nrm(ks[1], (DEC_BATCH, DEC_SEQ, D_MODEL), 1.0),
        'p_prompt': nrm(ks[2], (DEPTH, BATCH, SEQ, D_PLE), 1.0),
        'p_sample': nrm(ks[3], (DEPTH, DEC_BATCH, DEC_SEQ, D_PLE), 1.0),
        'state_gdn': nrm(ks[4], (L, DEC_BATCH, H_A, DK_A, DV_A), 0.1),
        'cache_gdn_conv': nrm(ks[5], (L, DEC_BATCH, CONV_W - 1, CONV_CH), 1.0),
        'state_rwkv': nrm(ks[6], (L, DEC_BATCH, H_B, N_B, N_B), 0.1),
        'cache_rwkv_shift': nrm(ks[7], (L, DEC_BATCH, SHIFT_W), 1.0),
        'g_mix': 1.0 + nrm(ks[8], (L, D_MODEL), 0.02),
        'w_in': nrm(ks[9], (L, D_MODEL, N_IN), D_MODEL ** -0.5),
        'conv_w': nrm(ks[11], (L, CONV_W, CONV_CH), CONV_W ** -0.5),
        'a_log': jnp.log(jax.random.uniform(ks[12], (L, H_A), F32, 1.0, 16.0)),
        'dt_bias': dt + jnp.log(-jnp.expm1(-dt)),
        'gdn_norm_g': 1.0 + nrm(ks[13], (L, DV_A), 0.02),
        'mu_shift': jax.random.uniform(ks[14], (L, SHIFT_W), F32),
        'w0': jax.random.uniform(ks[15], (L, W_B), F32, -6.0, -1.0),
        'w2': nrm(ks[16], (L, LORA_W, W_B), 0.5 * LORA_W ** -0.5),
        'a0': nrm(ks[17], (L, W_B), 0.1),
        'a2': nrm(ks[18], (L, LORA_A, W_B), 0.5 * LORA_A ** -0.5),
        'g2': nrm(ks[19], (L, LORA_G, W_B), LORA_G ** -0.5),
        'k_k': 0.85 + nrm(ks[20], (L, W_B), 0.02),
        'k_a': 1.0 + nrm(ks[21], (L, W_B), 0.02),
        'r_k': nrm(ks[22], (L, H_B, N_B), 0.1),
        'lnx_g': 1.0 + nrm(ks[23], (L, W_B), 0.02),
        'lnx_b': nrm(ks[24], (L, W_B), 0.02),
        'w_up_a': nrm(ks[25], (L, W_A, D_MODEL), W_A ** -0.5),
        'w_up_b': nrm(ks[26], (L, W_B, D_MODEL), W_B ** -0.5),
        'w_o': nrm(ks[27], (L, D_MODEL, D_MODEL), D_MODEL ** -0.5),
        'g_mlp': 1.0 + nrm(ks[28], (L, D_MODEL), 0.02),
        'w_ff1': nrm(ks[29], (L, D_MODEL, D_FF), D_MODEL ** -0.5),
        'w_ff2': nrm(ks[30], (L, D_FF, D_MODEL), D_FF ** -0.5),
        'g_ple': 1.0 + nrm(ks[31], (L, D_MODEL), 0.02),
        'w_ple_gate': nrm(ks[32], (L, D_MODEL, D_MODEL), D_MODEL ** -0.5),
        'w_ple': nrm(ks[33], (L, D_PLE, D_MODEL), D_PLE ** -0.5),
        'g_final': 1.0 + nrm(ks[34], (D_MODEL,), 0.02),
    }


def reference(x_prompt, x_sample, p_prompt, p_sample, state_gdn, cache_gdn_conv, state_rwkv,
              cache_rwkv_shift, g_mix, w_in, conv_w, a_log, dt_bias, gdn_norm_g, mu_shift, w0, w2,
              a0, a2, g2, k_k, k_a, r_k, lnx_g, lnx_b, w_up_a, w_up_b, w_o, g_mlp, w_ff1, w_ff2,
              g_ple, w_ple_gate, w_ple, g_final):
    params = (g_mix, w_in, conv_w, a_log, dt_bias, gdn_norm_g, mu_shift, w0, w2, a0, a2, g2,
              k_k, k_a, r_k, lnx_g, lnx_b, w_up_a, w_up_b, w_o, g_mlp, w_ff1, w_ff2,
              g_ple, w_ple_gate, w_ple)
    bp = x_prompt.shape[0]
    dt_p = x_prompt.dtype
    z_sg = jnp.zeros((DEPTH, bp, H_A, DK_A, DV_A), dt_p)
    z_buf = jnp.zeros((DEPTH, bp, CONV_W - 1, CONV_CH), dt_p)
    z_sr = jnp.zeros((DEPTH, bp, H_B, N_B, N_B), dt_p)
    z_sh = jnp.zeros((DEPTH, bp, SHIFT_W), dt_p)
    y_prompt, sg_p, buf_p, sr_p, sh_p = run_group(x_prompt, p_prompt, z_sg, z_buf, z_sr, z_sh,
                                                  params, g_final)
    y_sample, sg_s, buf_s, sr_s, sh_s = run_group(x_sample, p_sample, state_gdn, cache_gdn_conv,
                                                  state_rwkv, cache_rwkv_shift, params, g_final)
    return (y_prompt, y_sample, sg_p, buf_p, sr_p, sh_p, sg_s, buf_s, sr_s, sh_s)
```

```python
import contextlib
import numpy as np
import concourse.bass as bass
import concourse.mybir as mybir
from concourse.alu_op_type import AluOpType as ALU
from concourse.bass_utils import run_bass_kernel_spmd

F32 = mybir.dt.float32
BF16 = mybir.dt.bfloat16
AF = mybir.ActivationFunctionType

D = 2048
NKC = 16
H_A = 8
H_B = 16
NB_B = 8
CONV_CH = 3072
SHIFT_W = 3520
D_FF = 8192
EPS = 1e-6
LNX_EPS = 64e-5
SEM_LIM = 12000
DMA_LIM = 1500


class Buf:
    __slots__ = ("w", "r", "excl")

    def __init__(self, init=None, excl=False):
        self.w = dict(init) if init else {}
        self.r = {}
        self.excl = excl


class Op:
    __slots__ = ("eng", "stream", "fn", "waits", "idx", "signal", "is_dma", "vc", "batch")


ENGS = ("pe", "act", "dve", "pool", "sp")


class Sched:
    def __init__(self, nc):
        self.nc = nc
        self.ops = {e: [] for e in ENGS}
        self.stream_ops = {}
        self.stack = contextlib.ExitStack()
        self.n_t = 0
        self.clock = {e: {} for e in ENGS}
        self.batches = {}
        self.batch_closed = {}
        self.fence_next = False
        self.pe_kind = None

    def sbuf(self, shape, dtype, name=None):
        self.n_t += 1
        return self.stack.enter_context(self.nc.sbuf_tensor("sb_" + (name or f"t{self.n_t}"), list(shape), dtype))

    def psum(self, shape, dtype, name=None):
        self.n_t += 1
        return self.stack.enter_context(self.nc.psum_tensor(name or f"p{self.n_t}", list(shape), dtype))

    def barrier_state(self, exclude=()):
        return {s: len(sl) - 1 for s, sl in self.stream_ops.items() if sl and s not in exclude}

    @staticmethod
    def _flat(bs):
        out = []
        for b in bs:
            if isinstance(b, (list, tuple)):
                out.extend(Sched._flat(b))
            else:
                out.append(b)
        return out

    def op(self, eng, fn, reads=(), writes=(), dma=None, fence=False):
        reads = self._flat(reads)
        writes = self._flat(writes)
        o = Op()
        o.eng = eng
        o.is_dma = dma is not None
        o.stream = dma if o.is_dma else eng
        o.fn = fn
        o.signal = o.is_dma
        sl = self.stream_ops.setdefault(o.stream, [])
        o.idx = len(sl)
        need = {}
        same = o.stream
        isd = o.is_dma

        def req(s, i):
            if s in self.batches:
                bl = self.batches[s]
                bi = self.stream_ops[s][i].batch
                if bi == len(bl) - 1:
                    self.batch_closed[s] = True
                    i = len(self.stream_ops[s]) - 1
                else:
                    i = bl[bi + 1] - 1
            if need.get(s, -1) < i:
                need[s] = i

        if isd:
            bl = self.batches.setdefault(o.stream, [])
            if not bl or self.batch_closed.get(o.stream, False):
                if bl:
                    req(o.stream, o.idx - 1)
                bl.append(o.idx)
                self.batch_closed[o.stream] = False
            o.batch = len(bl) - 1
        if eng == "pe" and not isd:
            if fence and sl:
                req("pe", len(sl) - 1)
        for b in reads:
            for s, i in b.w.items():
                req(s, i)
            if b.excl:
                for s, i in b.r.items():
                    if s != same:
                        req(s, i)
        pe_self = (same == "pe")
        for b in writes:
            for s, i in b.w.items():
                if not (pe_self and s == same):
                    req(s, i)
            for s, i in b.r.items():
                req(s, i)
        clk = self.clock[eng]
        waits = []
        for s, i in need.items():
            if clk.get(s, -1) >= i:
                continue
            waits.append((s, i))
            p = self.stream_ops[s][i]
            p.signal = True
            for s2, i2 in p.vc.items():
                if clk.get(s2, -1) < i2:
                    clk[s2] = i2
            if clk.get(s, -1) < i:
                clk[s] = i
        o.waits = waits
        vc = dict(clk)
        vc[o.stream] = o.idx
        o.vc = vc
        sl.append(o)
        self.ops[eng].append(o)
        for b in reads:
            b.r[o.stream] = o.idx
        for b in writes:
            b.w = {o.stream: o.idx}
            b.r = {}
        return o

    def emit(self, final_streams=()):
        nc = self.nc
        sig = {}
        n_ep = {}
        for s, sl in self.stream_ops.items():
            if not sl:
                continue
            if sl[0].is_dma:
                bl = self.batches[s] + [len(sl)]
                ep, cnt = 0, 0
                for bi in range(len(bl) - 1):
                    size = bl[bi + 1] - bl[bi]
                    if cnt + size > DMA_LIM:
                        ep += 1
                        cnt = 0
                    for i in range(bl[bi], bl[bi + 1]):
                        cnt += 1
                        sig[(s, i)] = (ep, cnt * 16)
                n_ep[s] = ep + 1
            else:
                n = 0
                for o in sl:
                    if o.signal:
                        ep, v = divmod(n, SEM_LIM)
                        sig[(s, o.idx)] = (ep, v + 1)
                        n += 1
                n_ep[s] = (n + SEM_LIM - 1) // SEM_LIM
        sems = {}
        for s, k in n_ep.items():
            sems[s] = [self.stack.enter_context(nc.semaphore(f"s_{s}_{e}")) for e in range(k)]
        self.n_sems = sum(n_ep.values())
        block = self.stack.enter_context(nc.Block())

        def run_engine(eng_name):
            def body(e):
                for o in self.ops[eng_name]:
                    for (s, i) in o.waits:
                        ep, v = sig[(s, i)]
                        e.wait_ge(sems[s][ep], v)
                    ins = o.fn(e)
                    if o.signal:
                        ep, v = sig[(o.stream, o.idx)]
                        ins.then_inc(sems[o.stream][ep], 16 if o.is_dma else 1)
                for s in final_streams:
                    sl = self.stream_ops.get(s)
                    if sl and sl[0].eng == eng_name:
                        ep, v = sig[(s, len(sl) - 1)]
                        e.wait_ge(sems[s][ep], v)
            return body

        block.tensor(run_engine("pe"))
        block.scalar(run_engine("act"))
        block.vector(run_engine("dve"))
        block.gpsimd(run_engine("pool"))
        block.sync(run_engine("sp"))
        self.stack.close()


class TL:
    __slots__ = ("t", "b")

    def __init__(self, t, b=None):
        self.t = t
        self.b = b if b is not None else Buf()


class Builder:
    def __init__(self, NT, n_pre, n_main, with_sample=True, debug=()):
        self.NT = NT
        self.n_pre = n_pre
        self.n_main = n_main
        self.with_sample = with_sample
        self.debug = set(debug)
        self.nc = bass.Bass("TRN2", target_bir_lowering=False)
        self.S = Sched(self.nc)
        self.dbg_out = {}
        self.arena_gen = None

    def tt(self, out, in0, in1, op, R, W, eng="dve"):
        return self.S.op(eng, lambda e: e.tensor_tensor(out=out, in0=in0, in1=in1, op=op), reads=R, writes=W)

    def ts(self, out, in0, s1, s2, op0, op1, R, W, eng="dve"):
        if op1 is None:
            return self.S.op(eng, lambda e: e.tensor_scalar(out=out, in0=in0, scalar1=s1, scalar2=None, op0=op0),
                             reads=R, writes=W)
        return self.S.op(eng, lambda e: e.tensor_scalar(out=out, in0=in0, scalar1=s1, scalar2=s2, op0=op0, op1=op1),
                         reads=R, writes=W)

    def stt(self, out, in0, scalar, in1, op0, op1, R, W):
        return self.S.op("dve", lambda e: e.scalar_tensor_tensor(out=out, in0=in0, scalar=scalar, in1=in1,
                                                                op0=op0, op1=op1), reads=R, writes=W)

    def act(self, out, in_, func, R, W, bias=None, scale=None):
        kw = {}
        if bias is not None:
            kw["bias"] = bias
        if scale is not None:
            kw["scale"] = scale
        return self.S.op("act", lambda e: e.activation(out=out, in_=in_, func=func, **kw), reads=R, writes=W)

    def cp(self, eng, out, in_, R, W):
        if eng == "act":
            return self.S.op("act", lambda e: e.activation(out=out, in_=in_, func=AF.Copy), reads=R, writes=W)
        return self.S.op(eng, lambda e: e.tensor_copy(out=out, in_=in_), reads=R, writes=W)

    def mm(self, out, lhsT, rhs, start, stop, R, W, fence=False):
        kind = "bf" if lhsT.dtype == BF16 else "f32"
        if kind != self.S.pe_kind:
            fence = True
            self.S.pe_kind = kind
        return self.S.op("pe", lambda e: e.matmul(out, lhsT=lhsT, rhs=rhs, start=start, stop=stop), reads=R, writes=W,
                         fence=fence)

    def tr(self, out, in_, R, W):
        n = in_.shape[0]
        idn = self.ident.t[0:n, 0:n]
        fence = self.S.pe_kind != "f32"
        self.S.pe_kind = "f32"
        return self.S.op("pe", lambda e: e.transpose(out, in_, idn), reads=list(R) + [self.ident.b], writes=W,
                         fence=fence)

    def dma(self, eng, stream, out, in_, R, W, slow=False):
        if slow:
            return self.S.op(eng, lambda e: e.dma_start(out=out, in_=in_, allow_slow_non_contiguous=True),
                             reads=R, writes=W, dma=stream)
        return self.S.op(eng, lambda e: e.dma_start(out=out, in_=in_), reads=R, writes=W, dma=stream)

    def memset(self, ap, val, W, eng="pool"):
        return self.S.op(eng, lambda e: e.memset(ap, val), writes=W)

    def rsqrt(self, out, in_, scale, eps, R, W):
        self.act(out, in_, AF.Ln, R, W, bias=float(eps), scale=float(scale))
        self.act(out, out, AF.Exp, W, W, scale=-0.5)

    def init_psum(self):
        self.pdb = [self.S.psum([128, 1024], F32, name=f"psdb{i}") for i in range(4)]
        self.pbuf = [Buf(excl=True) for _ in range(8)]
        self.prr = 0
        self.psg = None
        self.prg = [0, 0]

    def ps1(self):
        g = self.psg
        if g is None:
            i = self.prr % 8
            self.prr += 1
        else:
            i = 4 * g + self.prg[g] % 4
            self.prg[g] += 1
        return TL(self.pdb[i // 2][:, (i % 2) * 512:(i % 2) * 512 + 512], self.pbuf[i])

    def ps2(self):
        g = self.psg
        if g is None:
            if self.prr % 2:
                self.prr += 1
            i = self.prr % 8
            self.prr += 2
        else:
            if self.prg[g] % 2:
                self.prg[g] += 1
            i = 4 * g + self.prg[g] % 4
            self.prg[g] += 2
        return self.pdb[i // 2], [self.pbuf[i], self.pbuf[i + 1]]

    def init_slabs(self, nbuf=3):
        self.slabs = [TL(self.S.sbuf([128, 16, 512], BF16, name=f"slab{i}")) for i in range(nbuf)]
        self.slab_i = 0

    def load_slab(self, pieces):
        bi = self.slab_i % len(self.slabs)
        sl = self.slabs[bi]
        self.slab_i += 1
        for (src, kc0, c0) in pieces:
            rows, cols = src.shape
            kc = rows // 128
            if rows % 128 == 0:
                self.dma("pool", f"slab{bi}", sl.t[:, kc0:kc0 + kc, c0:c0 + cols],
                         src.rearrange("(k p) c -> p k c", p=128), [], [sl.b])
            else:
                assert rows < 128
                self.dma("pool", f"slab{bi}", sl.t[0:rows, kc0, c0:c0 + cols], src, [], [sl.b])
        return sl

    def declare_io(self):
        nc = self.nc
        NT, n_pre, n_main = self.NT, self.n_pre, self.n_main
        I = {}

        def inp(name, shape):
            I[name] = nc.dram_tensor(name, list(shape), F32, kind="ExternalInput").ap()

        def outp(name, shape):
            I[name] = nc.dram_tensor(name, list(shape), F32, kind="ExternalOutput").ap()

        if n_pre:
            inp("xT_pre", [D, n_pre * NT])
        inp("xT_main", [D, n_main * NT])
        inp("pT_main", [256, n_main * NT])
        outp("yT_main", [D, n_main * NT])
        for g in (["p", "s"] if self.with_sample else ["p"]):
            outp(f"sg_{g}", [128, 8, 128])
            outp(f"conv_{g}", [128, 24, 3])
            outp(f"sr_{g}", [128, 8, 64])
            outp(f"sh_{g}", [128, 28])
        if self.with_sample:
            inp("xT_s", [D, 32])
            inp("pT_s", [256, 32])
            outp("yT_s", [D, 32])
            inp("sg_in", [128, 8, 128])
            inp("conv_in", [128, 24, 3])
            inp("sr_in", [128, 8, 64])
            inp("sh_in", [128, 28])
        inp("w_gdn", [D, 8, 512])
        inp("w_ab", [D, 16])
        inp("w_rw", [D, 8, 384])
        inp("w_lora", [D, 448])
        inp("w_gm", [D, 2, 2048])
        inp("w_up_a", [1024, D])
        inp("w_up_b", [1024, D])
        inp("w_o", [D, D])
        inp("w_ff1", [D, D_FF])
        inp("w_ff2", [D_FF, D])
        inp("w_ple_gate", [D, D])
        inp("w_ple", [256, D])
        inp("w2", [96, 1024])
        inp("a2", [96, 1024])
        inp("g2", [256, 1024])
        inp("gvec", [128, 4, 16])
        inp("convw", [128, 24, 4])
        inp("mu", [128, 28])
        inp("rwp", [128, 7, 8])
        inp("gng", [128, 1])
        inp("adt", [8, 2])
        self.I = I
        for name in self.debug:
            pass

    def setup_consts(self):
        S, I = self.S, self.I
        self.ident = TL(S.sbuf([128, 128], F32, name="ident"))
        self.memset(self.ident.t[:], 0.0, [self.ident.b])
        S.op("pool", lambda e: e.affine_select(out=self.ident.t[:], in_=self.ident.t[:], pattern=[[-1, 128]],
                                               compare_op=ALU.not_equal, fill=1.0, base=0, channel_multiplier=1),
             reads=[self.ident.b], writes=[self.ident.b])
        self.ones = TL(S.sbuf([128, 128], F32, name="ones"))
        self.memset(self.ones.t[:], 1.0, [self.ones.b])
        self.onesbd = TL(S.sbuf([128, 128], F32, name="onesbd"))
        self.memset(self.onesbd.t[:], 0.0, [self.onesbd.b])
        self.memset(self.onesbd.t[0:64, 0:64], 1.0, [self.onesbd.b])
        self.memset(self.onesbd.t[64:128, 64:128], 1.0, [self.onesbd.b])
        self.sel8 = TL(S.sbuf([8, 8, 128], F32, name="sel8"))
        self.memset(self.sel8.t[:], 0.0, [self.sel8.b])
        S.op("pool", lambda e: e.affine_select(out=self.sel8.t[:], in_=self.sel8.t[:], pattern=[[-1, 8], [0, 128]],
                                               compare_op=ALU.not_equal, fill=1.0, base=0, channel_multiplier=1),
             reads=[self.sel8.b], writes=[self.sel8.b])
        self.masks = TL(S.sbuf([64, 4, 64], F32, name="masks"))
        mk = self.masks
        self.memset(mk.t[:], 1.0, [mk.b])
        S.op("pool", lambda e: e.affine_select(out=mk.t[:, 0, :], in_=mk.t[:, 0, :], pattern=[[1, 64]],
                                               compare_op=ALU.is_ge, fill=0.0, base=0, channel_multiplier=-1),
             reads=[mk.b], writes=[mk.b])
        S.op("pool", lambda e: e.affine_select(out=mk.t[:, 1, :], in_=mk.t[:, 1, :], pattern=[[1, 64]],
                                               compare_op=ALU.is_gt, fill=0.0, base=0, channel_multiplier=-1),
             reads=[mk.b], writes=[mk.b])
        S.op("pool", lambda e: e.affine_select(out=mk.t[:, 2, :], in_=mk.t[:, 2, :], pattern=[[-1, 64]],
                                               compare_op=ALU.is_gt, fill=0.0, base=0, channel_multiplier=1),
             reads=[mk.b], writes=[mk.b])
        S.op("pool", lambda e: e.tensor_scalar(out=mk.t[:, 3, :], in0=mk.t[:, 1, :], scalar1=-1.0, scalar2=None,
                                               op0=ALU.mult), reads=[mk.b], writes=[mk.b])
        def ld(name, shape):
            t = TL(S.sbuf(shape, F32, name="c_" + name))
            self.dma("sp", "par", t.t[:], I[name], [], [t.b])
            return t
        self.gvec = ld("gvec", [128, 4, 16])
        self.convw = ld("convw", [128, 24, 4])
        self.mu = ld("mu", [128, 28])
        self.rwp = ld("rwp", [128, 7, 8])
        self.gng = ld("gng", [128, 1])
        self.adt = ld("adt", [8, 2])
        self.omu = TL(S.sbuf([128, 28], F32, name="omu"))
        self.ts(self.omu.t[:], self.mu.t[:], -1.0, 1.0, ALU.mult, ALU.add, [self.mu.b], [self.omu.b])
        self.omka = TL(S.sbuf([128, 8], F32, name="omka"))
        self.ts(self.omka.t[:], self.rwp.t[:, 3, :], -1.0, 1.0, ALU.mult, ALU.add, [self.rwp.b], [self.omka.b])
        self.nA = TL(S.sbuf([8, 1], F32, name="nA"))
        self.act(self.nA.t[:], self.adt.t[:, 0:1], AF.Exp, [self.adt.b], [self.nA.b])
        self.ts(self.nA.t[:], self.nA.t[:], -1.0, None, ALU.mult, None, [self.nA.b], [self.nA.b])
        self.w2 = TL(S.sbuf([96, 1024], BF16, name="w2"))
        self.a2 = TL(S.sbuf([96, 1024], BF16, name="a2"))
        self.g2 = TL(S.sbuf([128, 2, 1024], BF16, name="g2"))
        self.dma("pool", "lora", self.w2.t[:], I["w2"], [], [self.w2.b])
        self.dma("pool", "lora", self.a2.t[:], I["a2"], [], [self.a2.b])
        self.dma("pool", "lora", self.g2.t[:], I["g2"].rearrange("(k p) c -> p k c", p=128), [], [self.g2.b])

    def alloc_state(self):
        S = self.S
        self.Sg = [TL(S.sbuf([128, 128], F32, name=f"Sg{h}")) for h in range(8)]
        self.Pr = [TL(S.sbuf([128, 128], F32, name=f"Pr{i}")) for i in range(8)]
        self.halo_c = TL(S.sbuf([128, 24, 3], F32, name="halo_c"))
        self.halo_x = TL(S.sbuf([128, 28], F32, name="halo_x"))

    def zero_state(self):
        for t in self.Sg + self.Pr:
            self.memset(t.t[:], 0.0, [t.b])
        self.memset(self.halo_c.t[:], 0.0, [self.halo_c.b])
        self.memset(self.halo_x.t[:], 0.0, [self.halo_x.b])

    def load_state(self):
        I = self.I
        for h in range(8):
            self.dma("sp", "stin", self.Sg[h].t[:], I["sg_in"][:, h, :], [], [self.Sg[h].b])
            self.memset(self.Pr[h].t[:], 0.0, [self.Pr[h].b])
            self.dma("sp", "stin", self.Pr[h].t[0:64, 0:64], I["sr_in"][0:64, h, :], [], [self.Pr[h].b])
            self.dma("sp", "stin", self.Pr[h].t[64:128, 64:128], I["sr_in"][64:128, h, :], [], [self.Pr[h].b])
        self.dma("sp", "stin", self.halo_c.t[:], I["conv_in"], [], [self.halo_c.b])
        self.dma("sp", "stin", self.halo_x.t[:], I["sh_in"], [], [self.halo_x.b])

    def store_state(self, g):
        I = self.I
        for h in range(8):
            self.dma("sp", "outq", I[f"sg_{g}"][:, h, :], self.Sg[h].t[:], [self.Sg[h].b], [])
            self.dma("sp", "outq", I[f"sr_{g}"][0:64, h, :], self.Pr[h].t[0:64, 0:64], [self.Pr[h].b], [])
            self.dma("sp", "outq", I[f"sr_{g}"][64:128, h, :], self.Pr[h].t[64:128, 64:128], [self.Pr[h].b], [])
        self.dma("sp", "outq", I[f"conv_{g}"], self.halo_c.t[:], [self.halo_c.b], [])
        self.dma("sp", "outq", I[f"sh_{g}"], self.halo_x.t[:], [self.halo_x.b], [])


    def alloc_work(self):
        S, NT = self.S, self.NT
        self.SW = NT + 4
        self.xt = S.sbuf([128, NKC, NT], F32, name="xt")
        self.xb = [Buf() for _ in range(NKC)]
        self.ht = S.sbuf([128, NKC, NT], BF16, name="ht")
        self.hb = [Buf() for _ in range(NKC)]
        self.oat = S.sbuf([128, 8, NT], BF16, name="oat")
        self.oab = [Buf() for _ in range(8)]
        self.obt = S.sbuf([128, 8, NT], BF16, name="obt")
        self.obb = [Buf() for _ in range(8)]
        self.mgt = S.sbuf([128, NKC, NT], BF16, name="mgt")
        self.mgb = [Buf() for _ in range(NKC)]
        self.pT = TL(S.sbuf([128, 2, NT], BF16, name="pT"))
        self.rs_t = TL(S.sbuf([128, NT], F32, name="rs_t"))
        import os
        self.NA = 78 if os.environ.get("KNOALIAS") else 71
        self.NSLOT = 78
        self.arena = S.sbuf([128, self.NA, self.SW], F32, name="arena")
        self.slot_bufs = [Buf() for _ in range(self.NA)]
        self.mg32 = self.mgt[:].rearrange("p k t -> p (k t)").bitcast(F32)
        assert (self.NSLOT - self.NA) * self.SW <= NKC * NT // 2
        order = list(range(0, 28)) + list(range(32, 48)) + [48, 49, 50, 51]
        self.RMAP = {o: 30 + r for r, o in enumerate(order)}
        self.RMAP[29], self.RMAP[30], self.RMAP[31] = self.RMAP[1], self.RMAP[2], self.RMAP[3]
        self.rmask = {}
        for C in (64, 32):
            t = TL(S.sbuf([128, NT], F32, name=f"rmask{C}"))
            self.memset(t.t[:], 1.0, [t.b])
            self.memset(t.t[:].rearrange("p (c j) -> p c j", j=C)[:, :, 0:1], 0.0, [t.b])
            self.rmask[C] = t
        self.g8 = TL(S.sbuf([8, NT], F32, name="g8"))
        self.G8 = TL(S.sbuf([8, NT], F32, name="G8"))
        self.b8 = TL(S.sbuf([8, NT], F32, name="b8"))
        mc = NT // 32 * 8
        self.Gcol = TL(S.sbuf([64, mc], F32, name="Gcol"))
        self.bcol = TL(S.sbuf([64, mc], F32, name="bcol"))
        self.nbcol = TL(S.sbuf([64, mc], F32, name="nbcol"))
        self.nbeg = TL(S.sbuf([64, mc], F32, name="nbeg"))
        self.kdc = [TL(S.sbuf([64, NT // 32], F32, name=f"kdc{i}")) for i in range(2)]
        self.rv = [TL(S.sbuf([64, 128], F32, name=f"rv{i}")) for i in range(2)]
        self.vn = [TL(S.sbuf([64, 128], F32, name=f"vn{i}")) for i in range(2)]
        self.txw = TL(S.sbuf([96, NT], BF16, name="txw"))
        self.xab = TL(S.sbuf([96, NT], BF16, name="xab"))
        self.sxg = TL(S.sbuf([128, 2, NT], BF16, name="sxg"))
        self.WC = TL(S.sbuf([128, NT // 32], F32, name="WC"))
        self.Xs = [TL(S.sbuf([64, 128], F32, name=f"Xs{i}")) for i in range(2)]
        self.Us = [TL(S.sbuf([64, 128], F32, name=f"Us{i}")) for i in range(2)]
        self.Ys = [TL(S.sbuf([64, 128], F32, name=f"Ys{i}")) for i in range(2)]

    def X(self, k):
        return TL(self.xt[:, k, :], self.xb[k])

    def Hh(self, k):
        return TL(self.ht[:, k, :], self.hb[k])

    def slot(self, i, nslots=1):
        if i >= self.NA:
            assert nslots == 1
            j = i - self.NA
            c0, c1 = j * self.SW, (j + 1) * self.SW
            per = self.NT // 2
            return TL(self.mg32[:, c0:c1], [self.mgb[k] for k in range(c0 // per, (c1 - 1) // per + 1)])
        assert i + nslots <= self.NA
        if nslots == 1:
            return TL(self.arena[:, i, :], self.slot_bufs[i])
        return TL(self.arena[:, i:i + nslots, :].rearrange("p s w -> p (s w)"), self.slot_bufs[i:i + nslots])

    @staticmethod
    def B(*tls):
        out = []
        for t in tls:
            b = t.b if isinstance(t, TL) else t
            if isinstance(b, (list, tuple)):
                out.extend(b)
            else:
                out.append(b)
        return out

    def rmsnorm_x(self, n, which, out_fn=None):
        ps = self.ps1()
        sq = self.slot(0)
        for k in range(NKC):
            xk = self.X(k)
            self.act(sq.t[:, 0:n], xk.t[:, 0:n], AF.Square, [xk.b], [sq.b])
            self.mm(ps.t[:, 0:n], self.ones.t[:], sq.t[:, 0:n], k == 0, k == NKC - 1, [self.ones.b, sq.b], [ps.b])
        self.rsqrt(self.rs_t.t[:, 0:n], ps.t[:, 0:n], 1.0 / D, EPS, [ps.b], [self.rs_t.b])
        for k in range(NKC):
            xk = self.X(k)
            o = out_fn(k) if out_fn else self.Hh(k)
            self.stt(o.t[:, 0:n], xk.t[:, 0:n], self.gvec.t[:, which, k:k + 1], self.rs_t.t[:, 0:n],
                     ALU.mult, ALU.mult, [xk.b, self.gvec.b, self.rs_t.b], self.B(o))

    def dense(self, ps, M, n, slab, col0, rhs_tiles, kc0=0, kp=128):
        nk = len(rhs_tiles)
        for j, r in enumerate(rhs_tiles):
            self.mm(ps.t[0:M, 0:n], slab.t[0:kp, kc0 + j, col0:col0 + M], r.t[0:kp, 0:n], j == 0, j == nk - 1,
                    self.B(slab, r), self.B(ps))

    def neumann(self, P, L, nch, C, pslots, lslots, rslots):
        return self.neumann_multi([(P, L, pslots, lslots, rslots)], nch, C)[0]

    def neumann_multi(self, pairs, nch, C, sl=None):
        sl = sl or self.slot

        def v3(t):
            return t.t[0:C, 0:nch * C].rearrange("p (c j) -> p c j", j=C)
        idb = self.ident.t[0:C, 0:C].unsqueeze(1).to_broadcast([C, nch, C])
        st = []
        for (P, L, pslots, lslots, rslots) in pairs:
            R = sl(rslots[0])
            self.tt(v3(R), v3(P), idb, ALU.add, self.B(P, self.ident), self.B(R))
            st.append([P, L, R, pslots, lslots, rslots])
        nlev = {64: 5, 32: 4}[C]
        for k in range(1, nlev + 1):
            tmp = []
            for (P, L, R, pslots, lslots, rslots) in st:
                psL = self.ps1()
                for c in range(nch):
                    cs = slice(c * C, (c + 1) * C)
                    self.mm(psL.t[0:C, cs], P.t[0:C, cs], L.t[0:C, cs], True, True, self.B(P, L), self.B(psL))
                psP = None
                if k < nlev:
                    psP = self.ps1()
                    for c in range(nch):
                        cs = slice(c * C, (c + 1) * C)
                        self.mm(psP.t[0:C, cs], L.t[0:C, cs], P.t[0:C, cs], True, True, self.B(P, L), self.B(psP))
                tmp.append((psL, psP))
            tmp2 = []
            for e, (psL, psP) in zip(st, tmp):
                P, L, R, pslots, lslots, rslots = e
                Lk = sl(lslots[k % 2])
                self.cp("act", Lk.t[0:C, 0:nch * C], psL.t[0:C, 0:nch * C], self.B(psL), self.B(Lk))
                Pk = P
                if k < nlev:
                    Pk = sl(pslots[k % 2])
                    self.cp("dve", Pk.t[0:C, 0:nch * C], psP.t[0:C, 0:nch * C], self.B(psP), self.B(Pk))
                psR = self.ps1()
                for c in range(nch):
                    cs = slice(c * C, (c + 1) * C)
                    self.mm(psR.t[0:C, cs], Lk.t[0:C, cs], R.t[0:C, cs], True, True, self.B(Lk, R), self.B(psR))
                tmp2.append((Lk, Pk, psR))
            for e, (Lk, Pk, psR) in zip(st, tmp2):
                R, rslots = e[2], e[5]
                Rn = sl(rslots[k % 2])
                self.tt(Rn.t[0:C, 0:nch * C], R.t[0:C, 0:nch * C], psR.t[0:C, 0:nch * C], ALU.add,
                        self.B(R, psR), self.B(Rn))
                e[0], e[1], e[2] = Pk, Lk, Rn
        return [e[2] for e in st]

    def gdn_gen(self, n, C, full, all_xb=False):
        I = self.I
        nch = n // C
        Hk = [self.Hh(k) for k in range(NKC)]
        slab = self.load_slab([(I["w_ab"], 0, 0)])
        psA = self.ps1()
        self.dense(psA, 8, n, slab, 0, Hk)
        psB = self.ps1()
        self.dense(psB, 8, n, slab, 8, Hk)
        g8, G8, b8 = self.g8, self.G8, self.b8
        self.act(g8.t[:, 0:n], psA.t[0:8, 0:n], AF.Exp, self.B(psA, self.adt), [g8.b], bias=self.adt.t[:, 1:2])
        self.act(g8.t[:, 0:n], g8.t[:, 0:n], AF.Ln, [g8.b], [g8.b], bias=1.0)
        self.ts(g8.t[:, 0:n], g8.t[:, 0:n], self.nA.t[:, 0:1], None, ALU.mult, None, self.B(g8, self.nA), [g8.b])
        rm = self.rmask[C]
        self.S.op("dve", lambda e: e.tensor_tensor_scan(out=G8.t[:, 0:n], data0=rm.t[0:8, 0:n], data1=g8.t[:, 0:n],
                                                       initial=0.0, op0=ALU.mult, op1=ALU.add),
                  reads=self.B(g8, rm), writes=[G8.b])
        self.act(b8.t[:, 0:n], psB.t[0:8, 0:n], AF.Sigmoid, self.B(psB), [b8.b])
        Gcol, bcol, nbcol, nbeg = self.Gcol, self.bcol, self.nbcol, self.nbeg
        m8 = nch * 8
        for (src, dst) in ((G8, Gcol), (b8, bcol)):
            psT = self.ps1()
            for c in range(nch):
                self.tr(psT.t[0:C, c * 8:(c + 1) * 8], src.t[0:8, c * C:(c + 1) * C], self.B(src), self.B(psT))
            self.cp("dve", dst.t[0:C, 0:m8], psT.t[0:C, 0:m8], self.B(psT), [dst.b])
        self.ts(nbcol.t[0:C, 0:m8], bcol.t[0:C, 0:m8], -1.0, None, ALU.mult, None, [bcol.b], [nbcol.b])
        self.act(nbeg.t[0:C, 0:m8], Gcol.t[0:C, 0:m8], AF.Exp, [Gcol.b], [nbeg.b])
        self.tt(nbeg.t[0:C, 0:m8], nbeg.t[0:C, 0:m8], nbcol.t[0:C, 0:m8], ALU.mult, [nbeg.b, nbcol.b], [nbeg.b])
        yield
        Gcol3 = Gcol.t[0:C, 0:m8].rearrange("p (c h) -> p c h", h=8)
        bcol3 = bcol.t[0:C, 0:m8].rearrange("p (c h) -> p c h", h=8)
        nbcol3 = nbcol.t[0:C, 0:m8].rearrange("p (c h) -> p c h", h=8)
        nbeg3 = nbeg.t[0:C, 0:m8].rearrange("p (c h) -> p c h", h=8)
        mk = self.masks

        def mask(i):
            return mk.t[0:C, i:i + 1, 0:C].to_broadcast([C, nch, C])

        def v3(t):
            return t.t[0:C, 0:nch * C].rearrange("p (c j) -> p c j", j=C)

        for h in range(8):
            slab = self.load_slab([(I["w_gdn"][:, h, :], 0, 0)])
            kinds = (0, 1, 2) if (full or all_xb) else (1, 2)
            co = {}
            pss = {}
            for j in kinds + ((3,) if full else ()):
                pss[j] = self.ps1()
                self.dense(pss[j], 128, n, slab, j * 128, Hk)
            yield
            for j in kinds:
                blk = j * 8 + h
                ps = pss[j]
                raw = self.slot(1 + j)
                self.cp("dve", raw.t[:, 0:3], self.halo_c.t[:, blk, :], [self.halo_c.b], [raw.b])
                self.cp("act", raw.t[:, 3:3 + n], ps.t[:, 0:n], self.B(ps), [raw.b])
                self.cp("dve", self.halo_c.t[:, blk, :], raw.t[:, n:n + 3], [raw.b], [self.halo_c.b])
                if j == 0 and not full:
                    continue
                o = self.slot(4 + j)
                cw = self.convw
                self.ts(o.t[:, 0:n], raw.t[:, 0:n], cw.t[:, blk, 0:1], None, ALU.mult, None, self.B(raw, cw), [o.b])
                for tap in range(1, 4):
                    self.stt(o.t[:, 0:n], raw.t[:, tap:tap + n], cw.t[:, blk, tap:tap + 1], o.t[:, 0:n],
                             ALU.mult, ALU.add, self.B(raw, cw, o), [o.b])
                self.act(o.t[:, 0:n], o.t[:, 0:n], AF.Silu, [o.b], [o.b])
                co[j] = o
                yield
            if full:
                ps = pss[3]
                sgate = self.slot(7)
                self.act(sgate.t[:, 0:n], ps.t[:, 0:n], AF.Silu, self.B(ps), [sgate.b])
            for j in kinds:
                if j == 2 or j not in co:
                    continue
                o = co[j]
                sq = self.slot(0)
                self.act(sq.t[:, 0:n], o.t[:, 0:n], AF.Square, [o.b], [sq.b])
                psn = self.ps1()
                self.mm(psn.t[:, 0:n], self.ones.t[:], sq.t[:, 0:n], True, True, self.B(self.ones, sq), self.B(psn))
                rs = self.slot(29)
                self.rsqrt(rs.t[:, 0:n], psn.t[:, 0:n], 1.0, EPS, self.B(psn), [rs.b])
                if j == 0:
                    self.stt(o.t[:, 0:n], o.t[:, 0:n], float(128 ** -0.5), rs.t[:, 0:n], ALU.mult, ALU.mult,
                             self.B(o, rs), [o.b])
                else:
                    self.tt(o.t[:, 0:n], o.t[:, 0:n], rs.t[:, 0:n], ALU.mult, self.B(o, rs), [o.b])
            kn, vv = co[1], co[2]
            yield
            qn = co.get(0)
            psG = self.ps1()
            self.mm(psG.t[:, 0:n], self.sel8.t[:, h, :], G8.t[:, 0:n], True, True, self.B(self.sel8, G8), self.B(psG))
            Grow = self.slot(8)
            self.cp("dve", Grow.t[:, 0:n], psG.t[:, 0:n], self.B(psG), [Grow.b])
            eG = self.slot(10)
            self.act(eG.t[:, 0:n], psG.t[:, 0:n], AF.Exp, self.B(psG), [eG.b])
            psBt = self.ps1()
            self.mm(psBt.t[:, 0:n], self.sel8.t[:, h, :], b8.t[:, 0:n], True, True, self.B(self.sel8, b8), self.B(psBt))
            brow = self.slot(9)
            self.cp("act", brow.t[:, 0:n], psBt.t[:, 0:n], self.B(psBt), [brow.b])
            if full:
                qdec = self.slot(11)
                self.tt(qdec.t[:, 0:n], qn.t[:, 0:n], eG.t[:, 0:n], ALU.mult, self.B(qn, eG), [qdec.b])
            yield
            psKK = self.ps1()
            for c in range(nch):
                cs = slice(c * C, (c + 1) * C)
                self.mm(psKK.t[0:C, cs], kn.t[:, cs], kn.t[:, cs], True, True, [kn.b], self.B(psKK))
            if full:
                psQK = self.ps1()
                for c in range(nch):
                    cs = slice(c * C, (c + 1) * C)
                    self.mm(psQK.t[0:C, cs], kn.t[:, cs], qn.t[:, cs], True, True, [kn.b, qn.b], self.B(psQK))
            Gc_h = Gcol3[:, :, h:h + 1].to_broadcast([C, nch, C])
            t1 = self.slot(12)
            self.tt(v3(t1), v3(Grow), Gc_h, ALU.subtract, self.B(Grow, Gcol), [t1.b])
            tU = self.slot(13)
            self.tt(v3(tU), v3(t1), mask(0), ALU.mult, self.B(t1, mk), [tU.b])
            self.act(tU.t[0:C, 0:nch * C], tU.t[0:C, 0:nch * C], AF.Exp, [tU.b], [tU.b])
            self.tt(v3(tU), v3(tU), mask(0), ALU.mult, self.B(tU, mk), [tU.b])
            if full:
                QKT = self.slot(14)
                self.tt(v3(QKT), v3(psQK), v3(tU), ALU.mult, self.B(psQK, tU), [QKT.b])
            mbs = self.slot(15)
            self.tt(v3(mbs), v3(brow), mask(3), ALU.mult, self.B(brow, mk), [mbs.b])
            self.tt(v3(mbs), v3(mbs), v3(tU), ALU.mult, self.B(mbs, tU), [mbs.b])
            P0 = self.slot(16)
            self.tt(v3(P0), v3(psKK), v3(mbs), ALU.mult, self.B(psKK, mbs), [P0.b])
            yield
            tL = self.slot(18)
            self.tt(v3(tL), v3(t1), mask(2), ALU.mult, self.B(t1, mk), [tL.b])
            self.act(tL.t[0:C, 0:nch * C], tL.t[0:C, 0:nch * C], AF.Exp, [tL.b], [tL.b], scale=-1.0)
            mbL = self.slot(19)
            self.tt(v3(mbL), mask(2), nbcol3[:, :, h:h + 1].to_broadcast([C, nch, C]), ALU.mult,
                    self.B(mk, nbcol), [mbL.b])
            self.tt(v3(mbL), v3(mbL), v3(tL), ALU.mult, self.B(mbL, tL), [mbL.b])
            L0 = self.slot(17)
            self.tt(v3(L0), v3(psKK), v3(mbL), ALU.mult, self.B(psKK, mbL), [L0.b])
            yield
            R = self.neumann(P0, L0, nch, C, (16, 20), (17, 21), (22, 23))
            yield
            R3 = v3(R)
            kdc = self.kdc[h % 2]
            glast = Grow.t[0:C, 0:n].rearrange("p (c j) -> p c j", j=C)[:, :, C - 1:C]
            self.tt(kdc.t[0:C, 0:nch].unsqueeze(2), glast, Gcol3[:, :, h:h + 1], ALU.subtract,
                    self.B(Grow, Gcol), [kdc.b])
            self.act(kdc.t[0:C, 0:nch], kdc.t[0:C, 0:nch], AF.Exp, [kdc.b], [kdc.b])
            kdec = self.slot(24, 2)
            bv = self.slot(26, 2)
            for (src, dst, colv) in ((kn, kdec, kdc.t[0:C, 0:nch].unsqueeze(2).to_broadcast([C, nch, 128])),
                                     (vv, bv, bcol3[:, :, h:h + 1].to_broadcast([C, nch, 128]))):
                if nch * 128 <= 512:
                    pt = self.ps1()
                    ptt, ptb = pt.t, self.B(pt)
                else:
                    ptt, ptb = self.ps2()
                for c in range(nch):
                    self.tr(ptt[0:C, c * 128:(c + 1) * 128], src.t[:, c * C:(c + 1) * C], self.B(src), ptb)
                self.tt(dst.t[0:C, 0:nch * 128].rearrange("p (c d) -> p c d", d=128),
                        ptt[0:C, 0:nch * 128].rearrange("p (c d) -> p c d", d=128), colv, ALU.mult,
                        ptb + self.B(kdc, bcol), self.B(dst))
            kdec3 = kdec.t[0:C, 0:nch * 128].rearrange("p (c d) -> p c d", d=128)
            bv3 = bv.t[0:C, 0:nch * 128].rearrange("p (c d) -> p c d", d=128)
            if full:
                QKT3 = v3(QKT)
                o_h = self.slot(28)
            Sg = self.Sg[h]
            for c in range(nch):
                cs = slice(c * C, (c + 1) * C)
                psk = self.ps1()
                self.mm(psk.t[0:C, 0:128], kn.t[:, cs], Sg.t[:, :], True, True, self.B(kn, Sg), self.B(psk))
                rv = self.rv[c % 2]
                self.stt(rv.t[0:C, :], psk.t[0:C, 0:128], nbeg3[:, c, h:h + 1], bv3[:, c, :], ALU.mult, ALU.add,
                         self.B(psk, nbeg, bv), [rv.b])
                psv = self.ps1()
                self.mm(psv.t[0:C, 0:128], R3[:, c, :], rv.t[0:C, :], True, True, self.B(R, rv), self.B(psv))
                yield
                vn = self.vn[c % 2]
                self.cp("act", vn.t[0:C, :], psv.t[0:C, 0:128], self.B(psv), [vn.b])
                yield
                if full:
                    pso = self.ps1()
                    self.mm(pso.t[:, 0:C], Sg.t[:, :], qdec.t[:, cs], True, False, self.B(Sg, qdec), self.B(pso))
                    self.mm(pso.t[:, 0:C], vn.t[0:C, :], QKT3[:, c, :], False, True, self.B(vn, QKT), self.B(pso))
                    self.cp("act", o_h.t[:, cs], pso.t[:, 0:C], self.B(pso), [o_h.b])
                pss = self.ps1()
                self.mm(pss.t[:, 0:128], kdec3[:, c, :], vn.t[0:C, :], True, True, self.B(kdec, vn), self.B(pss))
                self.stt(Sg.t[:, :], Sg.t[:, :], eG.t[:, c * C + C - 1:c * C + C], pss.t[:, 0:128], ALU.mult, ALU.add,
                         self.B(Sg, eG, pss), [Sg.b])
                yield
            if full:
                sq = self.slot(0)
                self.act(sq.t[:, 0:n], o_h.t[:, 0:n], AF.Square, [o_h.b], [sq.b])
                psn = self.ps1()
                self.mm(psn.t[:, 0:n], self.ones.t[:], sq.t[:, 0:n], True, True, self.B(self.ones, sq), self.B(psn))
                rs = self.slot(29)
                self.rsqrt(rs.t[:, 0:n], psn.t[:, 0:n], 1.0 / 128, EPS, self.B(psn), [rs.b])
                self.stt(o_h.t[:, 0:n], o_h.t[:, 0:n], self.gng.t[:, 0:1], rs.t[:, 0:n], ALU.mult, ALU.mult,
                         self.B(o_h, self.gng, rs), [o_h.b])
                self.tt(self.oat[:, h, 0:n], o_h.t[:, 0:n], sgate.t[:, 0:n], ALU.mult, self.B(o_h, sgate),
                        [self.oab[h]])
                self.tap(f"oa{h}", self.oat[:, h, 0:n], [self.oab[h]], n)

    def tap(self, name, ap, bufs, n=None):
        full = f"{self.tile_tag}_{name}"
        if name not in self.debug and full not in self.debug:
            return
        shape = list(ap.shape)
        d = self.nc.dram_tensor("dbg_" + full, shape, ap.dtype, kind="ExternalOutput").ap()
        self.dbg_out["dbg_" + full] = shape
        self.dma("sp", "outq", d, ap, bufs, [])

    def rwkv_gen(self, n, C, full, all_xb):
        I = self.I
        nch = n // C
        RMAP = self.RMAP

        def rsl(i, nslots=1):
            return self.slot(RMAP[i], nslots)
        Hk = [self.Hh(k) for k in range(NKC)]
        mu, omu, hx, rwp = self.mu, self.omu, self.halo_x, self.rwp
        mk = self.masks

        def mask(i):
            return mk.t[0:C, i:i + 1, 0:C].to_broadcast([C, nch, C])

        def v3(t, w=None):
            w = w or C
            return t.t[0:C, 0:nch * w].rearrange("p (c j) -> p c j", j=w)

        def f3(t):
            return t.t[:, 0:n].rearrange("p (c j) -> p c j", j=C)

        def shift_mix(ps, M, blkid, rawslot, xmslot, need_xm=True):
            raw = rsl(rawslot)
            self.cp("dve", raw.t[0:M, 0:1], hx.t[0:M, blkid:blkid + 1], [hx.b], [raw.b])
            self.cp("act", raw.t[0:M, 1:1 + n], ps.t[0:M, 0:n], self.B(ps), [raw.b])
            self.cp("dve", hx.t[0:M, blkid:blkid + 1], raw.t[0:M, n:n + 1], [raw.b], [hx.b])
            if not need_xm:
                return None
            tmp = rsl(13)
            self.ts(tmp.t[0:M, 0:n], raw.t[0:M, 0:n], mu.t[0:M, blkid:blkid + 1], None, ALU.mult, None,
                    self.B(raw, mu), [tmp.b])
            xm = rsl(xmslot)
            self.stt(xm.t[0:M, 0:n], raw.t[0:M, 1:1 + n], omu.t[0:M, blkid:blkid + 1], tmp.t[0:M, 0:n],
                     ALU.mult, ALU.add, self.B(raw, omu, tmp), [xm.b])
            return xm

        slab = self.load_slab([(I["w_lora"], 0, 0)])
        lor = []
        for (blkid, col0, M, kind) in ((24, 0, 96, "xw"), (25, 96, 96, "xa"), (26, 192, 128, "xg0"), (27, 320, 128, "xg1")):
            isg = kind.startswith("xg")
            if isg and not (full or all_xb):
                continue
            ps = self.ps1()
            self.dense(ps, M, n, slab, col0, Hk)
            lor.append((blkid, M, kind, isg, ps))
        yield
        for (blkid, M, kind, isg, ps) in lor:
            xm = shift_mix(ps, M, blkid, 1, 4, need_xm=(full or not isg))
            if xm is None:
                continue
            if kind == "xw":
                self.act(self.txw.t[0:96, 0:n], xm.t[0:96, 0:n], AF.Tanh, [xm.b], [self.txw.b])
            elif kind == "xa":
                self.cp("dve", self.xab.t[0:96, 0:n], xm.t[0:96, 0:n], [xm.b], [self.xab.b])
            else:
                gi = 0 if kind == "xg0" else 1
                self.act(self.sxg.t[:, gi, 0:n], xm.t[:, 0:n], AF.Sigmoid, [xm.b], [self.sxg.b])
            yield
        import os
        ksub = int(os.environ.get("KSUB", "9"))
        if ksub < 2:
            return
        for i in range(8):
            bc = slice(i * 128, (i + 1) * 128)
            slab = self.load_slab([(I["w_rw"][:, i, :], 0, 0)])
            xm = {}
            for j in (0, 1, 2):
                if j == 0 and not (full or all_xb):
                    continue
                ps = self.ps1()
                self.dense(ps, 128, n, slab, j * 128, Hk)
                xm[j] = shift_mix(ps, 128, j * 8 + i, 1 + j, 4 + j, need_xm=(full or j != 0))
            r, k, v = xm.get(0), xm[1], xm[2]
            yield
            if ksub < 3:
                continue
            psw = self.ps1()
            self.mm(psw.t[:, 0:n], self.w2.t[0:96, bc], self.txw.t[0:96, 0:n], True, True,
                    self.B(self.w2, self.txw), self.B(psw))
            logw = rsl(7)
            self.act(logw.t[:, 0:n], psw.t[:, 0:n], AF.Sigmoid, self.B(psw, rwp), [logw.b], bias=rwp.t[:, 0, i:i + 1])
            self.ts(logw.t[:, 0:n], logw.t[:, 0:n], float(-np.exp(-0.5)), None, ALU.mult, None, [logw.b], [logw.b])
            Lc = rsl(8)
            rm = self.rmask[C]
            self.S.op("dve", lambda e, Lc=Lc, logw=logw, rm=rm: e.tensor_tensor_scan(
                out=Lc.t[:, 0:n], data0=rm.t[:, 0:n], data1=logw.t[:, 0:n], initial=0.0, op0=ALU.mult, op1=ALU.add),
                reads=self.B(logw, rm), writes=[Lc.b])
            psa = self.ps1()
            self.mm(psa.t[:, 0:n], self.a2.t[0:96, bc], self.xab.t[0:96, 0:n], True, True,
                    self.B(self.a2, self.xab), self.B(psa))
            a = rsl(9)
            self.act(a.t[:, 0:n], psa.t[:, 0:n], AF.Sigmoid, self.B(psa, rwp), [a.b], bias=rwp.t[:, 1, i:i + 1])
            if full:
                psg = self.ps1()
                self.mm(psg.t[:, 0:n], self.g2.t[:, 0, bc], self.sxg.t[:, 0, 0:n], True, False,
                        self.B(self.g2, self.sxg), self.B(psg))
                self.mm(psg.t[:, 0:n], self.g2.t[:, 1, bc], self.sxg.t[:, 1, 0:n], False, True,
                        self.B(self.g2, self.sxg), self.B(psg))
                gate = rsl(22)
                self.cp("act", gate.t[:, 0:n], psg.t[:, 0:n], self.B(psg), [gate.b])
            yield
            kk = rsl(10)
            self.ts(kk.t[:, 0:n], k.t[:, 0:n], rwp.t[:, 2, i:i + 1], None, ALU.mult, None, self.B(k, rwp), [kk.b])
            sq = rsl(0)
            self.act(sq.t[:, 0:n], kk.t[:, 0:n], AF.Square, [kk.b], [sq.b])
            psn = self.ps1()
            self.mm(psn.t[:, 0:n], self.onesbd.t[:], sq.t[:, 0:n], True, True, self.B(self.onesbd, sq), self.B(psn))
            rs = rsl(29)
            self.rsqrt(rs.t[:, 0:n], psn.t[:, 0:n], 1.0, EPS, self.B(psn), [rs.b])
            self.tt(kk.t[:, 0:n], kk.t[:, 0:n], rs.t[:, 0:n], ALU.mult, self.B(kk, rs), [kk.b])
            kh = rsl(11)
            self.ts(kh.t[:, 0:n], a.t[:, 0:n], rwp.t[:, 3, i:i + 1], self.omka.t[:, i:i + 1], ALU.mult, ALU.add,
                    self.B(a, rwp, self.omka), [kh.b])
            self.tt(kh.t[:, 0:n], kh.t[:, 0:n], k.t[:, 0:n], ALU.mult, self.B(kh, k), [kh.b])
            kka = rsl(12)
            self.tt(kka.t[:, 0:n], kk.t[:, 0:n], a.t[:, 0:n], ALU.mult, self.B(kk, a), [kka.b])
            yield
            Winv = rsl(14)
            self.act(Winv.t[:, 0:n], Lc.t[:, 0:n], AF.Exp, [Lc.b], [Winv.b], scale=-1.0)
            Wprev = rsl(15)
            self.tt(Wprev.t[:, 0:n], Lc.t[:, 0:n], logw.t[:, 0:n], ALU.subtract, self.B(Lc, logw), [Wprev.b])
            self.act(Wprev.t[:, 0:n], Wprev.t[:, 0:n], AF.Exp, [Wprev.b], [Wprev.b])
            AR = rsl(16, 2)
            AR4 = AR.t[:, 0:2 * n].rearrange("p (c two j) -> p c two j", two=2, j=C)
            self.tt(AR4[:, :, 0, :], f3(kk), f3(Wprev), ALU.mult, self.B(kk, Wprev), self.B(AR))
            if full:
                Wt = rsl(13)
                self.act(Wt.t[:, 0:n], Lc.t[:, 0:n], AF.Exp, [Lc.b], [Wt.b])
                self.tt(AR4[:, :, 1, :], f3(r), f3(Wt), ALU.mult, self.B(r, Wt), self.B(AR))
            bt = rsl(18)
            self.stt(bt.t[:, 0:n], kka.t[:, 0:n], -1.0, Winv.t[:, 0:n], ALU.mult, ALU.mult, self.B(kka, Winv), [bt.b])
            kt = rsl(19)
            self.tt(kt.t[:, 0:n], kh.t[:, 0:n], Winv.t[:, 0:n], ALU.mult, self.B(kh, Winv), [kt.b])
            edl = rsl(31)
            Lc3 = f3(Lc)
            self.tt(f3(edl), Lc3[:, :, C - 1:C].to_broadcast([128, nch, C]), Lc3, ALU.subtract, [Lc.b], [edl.b])
            self.act(edl.t[:, 0:n], edl.t[:, 0:n], AF.Exp, [edl.b], [edl.b])
            bdec = rsl(20)
            self.stt(bdec.t[:, 0:n], kka.t[:, 0:n], -1.0, edl.t[:, 0:n], ALU.mult, ALU.mult, self.B(kka, edl), [bdec.b])
            kdec = rsl(21)
            self.tt(kdec.t[:, 0:n], kh.t[:, 0:n], edl.t[:, 0:n], ALU.mult, self.B(kh, edl), [kdec.b])
            WC = self.WC
            self.act(WC.t[:, 0:nch].unsqueeze(2), Lc3[:, :, C - 1:C], AF.Exp, [Lc.b], [WC.b])
            if full:
                rk = rsl(0)
                self.stt(rk.t[:, 0:n], r.t[:, 0:n], rwp.t[:, 4, i:i + 1], kh.t[:, 0:n], ALU.mult, ALU.mult,
                         self.B(r, rwp, kh), [rk.b])
                psb = self.ps1()
                self.mm(psb.t[:, 0:n], self.onesbd.t[:], rk.t[:, 0:n], True, True, self.B(self.onesbd, rk), self.B(psb))
                bvv = rsl(30)
                self.tt(bvv.t[:, 0:n], psb.t[:, 0:n], v.t[:, 0:n], ALU.mult, self.B(psb, v), [bvv.b])
            if ksub < 4:
                continue
            yield
            tok = []
            for (src, s0) in ((v, 32), (bdec, 34), (kdec, 36)):
                dst = rsl(s0, 2)
                if nch * 128 <= 512:
                    pt = self.ps1()
                    ptt, ptb = pt.t, self.B(pt)
                else:
                    ptt, ptb = self.ps2()
                for c in range(nch):
                    self.tr(ptt[0:C, c * 128:(c + 1) * 128], src.t[:, c * C:(c + 1) * C], self.B(src), ptb)
                self.cp("act" if s0 == 34 else "dve", dst.t[0:C, 0:nch * 128], ptt[0:C, 0:nch * 128], ptb, self.B(dst))
                tok.append(dst)
            Vt, Bt, Kt = tok
            Vt3, Bt3, Kt3 = (t.t[0:C, 0:nch * 128].rearrange("p (c d) -> p c d", d=128) for t in tok)
            if ksub < 5:
                continue
            Rr, RabT, AakT, RakT = [], [], [], []
            W2 = 2 * C if full else C
            pairs = []
            for hp in (0, 1):
                pr = slice(hp * 64, hp * 64 + 64)
                outs = []
                for (lt, slots_) in ((bt, (26 if hp == 0 else 48, 38 + 3 * hp)), (kt, (39 + 3 * hp, 40 + 3 * hp))):
                    if nch * W2 <= 512:
                        pg = self.ps1()
                        pgt, pgb = pg.t, self.B(pg)
                    else:
                        pgt, pgb = self.ps2()
                    for c in range(nch):
                        self.mm(pgt[0:C, c * W2:(c + 1) * W2], lt.t[pr, c * C:(c + 1) * C],
                                AR.t[pr, c * 2 * C:c * 2 * C + W2], True, True, self.B(lt, AR), pgb, fence=False)
                    pg3 = pgt[0:C, 0:nch * W2].rearrange("p (c w) -> p c w", w=W2)
                    oA = rsl(slots_[0])
                    self.tt(v3(oA), pg3[:, :, 0:C], mask(1), ALU.mult, pgb + [mk.b], [oA.b])
                    outs.append(oA)
                    if full:
                        oR = rsl(slots_[1])
                        self.tt(v3(oR), pg3[:, :, C:2 * C], mask(0), ALU.mult, pgb + [mk.b], [oR.b])
                        outs.append(oR)
                    else:
                        outs.append(None)
                AabT, RabT_h, AakT_h, RakT_h = outs
                pg = self.ps1()
                for c in range(nch):
                    self.mm(pg.t[0:C, c * C:(c + 1) * C], AR.t[pr, c * 2 * C:c * 2 * C + C], bt.t[pr, c * C:(c + 1) * C],
                            True, True, self.B(AR, bt), self.B(pg), fence=False)
                Aab = rsl(24 if hp == 0 else 50)
                self.tt(v3(Aab), v3(pg), mask(2), ALU.mult, self.B(pg, mk), [Aab.b])
                pairs.append((AabT, Aab, (26, 27) if hp == 0 else (48, 49), (24, 25) if hp == 0 else (50, 51),
                              (44 + 2 * hp, 45 + 2 * hp)))
                RabT.append(RabT_h)
                AakT.append(AakT_h)
                RakT.append(RakT_h)
                yield
            Rr = self.neumann_multi(pairs, nch, C, sl=rsl)
            yield
            Pr = self.Pr[i]
            yfm = rsl(23)
            if ksub < 6:
                continue
            for c in range(nch):
                cs = slice(c * C, (c + 1) * C)
                psX = self.ps1()
                self.mm(psX.t[0:C, 0:128], AR.t[:, c * 2 * C:c * 2 * C + C], Pr.t[:, :], True, False,
                        self.B(AR, Pr), self.B(psX))
                for hp in (0, 1):
                    hc = slice(hp * 64, hp * 64 + 64)
                    self.mm(psX.t[0:C, hc], v3(AakT[hp])[:, c, :], Vt3[:, c, hc], False, hp == 1,
                            self.B(AakT[hp], Vt), self.B(psX))
                Xs = self.Xs[c % 2]
                self.cp("act", Xs.t[0:C, :], psX.t[0:C, 0:128], self.B(psX), [Xs.b])
                yield
                psU = self.ps1()
                for hp in (0, 1):
                    hc = slice(hp * 64, hp * 64 + 64)
                    self.mm(psU.t[0:C, hc], v3(Rr[hp])[:, c, :], Xs.t[0:C, hc], True, True, self.B(Rr[hp], Xs), self.B(psU))
                Us = self.Us[c % 2]
                self.cp("dve", Us.t[0:C, :], psU.t[0:C, 0:128], self.B(psU), [Us.b])
                yield
                if full:
                    psY = self.ps1()
                    self.mm(psY.t[0:C, 0:128], AR.t[:, c * 2 * C + C:(c + 1) * 2 * C], Pr.t[:, :], True, False,
                            self.B(AR, Pr), self.B(psY))
                    for hp in (0, 1):
                        hc = slice(hp * 64, hp * 64 + 64)
                        self.mm(psY.t[0:C, hc], v3(RabT[hp])[:, c, :], Us.t[0:C, hc], False, False,
                                self.B(RabT[hp], Us), self.B(psY))
                        self.mm(psY.t[0:C, hc], v3(RakT[hp])[:, c, :], Vt3[:, c, hc], False, hp == 1,
                                self.B(RakT[hp], Vt), self.B(psY))
                    Ys = self.Ys[c % 2]
                    self.cp("dve", Ys.t[0:C, :], psY.t[0:C, 0:128], self.B(psY), [Ys.b])
                    psT = self.ps1()
                    self.tr(psT.t[:, 0:C], Ys.t[0:C, :], [Ys.b], self.B(psT))
                    self.cp("act", yfm.t[:, cs], psT.t[:, 0:C], self.B(psT), [yfm.b])
                psP = self.ps1()
                self.mm(psP.t[:, 0:128], Bt3[:, c, :], Us.t[0:C, :], True, False, self.B(Bt, Us), self.B(psP))
                self.mm(psP.t[:, 0:128], Kt3[:, c, :], Vt3[:, c, :], False, True, self.B(Kt, Vt), self.B(psP))
                for hp in (0, 1):
                    pr = slice(hp * 64, hp * 64 + 64)
                    self.stt(Pr.t[pr, pr], Pr.t[pr, pr], WC.t[pr, c:c + 1], psP.t[pr, pr],
                             ALU.mult, ALU.add, self.B(Pr, WC, psP), [Pr.b])
                yield
            if full and ksub >= 7:
                psm = self.ps1()
                self.mm(psm.t[:, 0:n], self.onesbd.t[:], yfm.t[:, 0:n], True, True, self.B(self.onesbd, yfm), self.B(psm))
                self.stt(yfm.t[:, 0:n], psm.t[:, 0:n], -1.0 / 64, yfm.t[:, 0:n], ALU.mult, ALU.add,
                         self.B(psm, yfm), [yfm.b])
                sq = rsl(0)
                self.act(sq.t[:, 0:n], yfm.t[:, 0:n], AF.Square, [yfm.b], [sq.b])
                psv = self.ps1()
                self.mm(psv.t[:, 0:n], self.onesbd.t[:], sq.t[:, 0:n], True, True, self.B(self.onesbd, sq), self.B(psv))
                rs = rsl(29)
                self.rsqrt(rs.t[:, 0:n], psv.t[:, 0:n], 1.0 / 64, LNX_EPS, self.B(psv), [rs.b])
                self.tt(yfm.t[:, 0:n], yfm.t[:, 0:n], rs.t[:, 0:n], ALU.mult, self.B(yfm, rs), [yfm.b])
                self.ts(yfm.t[:, 0:n], yfm.t[:, 0:n], rwp.t[:, 5, i:i + 1], rwp.t[:, 6, i:i + 1], ALU.mult, ALU.add,
                        self.B(yfm, rwp), [yfm.b])
                self.tt(yfm.t[:, 0:n], yfm.t[:, 0:n], bvv.t[:, 0:n], ALU.add, self.B(yfm, bvv), [yfm.b])
                self.tt(self.obt[:, i, 0:n], yfm.t[:, 0:n], gate.t[:, 0:n], ALU.mult, self.B(yfm, gate), [self.obb[i]])
                self.tap(f"ob{i}", self.obt[:, i, 0:n], [self.obb[i]], n)

    def tail(self, n, pT_src, yT_dst):
        I = self.I
        Hk = [self.Hh(k) for k in range(NKC)]
        OA = [TL(self.oat[:, k, :], self.oab[k]) for k in range(8)]
        OB = [TL(self.obt[:, k, :], self.obb[k]) for k in range(8)]
        MG = [TL(self.mgt[:, k, :], self.mgb[k]) for k in range(NKC)]
        pT = self.pT
        self.dma("pool", "pin", pT.t[:, :, 0:n], pT_src.rearrange("(k p) t -> p k t", p=128), [], [pT.b])
        for q in range(4):
            qc = slice(q * 512, (q + 1) * 512)
            sgs = {}
            for gi in (0, 1):
                sG = self.load_slab([(I["w_gm"][:, gi, qc], 0, 0)])
                for jj in range(4):
                    pg = self.ps1()
                    self.dense(pg, 128, n, sG, jj * 128, Hk)
                    sg = self.slot(20 + gi * 4 + jj)
                    self.act(sg.t[:, 0:n], pg.t[:, 0:n], AF.Sigmoid, self.B(pg), [sg.b])
                    sgs[(gi, jj)] = sg
            sA = self.load_slab([(I["w_up_a"][:, qc], 0, 0), (I["w_up_b"][:, qc], 8, 0)])
            for jj in range(4):
                j = q * 4 + jj
                pa = self.ps1()
                self.dense(pa, 128, n, sA, jj * 128, OA, kc0=0)
                pb = self.ps1()
                self.dense(pb, 128, n, sA, jj * 128, OB, kc0=8)
                sa, sb = sgs[(0, jj)], sgs[(1, jj)]
                self.tt(sa.t[:, 0:n], pa.t[:, 0:n], sa.t[:, 0:n], ALU.mult, self.B(pa, sa), [sa.b])
                self.tt(sb.t[:, 0:n], pb.t[:, 0:n], sb.t[:, 0:n], ALU.mult, self.B(pb, sb), [sb.b])
                self.tt(self.mgt[:, j, 0:n], sa.t[:, 0:n], sb.t[:, 0:n], ALU.add, self.B(sa, sb), [self.mgb[j]])
        for q in range(4):
            s = self.load_slab([(I["w_o"][:, q * 512:(q + 1) * 512], 0, 0)])
            for jj in range(4):
                j = q * 4 + jj
                ps = self.ps1()
                self.dense(ps, 128, n, s, jj * 128, MG)
                xj = self.X(j)
                self.tt(xj.t[:, 0:n], xj.t[:, 0:n], ps.t[:, 0:n], ALU.add, self.B(xj, ps), [xj.b])
        self.tap("x1", self.xt[:, :, 0:n], self.xb)
        self.rmsnorm_x(n, 1)
        for g in range(4):
            for q in range(4):
                c0 = g * 2048 + q * 512
                s = self.load_slab([(I["w_ff1"][:, c0:c0 + 512], 0, 0)])
                for jj in range(4):
                    jh = q * 4 + jj
                    ps = self.ps1()
                    self.dense(ps, 128, n, s, jj * 128, Hk)
                    r = self.slot(1 + jj % 2)
                    self.act(r.t[:, 0:n], ps.t[:, 0:n], AF.Relu, self.B(ps), [r.b])
                    self.tt(self.mgt[:, jh, 0:n], r.t[:, 0:n], r.t[:, 0:n], ALU.mult, [r.b], [self.mgb[jh]])
            for q in range(4):
                s = self.load_slab([(I["w_ff2"][g * 2048:(g + 1) * 2048, q * 512:(q + 1) * 512], 0, 0)])
                for jj in range(4):
                    j = q * 4 + jj
                    ps = self.ps1()
                    self.dense(ps, 128, n, s, jj * 128, MG)
                    xj = self.X(j)
                    self.tt(xj.t[:, 0:n], xj.t[:, 0:n], ps.t[:, 0:n], ALU.add, self.B(xj, ps), [xj.b])
        self.tap("x2", self.xt[:, :, 0:n], self.xb)
        self.rmsnorm_x(n, 2)
        P2 = [TL(pT.t[:, 0, :], pT.b), TL(pT.t[:, 1, :], pT.b)]
        for q in range(4):
            qc = slice(q * 512, (q + 1) * 512)
            s = self.load_slab([(I["w_ple_gate"][:, qc], 0, 0)])
            s2 = self.load_slab([(I["w_ple"][:, qc], 0, 0)])
            for jj in range(4):
                j = q * 4 + jj
                pg = self.ps1()
                self.dense(pg, 128, n, s, jj * 128, Hk)
                pp = self.ps1()
                self.dense(pp, 128, n, s2, jj * 128, P2)
                sg = self.slot(1 + jj % 2)
                self.act(sg.t[:, 0:n], pg.t[:, 0:n], AF.Sigmoid, self.B(pg), [sg.b])
                self.tt(sg.t[:, 0:n], sg.t[:, 0:n], pp.t[:, 0:n], ALU.mult, self.B(sg, pp), [sg.b])
                xj = self.X(j)
                self.tt(xj.t[:, 0:n], xj.t[:, 0:n], sg.t[:, 0:n], ALU.add, self.B(xj, sg), [xj.b])
        self.rmsnorm_x(n, 3, out_fn=lambda k: self.slot(4 + k))
        for k in range(NKC):
            sk = self.slot(4 + k)
            self.dma("sp", "outq", yT_dst[k * 128:(k + 1) * 128, :], sk.t[:, 0:n], [sk.b], [])

    def tile(self, tag, xT_src, n, C, full, all_xb=False, pT_src=None, yT_dst=None):
        self.tile_tag = tag
        self.dma("sp", "xin", self.xt[:, :, 0:n], xT_src.rearrange("(k p) t -> p k t", p=128), [], self.xb)
        import os
        stage = int(os.environ.get("KSTAGE", "9"))
        self.rmsnorm_x(n, 0)
        self.tap("h", self.ht[:, :, 0:n], self.hb)
        gens = []
        if stage >= 2:
            gens.append(self.gdn_gen(n, C, full, all_xb))
        if stage >= 3:
            gens.append(self.rwkv_gen(n, C, full, all_xb))
        if os.environ.get("KSEQ"):
            for g in gens:
                for _ in g:
                    pass
            gens = []
        stride = int(os.environ.get("KSTRIDE", "1"))
        gid = {id(g): k for k, g in enumerate(gens)}
        while gens:
            for g in list(gens):
                self.psg = gid[id(g)] if len(gid) > 1 else None
                try:
                    for _ in range(stride):
                        next(g)
                except StopIteration:
                    gens.remove(g)
        self.psg = None
        if full and stage >= 4:
            self.tail(n, pT_src, yT_dst)

    def build(self):
        NT = self.NT
        self.declare_io()
        self.init_psum()
        self.init_slabs(3)
        self.setup_consts()
        self.alloc_state()
        self.alloc_work()
        self.zero_state()
        I = self.I
        for t in range(self.n_pre):
            ts_ = slice(t * NT, (t + 1) * NT)
            self.tile(f"pre{t}", I["xT_pre"][:, ts_], NT, 64, False, all_xb=(t == self.n_pre - 1))
        for t in range(self.n_main):
            ts_ = slice(t * NT, (t + 1) * NT)
            self.tile(f"main{t}", I["xT_main"][:, ts_], NT, 64, True, pT_src=I["pT_main"][:, ts_],
                      yT_dst=I["yT_main"][:, ts_])
        self.store_state("p")
        if self.with_sample:
            self.load_state()
            self.tile("samp", I["xT_s"], 32, 32, True, pT_src=I["pT_s"], yT_dst=I["yT_s"])
            self.store_state("s")
        self.S.emit(final_streams=["outq"])
        return self.nc


def _blk(v):
    return np.ascontiguousarray(v.reshape(-1, 128).T)


def _shift_layout(v):
    out = np.zeros((128, 28), np.float32)
    out[:, 0:24] = v[0:3072].reshape(24, 128).T
    out[0:96, 24] = v[3072:3168]
    out[0:96, 25] = v[3168:3264]
    out[:, 26] = v[3264:3392]
    out[:, 27] = v[3392:3520]
    return out


def _shift_unlayout(a):
    v = np.zeros(3520, np.float32)
    v[0:3072] = a[:, 0:24].T.reshape(-1)
    v[3072:3168] = a[0:96, 24]
    v[3168:3264] = a[0:96, 25]
    v[3264:3392] = a[:, 26]
    v[3392:3520] = a[:, 27]
    return v


def prep_weights(w):
    f = np.float32
    w_in = w["w_in"]
    o = {}
    g = np.empty((D, 8, 512), f)
    for h in range(8):
        g[:, h, 0:128] = w_in[:, h * 128:(h + 1) * 128]
        g[:, h, 128:256] = w_in[:, 1024 + h * 128:1024 + (h + 1) * 128]
        g[:, h, 256:384] = w_in[:, 2048 + h * 128:2048 + (h + 1) * 128]
        g[:, h, 384:512] = w_in[:, 3088 + h * 128:3088 + (h + 1) * 128]
    o["w_gdn"] = g
    o["w_ab"] = np.ascontiguousarray(w_in[:, 3072:3088])
    r = np.empty((D, 8, 384), f)
    for i in range(8):
        r[:, i, 0:128] = w_in[:, 4112 + i * 128:4112 + (i + 1) * 128]
        r[:, i, 128:256] = w_in[:, 5136 + i * 128:5136 + (i + 1) * 128]
        r[:, i, 256:384] = w_in[:, 6160 + i * 128:6160 + (i + 1) * 128]
    o["w_rw"] = r
    o["w_lora"] = np.ascontiguousarray(w_in[:, 7184:7632])
    o["w_gm"] = np.ascontiguousarray(w_in[:, 7632:11728].reshape(D, 2, 2048))
    for k in ("w_up_a", "w_up_b", "w_o", "w_ff1", "w_ff2", "w_ple_gate", "w_ple", "w2", "a2", "g2"):
        o[k] = np.ascontiguousarray(w[k], dtype=f)
    gv = np.empty((128, 4, 16), f)
    for i, k in enumerate(("g_mix", "g_mlp", "g_ple", "g_final")):
        gv[:, i, :] = _blk(w[k])
    o["gvec"] = gv
    cw = w["conv_w"]
    o["convw"] = np.ascontiguousarray(cw.reshape(4, 24, 128).transpose(2, 1, 0))
    o["mu"] = _shift_layout(w["mu_shift"])
    rp = np.empty((128, 7, 8), f)
    for i, k in enumerate(("w0", "a0", "k_k", "k_a", "r_k", "lnx_g", "lnx_b")):
        rp[:, i, :] = _blk(w[k].reshape(-1))
    o["rwp"] = rp
    o["gng"] = np.ascontiguousarray(w["gdn_norm_g"].reshape(128, 1))
    o["adt"] = np.ascontiguousarray(np.stack([w["a_log"], w["dt_bias"]], axis=1))
    return o


def state_to_dev(sg, conv, sr, sh):
    return {
        "sg_in": np.ascontiguousarray(sg.transpose(1, 0, 2)),
        "conv_in": np.ascontiguousarray(conv.reshape(3, 24, 128).transpose(2, 1, 0)),
        "sr_in": np.ascontiguousarray(sr.reshape(8, 2, 64, 64).transpose(1, 3, 0, 2).reshape(128, 8, 64)),
        "sh_in": _shift_layout(sh),
    }


def state_from_dev(sg, conv, sr, sh):
    return (np.ascontiguousarray(sg.transpose(1, 0, 2)),
            np.ascontiguousarray(conv.transpose(2, 1, 0).reshape(3, 3072)),
            np.ascontiguousarray(sr.reshape(2, 64, 8, 64).transpose(2, 0, 3, 1).reshape(16, 64, 64)),
            _shift_unlayout(sh))


_NC_CACHE = {}


def get_program(NT, n_pre, n_main, with_sample, debug=()):
    key = (NT, n_pre, n_main, with_sample, tuple(debug))
    if key not in _NC_CACHE:
        b = Builder(NT, n_pre, n_main, with_sample, debug)
        b.build()
        _NC_CACHE[key] = b
    return _NC_CACHE[key]


def kernel(**inputs):
    NT = 256
    f = np.float32
    x_prompt = np.asarray(inputs["x_prompt"], f)
    x_sample = np.asarray(inputs["x_sample"], f)
    p_prompt = np.asarray(inputs["p_prompt"], f)[0]
    p_sample = np.asarray(inputs["p_sample"], f)[0]
    B, T, _ = x_prompt.shape
    half = T // 2
    n_main = half // NT
    w = {k: np.asarray(v, f)[0] for k, v in inputs.items()
         if k not in ("x_prompt", "x_sample", "p_prompt", "p_sample", "state_gdn", "cache_gdn_conv",
                      "state_rwkv", "cache_rwkv_shift", "g_final")}
    w["g_final"] = np.asarray(inputs["g_final"], f)
    wd = prep_weights(w)
    bld = get_program(NT, n_main, n_main, True)
    in_maps = []
    zeros_pre = np.zeros((D, half), f)
    for c in range(8):
        b, hf = c // 2, c % 2
        m = dict(wd)
        xT = np.ascontiguousarray(x_prompt[b].T)
        m["xT_pre"] = np.ascontiguousarray(xT[:, 0:half]) if hf == 1 else zeros_pre
        m["xT_main"] = np.ascontiguousarray(xT[:, hf * half:(hf + 1) * half])
        m["pT_main"] = np.ascontiguousarray(p_prompt[b, hf * half:(hf + 1) * half].T)
        m["xT_s"] = np.ascontiguousarray(x_sample[c].T)
        m["pT_s"] = np.ascontiguousarray(p_sample[c].T)
        m.update(state_to_dev(np.asarray(inputs["state_gdn"], f)[0, c], np.asarray(inputs["cache_gdn_conv"], f)[0, c],
                              np.asarray(inputs["state_rwkv"], f)[0, c], np.asarray(inputs["cache_rwkv_shift"], f)[0, c]))
        in_maps.append(m)
    res = run_bass_kernel_spmd(bld.nc, in_maps, core_ids=list(range(8)))
    R = res.results
    y_prompt = np.empty((B, T, D), f)
    y_sample = np.empty((8, 32, D), f)
    sg_p = np.empty((1, B, 8, 128, 128), f)
    conv_p = np.empty((1, B, 3, 3072), f)
    sr_p = np.empty((1, B, 16, 64, 64), f)
    sh_p = np.empty((1, B, 3520), f)
    sg_s = np.empty((1, 8, 8, 128, 128), f)
    conv_s = np.empty((1, 8, 3, 3072), f)
    sr_s = np.empty((1, 8, 16, 64, 64), f)
    sh_s = np.empty((1, 8, 3520), f)
    for c in range(8):
        b, hf = c // 2, c % 2
        r = R[c]
        y_prompt[b, hf * half:(hf + 1) * half] = r["yT_main"].T
        y_sample[c] = r["yT_s"].T
        if hf == 1:
            sg_p[0, b], conv_p[0, b], sr_p[0, b], sh_p[0, b] = state_from_dev(r["sg_p"], r["conv_p"], r["sr_p"], r["sh_p"])
        sg_s[0, c], conv_s[0, c], sr_s[0, c], sh_s[0, c] = state_from_dev(r["sg_s"], r["conv_s"], r["sr_s"], r["sh_s"])
    return (y_prompt, y_sample, sg_p, conv_p, sr_p, sh_p, sg_s, conv_s, sr_s, sh_s)
```

```python
import contextlib
import numpy as np
import concourse.bass as bass
import concourse.mybir as mybir
from concourse.alu_op_type import AluOpType as ALU
from concourse.bass_utils import run_bass_kernel_spmd

F32 = mybir.dt.float32
BF16 = mybir.dt.bfloat16
AF = mybir.ActivationFunctionType

D = 2048
NKC = 16
H_A = 8
H_B = 16
NB_B = 8
CONV_CH = 3072
SHIFT_W = 3520
D_FF = 8192
EPS = 1e-6
LNX_EPS = 64e-5
SEM_LIM = 12000
DMA_LIM = 1500


class Buf:
    __slots__ = ("w", "r", "excl")

    def __init__(self, init=None, excl=False):
        self.w = dict(init) if init else {}
        self.r = {}
        self.excl = excl


class Op:
    __slots__ = ("eng", "stream", "fn", "waits", "idx", "signal", "is_dma", "vc", "batch")


ENGS = ("pe", "act", "dve", "pool", "sp")


class Sched:
    def __init__(self, nc):
        self.nc = nc
        self.ops = {e: [] for e in ENGS}
        self.stream_ops = {}
        self.stack = contextlib.ExitStack()
        self.n_t = 0
        self.clock = {e: {} for e in ENGS}
        self.batches = {}
        self.batch_closed = {}
        self.fence_next = False
        self.pe_kind = None

    def sbuf(self, shape, dtype, name=None):
        self.n_t += 1
        return self.stack.enter_context(self.nc.sbuf_tensor("sb_" + (name or f"t{self.n_t}"), list(shape), dtype))

    def psum(self, shape, dtype, name=None):
        self.n_t += 1
        return self.stack.enter_context(self.nc.psum_tensor(name or f"p{self.n_t}", list(shape), dtype))

    def barrier_state(self, exclude=()):
        return {s: len(sl) - 1 for s, sl in self.stream_ops.items() if sl and s not in exclude}

    @staticmethod
    def _flat(bs):
        out = []
        for b in bs:
            if isinstance(b, (list, tuple)):
                out.extend(Sched._flat(b))
            else:
                out.append(b)
        return out

    def op(self, eng, fn, reads=(), writes=(), dma=None, fence=False):
        reads = self._flat(reads)
        writes = self._flat(writes)
        o = Op()
        o.eng = eng
        o.is_dma = dma is not None
        o.stream = dma if o.is_dma else eng
        o.fn = fn
        o.signal = o.is_dma
        sl = self.stream_ops.setdefault(o.stream, [])
        o.idx = len(sl)
        need = {}
        same = o.stream
        isd = o.is_dma

        def req(s, i):
            if s in self.batches:
                bl = self.batches[s]
                bi = self.stream_ops[s][i].batch
                if bi == len(bl) - 1:
                    self.batch_closed[s] = True
                    i = len(self.stream_ops[s]) - 1
                else:
                    i = bl[bi + 1] - 1
            if need.get(s, -1) < i:
                need[s] = i

        if isd:
            bl = self.batches.setdefault(o.stream, [])
            if not bl or self.batch_closed.get(o.stream, False):
                if bl:
                    req(o.stream, o.idx - 1)
                bl.append(o.idx)
                self.batch_closed[o.stream] = False
            o.batch = len(bl) - 1
        if eng == "pe" and not isd:
            if fence and sl:
                req("pe", len(sl) - 1)
        for b in reads:
            for s, i in b.w.items():
                req(s, i)
            if b.excl:
                for s, i in b.r.items():
                    if s != same:
                        req(s, i)
        pe_self = (same == "pe")
        for b in writes:
            for s, i in b.w.items():
                if not (pe_self and s == same):
                    req(s, i)
            for s, i in b.r.items():
                req(s, i)
        clk = self.clock[eng]
        waits = []
        for s, i in need.items():
            if clk.get(s, -1) >= i:
                continue
            waits.append((s, i))
            p = self.stream_ops[s][i]
            p.signal = True
            for s2, i2 in p.vc.items():
                if clk.get(s2, -1) < i2:
                    clk[s2] = i2
            if clk.get(s, -1) < i:
                clk[s] = i
        o.waits = waits
        vc = dict(clk)
        vc[o.stream] = o.idx
        o.vc = vc
        sl.append(o)
        self.ops[eng].append(o)
        for b in reads:
            b.r[o.stream] = o.idx
        for b in writes:
            b.w = {o.stream: o.idx}
            b.r = {}
        return o

    def emit(self, final_streams=()):
        nc = self.nc
        sig = {}
        n_ep = {}
        for s, sl in self.stream_ops.items():
            if not sl:
                continue
            if sl[0].is_dma:
                bl = self.batches[s] + [len(sl)]
                ep, cnt = 0, 0
                for bi in range(len(bl) - 1):
                    size = bl[bi + 1] - bl[bi]
                    if cnt + size > DMA_LIM:
                        ep += 1
                        cnt = 0
                    for i in range(bl[bi], bl[bi + 1]):
                        cnt += 1
                        sig[(s, i)] = (ep, cnt * 16)
                n_ep[s] = ep + 1
            else:
                n = 0
                for o in sl:
                    if o.signal:
                        ep, v = divmod(n, SEM_LIM)
                        sig[(s, o.idx)] = (ep, v + 1)
                        n += 1
                n_ep[s] = (n + SEM_LIM - 1) // SEM_LIM
        sems = {}
        for s, k in n_ep.items():
            sems[s] = [self.stack.enter_context(nc.semaphore(f"s_{s}_{e}")) for e in range(k)]
        self.n_sems = sum(n_ep.values())
        block = self.stack.enter_context(nc.Block())

        def run_engine(eng_name):
            def body(e):
                for o in self.ops[eng_name]:
                    for (s, i) in o.waits:
                        ep, v = sig[(s, i)]
                        e.wait_ge(sems[s][ep], v)
                    ins = o.fn(e)
                    if o.signal:
                        ep, v = sig[(o.stream, o.idx)]
                        ins.then_inc(sems[o.stream][ep], 16 if o.is_dma else 1)
                for s in final_streams:
                    sl = self.stream_ops.get(s)
                    if sl and sl[0].eng == eng_name:
                        ep, v = sig[(s, len(sl) - 1)]
                        e.wait_ge(sems[s][ep], v)
            return body

        block.tensor(run_engine("pe"))
        block.scalar(run_engine("act"))
        block.vector(run_engine("dve"))
        block.gpsimd(run_engine("pool"))
        block.sync(run_engine("sp"))
        self.stack.close()


class TL:
    __slots__ = ("t", "b")

    def __init__(self, t, b=None):
        self.t = t
        self.b = b if b is not None else Buf()


class Builder:
    def __init__(self, NT, n_pre, n_main, with_sample=True, debug=()):
        self.NT = NT
        self.n_pre = n_pre
        self.n_main = n_main
        self.with_sample = with_sample
        self.debug = set(debug)
        self.nc = bass.Bass("TRN2", target_bir_lowering=False)
        self.S = Sched(self.nc)
        self.dbg_out = {}
        self.arena_gen = None

    def tt(self, out, in0, in1, op, R, W, eng="dve"):
        return self.S.op(eng, lambda e: e.tensor_tensor(out=out, in0=in0, in1=in1, op=op), reads=R, writes=W)

    def ts(self, out, in0, s1, s2, op0, op1, R, W, eng="dve"):
        if op1 is None:
            return self.S.op(eng, lambda e: e.tensor_scalar(out=out, in0=in0, scalar1=s1, scalar2=None, op0=op0),
                             reads=R, writes=W)
        return self.S.op(eng, lambda e: e.tensor_scalar(out=out, in0=in0, scalar1=s1, scalar2=s2, op0=op0, op1=op1),
                         reads=R, writes=W)

    def stt(self, out, in0, scalar, in1, op0, op1, R, W):
        return self.S.op("dve", lambda e: e.scalar_tensor_tensor(out=out, in0=in0, scalar=scalar, in1=in1,
                                                                op0=op0, op1=op1), reads=R, writes=W)

    def act(self, out, in_, func, R, W, bias=None, scale=None):
        kw = {}
        if bias is not None:
            kw["bias"] = bias
        if scale is not None:
            kw["scale"] = scale
        return self.S.op("act", lambda e: e.activation(out=out, in_=in_, func=func, **kw), reads=R, writes=W)

    def cp(self, eng, out, in_, R, W):
        if eng == "act":
            return self.S.op("act", lambda e: e.activation(out=out, in_=in_, func=AF.Copy), reads=R, writes=W)
        return self.S.op(eng, lambda e: e.tensor_copy(out=out, in_=in_), reads=R, writes=W)

    def mm(self, out, lhsT, rhs, start, stop, R, W, fence=False):
        kind = "bf" if lhsT.dtype == BF16 else "f32"
        if kind != self.S.pe_kind:
            fence = True
            self.S.pe_kind = kind
        return self.S.op("pe", lambda e: e.matmul(out, lhsT=lhsT, rhs=rhs, start=start, stop=stop), reads=R, writes=W,
                         fence=fence)

    def tr(self, out, in_, R, W):
        n = in_.shape[0]
        idn = self.ident.t[0:n, 0:n]
        fence = self.S.pe_kind != "f32"
        self.S.pe_kind = "f32"
        return self.S.op("pe", lambda e: e.transpose(out, in_, idn), reads=list(R) + [self.ident.b], writes=W,
                         fence=fence)

    def dma(self, eng, stream, out, in_, R, W, slow=False):
        if slow:
            return self.S.op(eng, lambda e: e.dma_start(out=out, in_=in_, allow_slow_non_contiguous=True),
                             reads=R, writes=W, dma=stream)
        return self.S.op(eng, lambda e: e.dma_start(out=out, in_=in_), reads=R, writes=W, dma=stream)

    def memset(self, ap, val, W, eng="pool"):
        return self.S.op(eng, lambda e: e.memset(ap, val), writes=W)

    def rsqrt(self, out, in_, scale, eps, R, W):
        self.act(out, in_, AF.Ln, R, W, bias=float(eps), scale=float(scale))
        self.act(out, out, AF.Exp, W, W, scale=-0.5)

    def init_psum(self):
        self.pdb = [self.S.psum([128, 1024], F32, name=f"psdb{i}") for i in range(4)]
        self.pbuf = [Buf(excl=True) for _ in range(8)]
        self.prr = 0
        self.psg = None
        self.prg = [0, 0]

    def ps1(self):
        g = self.psg
        if g is None:
            i = self.prr % 8
            self.prr += 1
        else:
            i = 4 * g + self.prg[g] % 4
            self.prg[g] += 1
        return TL(self.pdb[i // 2][:, (i % 2) * 512:(i % 2) * 512 + 512], self.pbuf[i])

    def ps2(self):
        g = self.psg
        if g is None:
            if self.prr % 2:
                self.prr += 1
            i = self.prr % 8
            self.prr += 2
        else:
            if self.prg[g] % 2:
                self.prg[g] += 1
            i = 4 * g + self.prg[g] % 4
            self.prg[g] += 2
        return self.pdb[i // 2], [self.pbuf[i], self.pbuf[i + 1]]

    def init_slabs(self, nbuf=3):
        self.slabs = [TL(self.S.sbuf([128, 16, 512], BF16, name=f"slab{i}")) for i in range(nbuf)]
        self.slab_i = 0

    def load_slab(self, pieces):
        bi = self.slab_i % len(self.slabs)
        sl = self.slabs[bi]
        self.slab_i += 1
        for (src, kc0, c0) in pieces:
            rows, cols = src.shape
            kc = rows // 128
            if rows % 128 == 0:
                self.dma("pool", f"slab{bi}", sl.t[:, kc0:kc0 + kc, c0:c0 + cols],
                         src.rearrange("(k p) c -> p k c", p=128), [], [sl.b])
            else:
                assert rows < 128
                self.dma("pool", f"slab{bi}", sl.t[0:rows, kc0, c0:c0 + cols], src, [], [sl.b])
        return sl

    def declare_io(self):
        nc = self.nc
        NT, n_pre, n_main = self.NT, self.n_pre, self.n_main
        I = {}

        def inp(name, shape):
            I[name] = nc.dram_tensor(name, list(shape), F32, kind="ExternalInput").ap()

        def outp(name, shape):
            I[name] = nc.dram_tensor(name, list(shape), F32, kind="ExternalOutput").ap()

        if n_pre:
            inp("xT_pre", [D, n_pre * NT])
        inp("xT_main", [D, n_main * NT])
        inp("pT_main", [256, n_main * NT])
        outp("yT_main", [D, n_main * NT])
        for g in (["p", "s"] if self.with_sample else ["p"]):
            outp(f"sg_{g}", [128, 8, 128])
            outp(f"conv_{g}", [128, 24, 3])
            outp(f"sr_{g}", [128, 8, 64])
            outp(f"sh_{g}", [128, 28])
        if self.with_sample:
            inp("xT_s", [D, 32])
            inp("pT_s", [256, 32])
            outp("yT_s", [D, 32])
            inp("sg_in", [128, 8, 128])
            inp("conv_in", [128, 24, 3])
            inp("sr_in", [128, 8, 64])
            inp("sh_in", [128, 28])
        inp("w_gdn", [D, 8, 512])
        inp("w_ab", [D, 16])
        inp("w_rw", [D, 8, 384])
        inp("w_lora", [D, 448])
        inp("w_gm", [D, 2, 2048])
        inp("w_up_a", [1024, D])
        inp("w_up_b", [1024, D])
        inp("w_o", [D, D])
        inp("w_ff1", [D, D_FF])
        inp("w_ff2", [D_FF, D])
        inp("w_ple_gate", [D, D])
        inp("w_ple", [256, D])
        inp("w2", [96, 1024])
        inp("a2", [96, 1024])
        inp("g2", [256, 1024])
        inp("gvec", [128, 4, 16])
        inp("convw", [128, 24, 4])
        inp("mu", [128, 28])
        inp("rwp", [128, 7, 8])
        inp("gng", [128, 1])
        inp("adt", [8, 2])
        self.I = I
        for name in self.debug:
            pass

    def setup_consts(self):
        S, I = self.S, self.I
        self.ident = TL(S.sbuf([128, 128], F32, name="ident"))
        self.memset(self.ident.t[:], 0.0, [self.ident.b])
        S.op("pool", lambda e: e.affine_select(out=self.ident.t[:], in_=self.ident.t[:], pattern=[[-1, 128]],
                                               compare_op=ALU.not_equal, fill=1.0, base=0, channel_multiplier=1),
             reads=[self.ident.b], writes=[self.ident.b])
        self.ones = TL(S.sbuf([128, 128], F32, name="ones"))
        self.memset(self.ones.t[:], 1.0, [self.ones.b])
        self.onesbd = TL(S.sbuf([128, 128], F32, name="onesbd"))
        self.memset(self.onesbd.t[:], 0.0, [self.onesbd.b])
        self.memset(self.onesbd.t[0:64, 0:64], 1.0, [self.onesbd.b])
        self.memset(self.onesbd.t[64:128, 64:128], 1.0, [self.onesbd.b])
        self.sel8 = TL(S.sbuf([8, 8, 128], F32, name="sel8"))
        self.memset(self.sel8.t[:], 0.0, [self.sel8.b])
        S.op("pool", lambda e: e.affine_select(out=self.sel8.t[:], in_=self.sel8.t[:], pattern=[[-1, 8], [0, 128]],
                                               compare_op=ALU.not_equal, fill=1.0, base=0, channel_multiplier=1),
             reads=[self.sel8.b], writes=[self.sel8.b])
        self.masks = TL(S.sbuf([64, 4, 64], F32, name="masks"))
        mk = self.masks
        self.memset(mk.t[:], 1.0, [mk.b])
        S.op("pool", lambda e: e.affine_select(out=mk.t[:, 0, :], in_=mk.t[:, 0, :], pattern=[[1, 64]],
                                               compare_op=ALU.is_ge, fill=0.0, base=0, channel_multiplier=-1),
             reads=[mk.b], writes=[mk.b])
        S.op("pool", lambda e: e.affine_select(out=mk.t[:, 1, :], in_=mk.t[:, 1, :], pattern=[[1, 64]],
                                               compare_op=ALU.is_gt, fill=0.0, base=0, channel_multiplier=-1),
             reads=[mk.b], writes=[mk.b])
        S.op("pool", lambda e: e.affine_select(out=mk.t[:, 2, :], in_=mk.t[:, 2, :], pattern=[[-1, 64]],
                                               compare_op=ALU.is_gt, fill=0.0, base=0, channel_multiplier=1),
             reads=[mk.b], writes=[mk.b])
        S.op("pool", lambda e: e.tensor_scalar(out=mk.t[:, 3, :], in0=mk.t[:, 1, :], scalar1=-1.0, scalar2=None,
                                               op0=ALU.mult), reads=[mk.b], writes=[mk.b])
        def ld(name, shape):
            t = TL(S.sbuf(shape, F32, name="c_" + name))
            self.dma("sp", "par", t.t[:], I[name], [], [t.b])
            return t
        self.gvec = ld("gvec", [128, 4, 16])
        self.convw = ld("convw", [128, 24, 4])
        self.mu = ld("mu", [128, 28])
        self.rwp = ld("rwp", [128, 7, 8])
        self.gng = ld("gng", [128, 1])
        self.adt = ld("adt", [8, 2])
        self.omu = TL(S.sbuf([128, 28], F32, name="omu"))
        self.ts(self.omu.t[:], self.mu.t[:], -1.0, 1.0, ALU.mult, ALU.add, [self.mu.b], [self.omu.b])
        self.omka = TL(S.sbuf([128, 8], F32, name="omka"))
        self.ts(self.omka.t[:], self.rwp.t[:, 3, :], -1.0, 1.0, ALU.mult, ALU.add, [self.rwp.b], [self.omka.b])
        self.nA = TL(S.sbuf([8, 1], F32, name="nA"))
        self.act(self.nA.t[:], self.adt.t[:, 0:1], AF.Exp, [self.adt.b], [self.nA.b])
        self.ts(self.nA.t[:], self.nA.t[:], -1.0, None, ALU.mult, None, [self.nA.b], [self.nA.b])
        self.w2 = TL(S.sbuf([96, 1024], BF16, name="w2"))
        self.a2 = TL(S.sbuf([96, 1024], BF16, name="a2"))
        self.g2 = TL(S.sbuf([128, 2, 1024], BF16, name="g2"))
        self.dma("pool", "lora", self.w2.t[:], I["w2"], [], [self.w2.b])
        self.dma("pool", "lora", self.a2.t[:], I["a2"], [], [self.a2.b])
        self.dma("pool", "lora", self.g2.t[:], I["g2"].rearrange("(k p) c -> p k c", p=128), [], [self.g2.b])

    def alloc_state(self):
        S = self.S
        self.Sg = [TL(S.sbuf([128, 128], F32, name=f"Sg{h}")) for h in range(8)]
        self.Pr = [TL(S.sbuf([128, 128], F32, name=f"Pr{i}")) for i in range(8)]
        self.halo_c = TL(S.sbuf([128, 24, 3], F32, name="halo_c"))
        self.halo_x = TL(S.sbuf([128, 28], F32, name="halo_x"))

    def zero_state(self):
        for t in self.Sg + self.Pr:
            self.memset(t.t[:], 0.0, [t.b])
        self.memset(self.halo_c.t[:], 0.0, [self.halo_c.b])
        self.memset(self.halo_x.t[:], 0.0, [self.halo_x.b])

    def load_state(self):
        I = self.I
        for h in range(8):
            self.dma("sp", "stin", self.Sg[h].t[:], I["sg_in"][:, h, :], [], [self.Sg[h].b])
            self.memset(self.Pr[h].t[:], 0.0, [self.Pr[h].b])
            self.dma("sp", "stin", self.Pr[h].t[0:64, 0:64], I["sr_in"][0:64, h, :], [], [self.Pr[h].b])
            self.dma("sp", "stin", self.Pr[h].t[64:128, 64:128], I["sr_in"][64:128, h, :], [], [self.Pr[h].b])
        self.dma("sp", "stin", self.halo_c.t[:], I["conv_in"], [], [self.halo_c.b])
        self.dma("sp", "stin", self.halo_x.t[:], I["sh_in"], [], [self.halo_x.b])

    def store_state(self, g):
        I = self.I
        for h in range(8):
            self.dma("sp", "outq", I[f"sg_{g}"][:, h, :], self.Sg[h].t[:], [self.Sg[h].b], [])
            self.dma("sp", "outq", I[f"sr_{g}"][0:64, h, :], self.Pr[h].t[0:64, 0:64], [self.Pr[h].b], [])
            self.dma("sp", "outq", I[f"sr_{g}"][64:128, h, :], self.Pr[h].t[64:128, 64:128], [self.Pr[h].b], [])
        self.dma("sp", "outq", I[f"conv_{g}"], self.halo_c.t[:], [self.halo_c.b], [])
        self.dma("sp", "outq", I[f"sh_{g}"], self.halo_x.t[:], [self.halo_x.b], [])


    def alloc_work(self):
        S, NT = self.S, self.NT
        self.SW = NT + 4
        self.xt = S.sbuf([128, NKC, NT], F32, name="xt")
        self.xb = [Buf() for _ in range(NKC)]
        self.ht = S.sbuf([128, NKC, NT], BF16, name="ht")
        self.hb = [Buf() for _ in range(NKC)]
        self.oat = S.sbuf([128, 8, NT], BF16, name="oat")
        self.oab = [Buf() for _ in range(8)]
        self.obt = S.sbuf([128, 8, NT], BF16, name="obt")
        self.obb = [Buf() for _ in range(8)]
        self.mgt = S.sbuf([128, NKC, NT], BF16, name="mgt")
        self.mgb = [Buf() for _ in range(NKC)]
        self.pT = TL(S.sbuf([128, 2, NT], BF16, name="pT"))
        self.rs_t = TL(S.sbuf([128, NT], F32, name="rs_t"))
        import os
        self.NA = 78 if os.environ.get("KNOALIAS") else 71
        self.NSLOT = 78
        self.arena = S.sbuf([128, self.NA, self.SW], F32, name="arena")
        self.slot_bufs = [Buf() for _ in range(self.NA)]
        self.mg32 = self.mgt[:].rearrange("p k t -> p (k t)").bitcast(F32)
        assert (self.NSLOT - self.NA) * self.SW <= NKC * NT // 2
        order = list(range(0, 28)) + list(range(32, 48)) + [48, 49, 50, 51]
        self.RMAP = {o: 30 + r for r, o in enumerate(order)}
        self.RMAP[29], self.RMAP[30], self.RMAP[31] = self.RMAP[1], self.RMAP[2], self.RMAP[3]
        self.rmask = {}
        for C in (64, 32):
            t = TL(S.sbuf([128, NT], F32, name=f"rmask{C}"))
            self.memset(t.t[:], 1.0, [t.b])
            self.memset(t.t[:].rearrange("p (c j) -> p c j", j=C)[:, :, 0:1], 0.0, [t.b])
            self.rmask[C] = t
        self.g8 = TL(S.sbuf([8, NT], F32, name="g8"))
        self.G8 = TL(S.sbuf([8, NT], F32, name="G8"))
        self.b8 = TL(S.sbuf([8, NT], F32, name="b8"))
        mc = NT // 32 * 8
        self.Gcol = TL(S.sbuf([64, mc], F32, name="Gcol"))
        self.bcol = TL(S.sbuf([64, mc], F32, name="bcol"))
        self.nbcol = TL(S.sbuf([64, mc], F32, name="nbcol"))
        self.nbeg = TL(S.sbuf([64, mc], F32, name="nbeg"))
        self.kdc = [TL(S.sbuf([64, NT // 32], F32, name=f"kdc{i}")) for i in range(2)]
        self.rv = [TL(S.sbuf([64, 128], BF16, name=f"rv{i}")) for i in range(2)]
        self.knqb = TL(S.sbuf([128, 2, NT], BF16, name="knqb"))
        self.arb = TL(S.sbuf([128, 2 * NT], BF16, name="arb"))
        self.btb = TL(S.sbuf([128, NT], BF16, name="btb"))
        self.ktb = TL(S.sbuf([128, NT], BF16, name="ktb"))
        self.vn = [TL(S.sbuf([64, 128], F32, name=f"vn{i}")) for i in range(2)]
        self.txw = TL(S.sbuf([96, NT], BF16, name="txw"))
        self.xab = TL(S.sbuf([96, NT], BF16, name="xab"))
        self.sxg = TL(S.sbuf([128, 2, NT], BF16, name="sxg"))
        self.WC = TL(S.sbuf([128, NT // 32], F32, name="WC"))
        self.Xs = [TL(S.sbuf([64, 128], BF16, name=f"Xs{i}")) for i in range(2)]
        self.Us = [TL(S.sbuf([64, 128], F32, name=f"Us{i}")) for i in range(2)]
        self.Ys = [TL(S.sbuf([64, 128], F32, name=f"Ys{i}")) for i in range(2)]

    def X(self, k):
        return TL(self.xt[:, k, :], self.xb[k])

    def Hh(self, k):
        return TL(self.ht[:, k, :], self.hb[k])

    def slot(self, i, nslots=1):
        if i >= self.NA:
            assert nslots == 1
            j = i - self.NA
            c0, c1 = j * self.SW, (j + 1) * self.SW
            per = self.NT // 2
            return TL(self.mg32[:, c0:c1], [self.mgb[k] for k in range(c0 // per, (c1 - 1) // per + 1)])
        assert i + nslots <= self.NA
        if nslots == 1:
            return TL(self.arena[:, i, :], self.slot_bufs[i])
        return TL(self.arena[:, i:i + nslots, :].rearrange("p s w -> p (s w)"), self.slot_bufs[i:i + nslots])

    @staticmethod
    def B(*tls):
        out = []
        for t in tls:
            b = t.b if isinstance(t, TL) else t
            if isinstance(b, (list, tuple)):
                out.extend(b)
            else:
                out.append(b)
        return out

    def rmsnorm_x(self, n, which, out_fn=None):
        ps = self.ps1()
        sq = self.slot(0)
        for k in range(NKC):
            xk = self.X(k)
            self.act(sq.t[:, 0:n], xk.t[:, 0:n], AF.Square, [xk.b], [sq.b])
            self.mm(ps.t[:, 0:n], self.ones.t[:], sq.t[:, 0:n], k == 0, k == NKC - 1, [self.ones.b, sq.b], [ps.b])
        self.rsqrt(self.rs_t.t[:, 0:n], ps.t[:, 0:n], 1.0 / D, EPS, [ps.b], [self.rs_t.b])
        for k in range(NKC):
            xk = self.X(k)
            o = out_fn(k) if out_fn else self.Hh(k)
            self.stt(o.t[:, 0:n], xk.t[:, 0:n], self.gvec.t[:, which, k:k + 1], self.rs_t.t[:, 0:n],
                     ALU.mult, ALU.mult, [xk.b, self.gvec.b, self.rs_t.b], self.B(o))

    def dense(self, ps, M, n, slab, col0, rhs_tiles, kc0=0, kp=128):
        nk = len(rhs_tiles)
        for j, r in enumerate(rhs_tiles):
            self.mm(ps.t[0:M, 0:n], slab.t[0:kp, kc0 + j, col0:col0 + M], r.t[0:kp, 0:n], j == 0, j == nk - 1,
                    self.B(slab, r), self.B(ps))

    def neumann(self, P, L, nch, C, pslots, lslots, rslots):
        return self.neumann_multi([(P, L, pslots, lslots, rslots)], nch, C)[0]

    def neumann_multi(self, pairs, nch, C, sl=None):
        sl = sl or self.slot

        def bt_(t):
            return t.t.bitcast(BF16)

        def v3(t):
            return bt_(t)[0:C, 0:nch * C].rearrange("p (c j) -> p c j", j=C)
        idb = self.ident.t[0:C, 0:C].unsqueeze(1).to_broadcast([C, nch, C])
        st = []
        for (P, L, pslots, lslots, rslots) in pairs:
            R = sl(rslots[0])
            self.tt(v3(R), v3(P), idb, ALU.add, self.B(P, self.ident), self.B(R))
            st.append([P, L, R, pslots, lslots, rslots])
        nlev = {64: 5, 32: 4}[C]
        for k in range(1, nlev + 1):
            tmp = []
            for (P, L, R, pslots, lslots, rslots) in st:
                Pb, Lb = bt_(P), bt_(L)
                psL = self.ps1()
                for c in range(nch):
                    cs = slice(c * C, (c + 1) * C)
                    self.mm(psL.t[0:C, cs], Pb[0:C, cs], Lb[0:C, cs], True, True, self.B(P, L), self.B(psL))
                psP = None
                if k < nlev:
                    psP = self.ps1()
                    for c in range(nch):
                        cs = slice(c * C, (c + 1) * C)
                        self.mm(psP.t[0:C, cs], Lb[0:C, cs], Pb[0:C, cs], True, True, self.B(P, L), self.B(psP))
                tmp.append((psL, psP))
            tmp2 = []
            for e, (psL, psP) in zip(st, tmp):
                P, L, R, pslots, lslots, rslots = e
                Lk = sl(lslots[k % 2])
                self.cp("act", bt_(Lk)[0:C, 0:nch * C], psL.t[0:C, 0:nch * C], self.B(psL), self.B(Lk))
                Pk = P
                if k < nlev:
                    Pk = sl(pslots[k % 2])
                    self.cp("dve", bt_(Pk)[0:C, 0:nch * C], psP.t[0:C, 0:nch * C], self.B(psP), self.B(Pk))
                psR = self.ps1()
                for c in range(nch):
                    cs = slice(c * C, (c + 1) * C)
                    self.mm(psR.t[0:C, cs], bt_(Lk)[0:C, cs], bt_(R)[0:C, cs], True, True, self.B(Lk, R), self.B(psR))
                tmp2.append((Lk, Pk, psR))
            for e, (Lk, Pk, psR) in zip(st, tmp2):
                R, rslots = e[2], e[5]
                Rn = sl(rslots[k % 2])
                self.tt(bt_(Rn)[0:C, 0:nch * C], bt_(R)[0:C, 0:nch * C], psR.t[0:C, 0:nch * C], ALU.add,
                        self.B(R, psR), self.B(Rn))
                e[0], e[1], e[2] = Pk, Lk, Rn
        return [e[2] for e in st]

    def gdn_gen(self, n, C, full, all_xb=False):
        I = self.I
        nch = n // C
        Hk = [self.Hh(k) for k in range(NKC)]
        slab = self.load_slab([(I["w_ab"], 0, 0)])
        psA = self.ps1()
        self.dense(psA, 8, n, slab, 0, Hk)
        psB = self.ps1()
        self.dense(psB, 8, n, slab, 8, Hk)
        g8, G8, b8 = self.g8, self.G8, self.b8
        self.act(g8.t[:, 0:n], psA.t[0:8, 0:n], AF.Exp, self.B(psA, self.adt), [g8.b], bias=self.adt.t[:, 1:2])
        self.act(g8.t[:, 0:n], g8.t[:, 0:n], AF.Ln, [g8.b], [g8.b], bias=1.0)
        self.ts(g8.t[:, 0:n], g8.t[:, 0:n], self.nA.t[:, 0:1], None, ALU.mult, None, self.B(g8, self.nA), [g8.b])
        rm = self.rmask[C]
        self.S.op("dve", lambda e: e.tensor_tensor_scan(out=G8.t[:, 0:n], data0=rm.t[0:8, 0:n], data1=g8.t[:, 0:n],
                                                       initial=0.0, op0=ALU.mult, op1=ALU.add),
                  reads=self.B(g8, rm), writes=[G8.b])
        self.act(b8.t[:, 0:n], psB.t[0:8, 0:n], AF.Sigmoid, self.B(psB), [b8.b])
        Gcol, bcol, nbcol, nbeg = self.Gcol, self.bcol, self.nbcol, self.nbeg
        m8 = nch * 8
        for (src, dst) in ((G8, Gcol), (b8, bcol)):
            psT = self.ps1()
            for c in range(nch):
                self.tr(psT.t[0:C, c * 8:(c + 1) * 8], src.t[0:8, c * C:(c + 1) * C], self.B(src), self.B(psT))
            self.cp("dve", dst.t[0:C, 0:m8], psT.t[0:C, 0:m8], self.B(psT), [dst.b])
        self.ts(nbcol.t[0:C, 0:m8], bcol.t[0:C, 0:m8], -1.0, None, ALU.mult, None, [bcol.b], [nbcol.b])
        self.act(nbeg.t[0:C, 0:m8], Gcol.t[0:C, 0:m8], AF.Exp, [Gcol.b], [nbeg.b])
        self.tt(nbeg.t[0:C, 0:m8], nbeg.t[0:C, 0:m8], nbcol.t[0:C, 0:m8], ALU.mult, [nbeg.b, nbcol.b], [nbeg.b])
        yield
        Gcol3 = Gcol.t[0:C, 0:m8].rearrange("p (c h) -> p c h", h=8)
        bcol3 = bcol.t[0:C, 0:m8].rearrange("p (c h) -> p c h", h=8)
        nbcol3 = nbcol.t[0:C, 0:m8].rearrange("p (c h) -> p c h", h=8)
        nbeg3 = nbeg.t[0:C, 0:m8].rearrange("p (c h) -> p c h", h=8)
        mk = self.masks

        def mask(i):
            return mk.t[0:C, i:i + 1, 0:C].to_broadcast([C, nch, C])

        def v3(t):
            return t.t[0:C, 0:nch * C].rearrange("p (c j) -> p c j", j=C)

        def v3b(t):
            return t.t.bitcast(BF16)[0:C, 0:nch * C].rearrange("p (c j) -> p c j", j=C)

        for h in range(8):
            slab = self.load_slab([(I["w_gdn"][:, h, :], 0, 0)])
            kinds = (0, 1, 2) if (full or all_xb) else (1, 2)
            co = {}
            pss = {}
            for j in kinds + ((3,) if full else ()):
                pss[j] = self.ps1()
                self.dense(pss[j], 128, n, slab, j * 128, Hk)
            yield
            for j in kinds:
                blk = j * 8 + h
                ps = pss[j]
                raw = self.slot(1 + j)
                self.cp("dve", raw.t[:, 0:3], self.halo_c.t[:, blk, :], [self.halo_c.b], [raw.b])
                self.cp("act", raw.t[:, 3:3 + n], ps.t[:, 0:n], self.B(ps), [raw.b])
                self.cp("dve", self.halo_c.t[:, blk, :], raw.t[:, n:n + 3], [raw.b], [self.halo_c.b])
                if j == 0 and not full:
                    continue
                o = self.slot(4 + j)
                cw = self.convw
                self.ts(o.t[:, 0:n], raw.t[:, 0:n], cw.t[:, blk, 0:1], None, ALU.mult, None, self.B(raw, cw), [o.b])
                for tap in range(1, 4):
                    self.stt(o.t[:, 0:n], raw.t[:, tap:tap + n], cw.t[:, blk, tap:tap + 1], o.t[:, 0:n],
                             ALU.mult, ALU.add, self.B(raw, cw, o), [o.b])
                self.act(o.t[:, 0:n], o.t[:, 0:n], AF.Silu, [o.b], [o.b])
                co[j] = o
                yield
            if full:
                ps = pss[3]
                sgate = self.slot(7)
                self.act(sgate.t[:, 0:n], ps.t[:, 0:n], AF.Silu, self.B(ps), [sgate.b])
            for j in kinds:
                if j == 2 or j not in co:
                    continue
                o = co[j]
                sq = self.slot(0)
                self.act(sq.t[:, 0:n], o.t[:, 0:n], AF.Square, [o.b], [sq.b])
                psn = self.ps1()
                self.mm(psn.t[:, 0:n], self.ones.t[:], sq.t[:, 0:n], True, True, self.B(self.ones, sq), self.B(psn))
                rs = self.slot(29)
                self.rsqrt(rs.t[:, 0:n], psn.t[:, 0:n], 1.0, EPS, self.B(psn), [rs.b])
                if j == 0:
                    self.stt(o.t[:, 0:n], o.t[:, 0:n], float(128 ** -0.5), rs.t[:, 0:n], ALU.mult, ALU.mult,
                             self.B(o, rs), [o.b])
                else:
                    self.tt(o.t[:, 0:n], o.t[:, 0:n], rs.t[:, 0:n], ALU.mult, self.B(o, rs), [o.b])
            kn, vv = co[1], co[2]
            yield
            qn = co.get(0)
            knqb = self.knqb
            self.cp("act", knqb.t[:, 0, 0:n], kn.t[:, 0:n], [kn.b], [knqb.b])
            if full:
                self.cp("act", knqb.t[:, 1, 0:n], qn.t[:, 0:n], [qn.b], [knqb.b])
            psG = self.ps1()
            self.mm(psG.t[:, 0:n], self.sel8.t[:, h, :], G8.t[:, 0:n], True, True, self.B(self.sel8, G8), self.B(psG))
            Grow = self.slot(8)
            self.cp("dve", Grow.t[:, 0:n], psG.t[:, 0:n], self.B(psG), [Grow.b])
            eG = self.slot(10)
            self.act(eG.t[:, 0:n], psG.t[:, 0:n], AF.Exp, self.B(psG), [eG.b])
            psBt = self.ps1()
            self.mm(psBt.t[:, 0:n], self.sel8.t[:, h, :], b8.t[:, 0:n], True, True, self.B(self.sel8, b8), self.B(psBt))
            brow = self.slot(9)
            self.cp("act", brow.t[:, 0:n], psBt.t[:, 0:n], self.B(psBt), [brow.b])
            if full:
                qdec = self.slot(11)
                self.tt(qdec.t[:, 0:n], qn.t[:, 0:n], eG.t[:, 0:n], ALU.mult, self.B(qn, eG), [qdec.b])
            yield
            psKK = self.ps1()
            for c in range(nch):
                cs = slice(c * C, (c + 1) * C)
                self.mm(psKK.t[0:C, cs], knqb.t[:, 0, cs], knqb.t[:, 0, cs], True, True, [knqb.b], self.B(psKK))
            if full:
                psQK = self.ps1()
                for c in range(nch):
                    cs = slice(c * C, (c + 1) * C)
                    self.mm(psQK.t[0:C, cs], knqb.t[:, 0, cs], knqb.t[:, 1, cs], True, True, [knqb.b], self.B(psQK))
            Gc_h = Gcol3[:, :, h:h + 1].to_broadcast([C, nch, C])
            t1 = self.slot(12)
            self.tt(v3(t1), v3(Grow), Gc_h, ALU.subtract, self.B(Grow, Gcol), [t1.b])
            tU = self.slot(13)
            self.tt(v3(tU), v3(t1), mask(0), ALU.mult, self.B(t1, mk), [tU.b])
            self.act(tU.t[0:C, 0:nch * C], tU.t[0:C, 0:nch * C], AF.Exp, [tU.b], [tU.b])
            self.tt(v3(tU), v3(tU), mask(0), ALU.mult, self.B(tU, mk), [tU.b])
            if full:
                QKT = self.slot(14)
                self.tt(v3(QKT), v3(psQK), v3(tU), ALU.mult, self.B(psQK, tU), [QKT.b])
            mbs = self.slot(15)
            self.tt(v3(mbs), v3(brow), mask(3), ALU.mult, self.B(brow, mk), [mbs.b])
            self.tt(v3(mbs), v3(mbs), v3(tU), ALU.mult, self.B(mbs, tU), [mbs.b])
            P0 = self.slot(16)
            self.tt(v3b(P0), v3(psKK), v3(mbs), ALU.mult, self.B(psKK, mbs), [P0.b])
            yield
            tL = self.slot(18)
            self.tt(v3(tL), v3(t1), mask(2), ALU.mult, self.B(t1, mk), [tL.b])
            self.act(tL.t[0:C, 0:nch * C], tL.t[0:C, 0:nch * C], AF.Exp, [tL.b], [tL.b], scale=-1.0)
            mbL = self.slot(19)
            self.tt(v3(mbL), mask(2), nbcol3[:, :, h:h + 1].to_broadcast([C, nch, C]), ALU.mult,
                    self.B(mk, nbcol), [mbL.b])
            self.tt(v3(mbL), v3(mbL), v3(tL), ALU.mult, self.B(mbL, tL), [mbL.b])
            L0 = self.slot(17)
            self.tt(v3b(L0), v3(psKK), v3(mbL), ALU.mult, self.B(psKK, mbL), [L0.b])
            yield
            R = self.neumann(P0, L0, nch, C, (16, 20), (17, 21), (22, 23))
            yield
            R3 = v3b(R)
            kdc = self.kdc[h % 2]
            glast = Grow.t[0:C, 0:n].rearrange("p (c j) -> p c j", j=C)[:, :, C - 1:C]
            self.tt(kdc.t[0:C, 0:nch].unsqueeze(2), glast, Gcol3[:, :, h:h + 1], ALU.subtract,
                    self.B(Grow, Gcol), [kdc.b])
            self.act(kdc.t[0:C, 0:nch], kdc.t[0:C, 0:nch], AF.Exp, [kdc.b], [kdc.b])
            kdec = self.slot(24, 2)
            bv = self.slot(26, 2)
            for (src, dst, colv) in ((kn, kdec, kdc.t[0:C, 0:nch].unsqueeze(2).to_broadcast([C, nch, 128])),
                                     (vv, bv, bcol3[:, :, h:h + 1].to_broadcast([C, nch, 128]))):
                if nch * 128 <= 512:
                    pt = self.ps1()
                    ptt, ptb = pt.t, self.B(pt)
                else:
                    ptt, ptb = self.ps2()
                for c in range(nch):
                    self.tr(ptt[0:C, c * 128:(c + 1) * 128], src.t[:, c * C:(c + 1) * C], self.B(src), ptb)
                self.tt(dst.t[0:C, 0:nch * 128].rearrange("p (c d) -> p c d", d=128),
                        ptt[0:C, 0:nch * 128].rearrange("p (c d) -> p c d", d=128), colv, ALU.mult,
                        ptb + self.B(kdc, bcol), self.B(dst))
            kdec3 = kdec.t[0:C, 0:nch * 128].rearrange("p (c d) -> p c d", d=128)
            bv3 = bv.t[0:C, 0:nch * 128].rearrange("p (c d) -> p c d", d=128)
            if full:
                QKT3 = v3(QKT)
                o_h = self.slot(28)
            Sg = self.Sg[h]
            for c in range(nch):
                cs = slice(c * C, (c + 1) * C)
                psk = self.ps1()
                self.mm(psk.t[0:C, 0:128], kn.t[:, cs], Sg.t[:, :], True, True, self.B(kn, Sg), self.B(psk))
                rv = self.rv[c % 2]
                self.stt(rv.t[0:C, :], psk.t[0:C, 0:128], nbeg3[:, c, h:h + 1], bv3[:, c, :], ALU.mult, ALU.add,
                         self.B(psk, nbeg, bv), [rv.b])
                psv = self.ps1()
                self.mm(psv.t[0:C, 0:128], R3[:, c, :], rv.t[0:C, :], True, True, self.B(R, rv), self.B(psv))
                yield
                vn = self.vn[c % 2]
                self.cp("act", vn.t[0:C, :], psv.t[0:C, 0:128], self.B(psv), [vn.b])
                yield
                if full:
                    pso = self.ps1()
                    self.mm(pso.t[:, 0:C], Sg.t[:, :], qdec.t[:, cs], True, False, self.B(Sg, qdec), self.B(pso))
                    self.mm(pso.t[:, 0:C], vn.t[0:C, :], QKT3[:, c, :], False, True, self.B(vn, QKT), self.B(pso))
                    self.cp("act", o_h.t[:, cs], pso.t[:, 0:C], self.B(pso), [o_h.b])
                pss = self.ps1()
                self.mm(pss.t[:, 0:128], kdec3[:, c, :], vn.t[0:C, :], True, True, self.B(kdec, vn), self.B(pss))
                self.stt(Sg.t[:, :], Sg.t[:, :], eG.t[:, c * C + C - 1:c * C + C], pss.t[:, 0:128], ALU.mult, ALU.add,
                         self.B(Sg, eG, pss), [Sg.b])
                yield
            if full:
                sq = self.slot(0)
                self.act(sq.t[:, 0:n], o_h.t[:, 0:n], AF.Square, [o_h.b], [sq.b])
                psn = self.ps1()
                self.mm(psn.t[:, 0:n], self.ones.t[:], sq.t[:, 0:n], True, True, self.B(self.ones, sq), self.B(psn))
                rs = self.slot(29)
                self.rsqrt(rs.t[:, 0:n], psn.t[:, 0:n], 1.0 / 128, EPS, self.B(psn), [rs.b])
                self.stt(o_h.t[:, 0:n], o_h.t[:, 0:n], self.gng.t[:, 0:1], rs.t[:, 0:n], ALU.mult, ALU.mult,
                         self.B(o_h, self.gng, rs), [o_h.b])
                self.tt(self.oat[:, h, 0:n], o_h.t[:, 0:n], sgate.t[:, 0:n], ALU.mult, self.B(o_h, sgate),
                        [self.oab[h]])
                self.tap(f"oa{h}", self.oat[:, h, 0:n], [self.oab[h]], n)

    def tap(self, name, ap, bufs, n=None):
        full = f"{self.tile_tag}_{name}"
        if name not in self.debug and full not in self.debug:
            return
        shape = list(ap.shape)
        d = self.nc.dram_tensor("dbg_" + full, shape, ap.dtype, kind="ExternalOutput").ap()
        self.dbg_out["dbg_" + full] = shape
        self.dma("sp", "outq", d, ap, bufs, [])

    def rwkv_gen(self, n, C, full, all_xb):
        I = self.I
        nch = n // C
        RMAP = self.RMAP

        def rsl(i, nslots=1):
            return self.slot(RMAP[i], nslots)
        Hk = [self.Hh(k) for k in range(NKC)]
        mu, omu, hx, rwp = self.mu, self.omu, self.halo_x, self.rwp
        mk = self.masks

        def mask(i):
            return mk.t[0:C, i:i + 1, 0:C].to_broadcast([C, nch, C])

        def v3(t, w=None):
            w = w or C
            return t.t[0:C, 0:nch * w].rearrange("p (c j) -> p c j", j=w)

        def f3(t):
            return t.t[:, 0:n].rearrange("p (c j) -> p c j", j=C)

        def v3b(t):
            return t.t.bitcast(BF16)[0:C, 0:nch * C].rearrange("p (c j) -> p c j", j=C)

        def shift_mix(ps, M, blkid, rawslot, xmslot, need_xm=True):
            raw = rsl(rawslot)
            self.cp("dve", raw.t[0:M, 0:1], hx.t[0:M, blkid:blkid + 1], [hx.b], [raw.b])
            self.cp("act", raw.t[0:M, 1:1 + n], ps.t[0:M, 0:n], self.B(ps), [raw.b])
            self.cp("dve", hx.t[0:M, blkid:blkid + 1], raw.t[0:M, n:n + 1], [raw.b], [hx.b])
            if not need_xm:
                return None
            tmp = rsl(13)
            self.ts(tmp.t[0:M, 0:n], raw.t[0:M, 0:n], mu.t[0:M, blkid:blkid + 1], None, ALU.mult, None,
                    self.B(raw, mu), [tmp.b])
            xm = rsl(xmslot)
            self.stt(xm.t[0:M, 0:n], raw.t[0:M, 1:1 + n], omu.t[0:M, blkid:blkid + 1], tmp.t[0:M, 0:n],
                     ALU.mult, ALU.add, self.B(raw, omu, tmp), [xm.b])
            return xm

        slab = self.load_slab([(I["w_lora"], 0, 0)])
        lor = []
        for (blkid, col0, M, kind) in ((24, 0, 96, "xw"), (25, 96, 96, "xa"), (26, 192, 128, "xg0"), (27, 320, 128, "xg1")):
            isg = kind.startswith("xg")
            if isg and not (full or all_xb):
                continue
            ps = self.ps1()
            self.dense(ps, M, n, slab, col0, Hk)
            lor.append((blkid, M, kind, isg, ps))
        yield
        for (blkid, M, kind, isg, ps) in lor:
            xm = shift_mix(ps, M, blkid, 1, 4, need_xm=(full or not isg))
            if xm is None:
                continue
            if kind == "xw":
                self.act(self.txw.t[0:96, 0:n], xm.t[0:96, 0:n], AF.Tanh, [xm.b], [self.txw.b])
            elif kind == "xa":
                self.cp("dve", self.xab.t[0:96, 0:n], xm.t[0:96, 0:n], [xm.b], [self.xab.b])
            else:
                gi = 0 if kind == "xg0" else 1
                self.act(self.sxg.t[:, gi, 0:n], xm.t[:, 0:n], AF.Sigmoid, [xm.b], [self.sxg.b])
            yield
        import os
        ksub = int(os.environ.get("KSUB", "9"))
        if ksub < 2:
            return
        for i in range(8):
            bc = slice(i * 128, (i + 1) * 128)
            slab = self.load_slab([(I["w_rw"][:, i, :], 0, 0)])
            xm = {}
            for j in (0, 1, 2):
                if j == 0 and not (full or all_xb):
                    continue
                ps = self.ps1()
                self.dense(ps, 128, n, slab, j * 128, Hk)
                xm[j] = shift_mix(ps, 128, j * 8 + i, 1 + j, 4 + j, need_xm=(full or j != 0))
            r, k, v = xm.get(0), xm[1], xm[2]
            yield
            if ksub < 3:
                continue
            psw = self.ps1()
            self.mm(psw.t[:, 0:n], self.w2.t[0:96, bc], self.txw.t[0:96, 0:n], True, True,
                    self.B(self.w2, self.txw), self.B(psw))
            logw = rsl(7)
            self.act(logw.t[:, 0:n], psw.t[:, 0:n], AF.Sigmoid, self.B(psw, rwp), [logw.b], bias=rwp.t[:, 0, i:i + 1])
            self.ts(logw.t[:, 0:n], logw.t[:, 0:n], float(-np.exp(-0.5)), None, ALU.mult, None, [logw.b], [logw.b])
            Lc = rsl(8)
            rm = self.rmask[C]
            self.S.op("dve", lambda e, Lc=Lc, logw=logw, rm=rm: e.tensor_tensor_scan(
                out=Lc.t[:, 0:n], data0=rm.t[:, 0:n], data1=logw.t[:, 0:n], initial=0.0, op0=ALU.mult, op1=ALU.add),
                reads=self.B(logw, rm), writes=[Lc.b])
            psa = self.ps1()
            self.mm(psa.t[:, 0:n], self.a2.t[0:96, bc], self.xab.t[0:96, 0:n], True, True,
                    self.B(self.a2, self.xab), self.B(psa))
            a = rsl(9)
            self.act(a.t[:, 0:n], psa.t[:, 0:n], AF.Sigmoid, self.B(psa, rwp), [a.b], bias=rwp.t[:, 1, i:i + 1])
            if full:
                psg = self.ps1()
                self.mm(psg.t[:, 0:n], self.g2.t[:, 0, bc], self.sxg.t[:, 0, 0:n], True, False,
                        self.B(self.g2, self.sxg), self.B(psg))
                self.mm(psg.t[:, 0:n], self.g2.t[:, 1, bc], self.sxg.t[:, 1, 0:n], False, True,
                        self.B(self.g2, self.sxg), self.B(psg))
                gate = rsl(22)
                self.cp("act", gate.t[:, 0:n], psg.t[:, 0:n], self.B(psg), [gate.b])
            yield
            kk = rsl(10)
            self.ts(kk.t[:, 0:n], k.t[:, 0:n], rwp.t[:, 2, i:i + 1], None, ALU.mult, None, self.B(k, rwp), [kk.b])
            sq = rsl(0)
            self.act(sq.t[:, 0:n], kk.t[:, 0:n], AF.Square, [kk.b], [sq.b])
            psn = self.ps1()
            self.mm(psn.t[:, 0:n], self.onesbd.t[:], sq.t[:, 0:n], True, True, self.B(self.onesbd, sq), self.B(psn))
            rs = rsl(29)
            self.rsqrt(rs.t[:, 0:n], psn.t[:, 0:n], 1.0, EPS, self.B(psn), [rs.b])
            self.tt(kk.t[:, 0:n], kk.t[:, 0:n], rs.t[:, 0:n], ALU.mult, self.B(kk, rs), [kk.b])
            kh = rsl(11)
            self.ts(kh.t[:, 0:n], a.t[:, 0:n], rwp.t[:, 3, i:i + 1], self.omka.t[:, i:i + 1], ALU.mult, ALU.add,
                    self.B(a, rwp, self.omka), [kh.b])
            self.tt(kh.t[:, 0:n], kh.t[:, 0:n], k.t[:, 0:n], ALU.mult, self.B(kh, k), [kh.b])
            kka = rsl(12)
            self.tt(kka.t[:, 0:n], kk.t[:, 0:n], a.t[:, 0:n], ALU.mult, self.B(kk, a), [kka.b])
            yield
            Winv = rsl(14)
            self.act(Winv.t[:, 0:n], Lc.t[:, 0:n], AF.Exp, [Lc.b], [Winv.b], scale=-1.0)
            Wprev = rsl(15)
            self.tt(Wprev.t[:, 0:n], Lc.t[:, 0:n], logw.t[:, 0:n], ALU.subtract, self.B(Lc, logw), [Wprev.b])
            self.act(Wprev.t[:, 0:n], Wprev.t[:, 0:n], AF.Exp, [Wprev.b], [Wprev.b])
            AR = rsl(16, 2)
            AR4 = AR.t[:, 0:2 * n].rearrange("p (c two j) -> p c two j", two=2, j=C)
            self.tt(AR4[:, :, 0, :], f3(kk), f3(Wprev), ALU.mult, self.B(kk, Wprev), self.B(AR))
            if full:
                Wt = rsl(13)
                self.act(Wt.t[:, 0:n], Lc.t[:, 0:n], AF.Exp, [Lc.b], [Wt.b])
                self.tt(AR4[:, :, 1, :], f3(r), f3(Wt), ALU.mult, self.B(r, Wt), self.B(AR))
            btb, ktb, arb = self.btb, self.ktb, self.arb
            self.stt(btb.t[:, 0:n], kka.t[:, 0:n], -1.0, Winv.t[:, 0:n], ALU.mult, ALU.mult, self.B(kka, Winv), [btb.b])
            self.tt(ktb.t[:, 0:n], kh.t[:, 0:n], Winv.t[:, 0:n], ALU.mult, self.B(kh, Winv), [ktb.b])
            if full:
                self.cp("act", arb.t[:, 0:2 * n], AR.t[:, 0:2 * n], self.B(AR), [arb.b])
            else:
                arb4 = arb.t[:, 0:2 * n].rearrange("p (c two j) -> p c two j", two=2, j=C)
                self.cp("act", arb4[:, :, 0, :], AR4[:, :, 0, :], self.B(AR), [arb.b])
            edl = rsl(31)
            Lc3 = f3(Lc)
            self.tt(f3(edl), Lc3[:, :, C - 1:C].to_broadcast([128, nch, C]), Lc3, ALU.subtract, [Lc.b], [edl.b])
            self.act(edl.t[:, 0:n], edl.t[:, 0:n], AF.Exp, [edl.b], [edl.b])
            bdec = rsl(20)
            self.stt(bdec.t[:, 0:n], kka.t[:, 0:n], -1.0, edl.t[:, 0:n], ALU.mult, ALU.mult, self.B(kka, edl), [bdec.b])
            kdec = rsl(21)
            self.tt(kdec.t[:, 0:n], kh.t[:, 0:n], edl.t[:, 0:n], ALU.mult, self.B(kh, edl), [kdec.b])
            WC = self.WC
            self.act(WC.t[:, 0:nch].unsqueeze(2), Lc3[:, :, C - 1:C], AF.Exp, [Lc.b], [WC.b])
            if full:
                rk = rsl(0)
                self.stt(rk.t[:, 0:n], r.t[:, 0:n], rwp.t[:, 4, i:i + 1], kh.t[:, 0:n], ALU.mult, ALU.mult,
                         self.B(r, rwp, kh), [rk.b])
                psb = self.ps1()
                self.mm(psb.t[:, 0:n], self.onesbd.t[:], rk.t[:, 0:n], True, True, self.B(self.onesbd, rk), self.B(psb))
                bvv = rsl(30)
                self.tt(bvv.t[:, 0:n], psb.t[:, 0:n], v.t[:, 0:n], ALU.mult, self.B(psb, v), [bvv.b])
            if ksub < 4:
                continue
            yield
            tok = []
            for (src, s0) in ((v, 32), (bdec, 34), (kdec, 36)):
                dst = rsl(s0, 2)
                if nch * 128 <= 512:
                    pt = self.ps1()
                    ptt, ptb = pt.t, self.B(pt)
                else:
                    ptt, ptb = self.ps2()
                for c in range(nch):
                    self.tr(ptt[0:C, c * 128:(c + 1) * 128], src.t[:, c * C:(c + 1) * C], self.B(src), ptb)
                self.cp("act" if s0 == 34 else "dve", dst.t[0:C, 0:nch * 128], ptt[0:C, 0:nch * 128], ptb, self.B(dst))
                tok.append(dst)
            Vt, Bt, Kt = tok
            Vt3, Bt3, Kt3 = (t.t[0:C, 0:nch * 128].rearrange("p (c d) -> p c d", d=128) for t in tok)
            if ksub < 5:
                continue
            Rr, RabT, AakT, RakT = [], [], [], []
            W2 = 2 * C if full else C
            pairs = []
            for hp in (0, 1):
                pr = slice(hp * 64, hp * 64 + 64)
                outs = []
                for gi_, (lt, slots_) in enumerate(((btb, (26 if hp == 0 else 48, 38 + 3 * hp)), (ktb, (39 + 3 * hp, 40 + 3 * hp)))):
                    if nch * W2 <= 512:
                        pg = self.ps1()
                        pgt, pgb = pg.t, self.B(pg)
                    else:
                        pgt, pgb = self.ps2()
                    for c in range(nch):
                        self.mm(pgt[0:C, c * W2:(c + 1) * W2], lt.t[pr, c * C:(c + 1) * C],
                                arb.t[pr, c * 2 * C:c * 2 * C + W2], True, True, self.B(lt, arb), pgb)
                    pg3 = pgt[0:C, 0:nch * W2].rearrange("p (c w) -> p c w", w=W2)
                    oA = rsl(slots_[0])
                    self.tt(v3b(oA) if gi_ == 0 else v3(oA), pg3[:, :, 0:C], mask(1), ALU.mult, pgb + [mk.b], [oA.b])
                    outs.append(oA)
                    if full:
                        oR = rsl(slots_[1])
                        self.tt(v3(oR), pg3[:, :, C:2 * C], mask(0), ALU.mult, pgb + [mk.b], [oR.b])
                        outs.append(oR)
                    else:
                        outs.append(None)
                AabT, RabT_h, AakT_h, RakT_h = outs
                pg = self.ps1()
                for c in range(nch):
                    self.mm(pg.t[0:C, c * C:(c + 1) * C], arb.t[pr, c * 2 * C:c * 2 * C + C], btb.t[pr, c * C:(c + 1) * C],
                            True, True, self.B(arb, btb), self.B(pg))
                Aab = rsl(24 if hp == 0 else 50)
                self.tt(v3b(Aab), v3(pg), mask(2), ALU.mult, self.B(pg, mk), [Aab.b])
                pairs.append((AabT, Aab, (26, 27) if hp == 0 else (48, 49), (24, 25) if hp == 0 else (50, 51),
                              (44 + 2 * hp, 45 + 2 * hp)))
                RabT.append(RabT_h)
                AakT.append(AakT_h)
                RakT.append(RakT_h)
                yield
            Rr = self.neumann_multi(pairs, nch, C, sl=rsl)
            yield
            Pr = self.Pr[i]
            yfm = rsl(23)
            if ksub < 6:
                continue
            for c in range(nch):
                cs = slice(c * C, (c + 1) * C)
                psX = self.ps1()
                self.mm(psX.t[0:C, 0:128], AR.t[:, c * 2 * C:c * 2 * C + C], Pr.t[:, :], True, False,
                        self.B(AR, Pr), self.B(psX))
                for hp in (0, 1):
                    hc = slice(hp * 64, hp * 64 + 64)
                    self.mm(psX.t[0:C, hc], v3(AakT[hp])[:, c, :], Vt3[:, c, hc], False, hp == 1,
                            self.B(AakT[hp], Vt), self.B(psX))
                Xs = self.Xs[c % 2]
                self.cp("act", Xs.t[0:C, :], psX.t[0:C, 0:128], self.B(psX), [Xs.b])
                yield
                psU = self.ps1()
                for hp in (0, 1):
                    hc = slice(hp * 64, hp * 64 + 64)
                    self.mm(psU.t[0:C, hc], v3b(Rr[hp])[:, c, :], Xs.t[0:C, hc], True, True, self.B(Rr[hp], Xs), self.B(psU))
                Us = self.Us[c % 2]
                self.cp("dve", Us.t[0:C, :], psU.t[0:C, 0:128], self.B(psU), [Us.b])
                yield
                if full:
                    psY = self.ps1()
                    self.mm(psY.t[0:C, 0:128], AR.t[:, c * 2 * C + C:(c + 1) * 2 * C], Pr.t[:, :], True, False,
                            self.B(AR, Pr), self.B(psY))
                    for hp in (0, 1):
                        hc = slice(hp * 64, hp * 64 + 64)
                        self.mm(psY.t[0:C, hc], v3(RabT[hp])[:, c, :], Us.t[0:C, hc], False, False,
                                self.B(RabT[hp], Us), self.B(psY))
                        self.mm(psY.t[0:C, hc], v3(RakT[hp])[:, c, :], Vt3[:, c, hc], False, hp == 1,
                                self.B(RakT[hp], Vt), self.B(psY))
                    Ys = self.Ys[c % 2]
                    self.cp("dve", Ys.t[0:C, :], psY.t[0:C, 0:128], self.B(psY), [Ys.b])
                    psT = self.ps1()
                    self.tr(psT.t[:, 0:C], Ys.t[0:C, :], [Ys.b], self.B(psT))
                    self.cp("act", yfm.t[:, cs], psT.t[:, 0:C], self.B(psT), [yfm.b])
                psP = self.ps1()
                self.mm(psP.t[:, 0:128], Bt3[:, c, :], Us.t[0:C, :], True, False, self.B(Bt, Us), self.B(psP))
                self.mm(psP.t[:, 0:128], Kt3[:, c, :], Vt3[:, c, :], False, True, self.B(Kt, Vt), self.B(psP))
                for hp in (0, 1):
                    pr = slice(hp * 64, hp * 64 + 64)
                    self.stt(Pr.t[pr, pr], Pr.t[pr, pr], WC.t[pr, c:c + 1], psP.t[pr, pr],
                             ALU.mult, ALU.add, self.B(Pr, WC, psP), [Pr.b])
                yield
            if full and ksub >= 7:
                psm = self.ps1()
                self.mm(psm.t[:, 0:n], self.onesbd.t[:], yfm.t[:, 0:n], True, True, self.B(self.onesbd, yfm), self.B(psm))
                self.stt(yfm.t[:, 0:n], psm.t[:, 0:n], -1.0 / 64, yfm.t[:, 0:n], ALU.mult, ALU.add,
                         self.B(psm, yfm), [yfm.b])
                sq = rsl(0)
                self.act(sq.t[:, 0:n], yfm.t[:, 0:n], AF.Square, [yfm.b], [sq.b])
                psv = self.ps1()
                self.mm(psv.t[:, 0:n], self.onesbd.t[:], sq.t[:, 0:n], True, True, self.B(self.onesbd, sq), self.B(psv))
                rs = rsl(29)
                self.rsqrt(rs.t[:, 0:n], psv.t[:, 0:n], 1.0 / 64, LNX_EPS, self.B(psv), [rs.b])
                self.tt(yfm.t[:, 0:n], yfm.t[:, 0:n], rs.t[:, 0:n], ALU.mult, self.B(yfm, rs), [yfm.b])
                self.ts(yfm.t[:, 0:n], yfm.t[:, 0:n], rwp.t[:, 5, i:i + 1], rwp.t[:, 6, i:i + 1], ALU.mult, ALU.add,
                        self.B(yfm, rwp), [yfm.b])
                self.tt(yfm.t[:, 0:n], yfm.t[:, 0:n], bvv.t[:, 0:n], ALU.add, self.B(yfm, bvv), [yfm.b])
                self.tt(self.obt[:, i, 0:n], yfm.t[:, 0:n], gate.t[:, 0:n], ALU.mult, self.B(yfm, gate), [self.obb[i]])
                self.tap(f"ob{i}", self.obt[:, i, 0:n], [self.obb[i]], n)

    def tail(self, n, pT_src, yT_dst):
        I = self.I
        Hk = [self.Hh(k) for k in range(NKC)]
        OA = [TL(self.oat[:, k, :], self.oab[k]) for k in range(8)]
        OB = [TL(self.obt[:, k, :], self.obb[k]) for k in range(8)]
        MG = [TL(self.mgt[:, k, :], self.mgb[k]) for k in range(NKC)]
        pT = self.pT
        self.dma("pool", "pin", pT.t[:, :, 0:n], pT_src.rearrange("(k p) t -> p k t", p=128), [], [pT.b])
        for q in range(4):
            qc = slice(q * 512, (q + 1) * 512)
            sgs = {}
            for gi in (0, 1):
                sG = self.load_slab([(I["w_gm"][:, gi, qc], 0, 0)])
                for jj in range(4):
                    pg = self.ps1()
                    self.dense(pg, 128, n, sG, jj * 128, Hk)
                    sg = self.slot(20 + gi * 4 + jj)
                    self.act(sg.t[:, 0:n], pg.t[:, 0:n], AF.Sigmoid, self.B(pg), [sg.b])
                    sgs[(gi, jj)] = sg
            sA = self.load_slab([(I["w_up_a"][:, qc], 0, 0), (I["w_up_b"][:, qc], 8, 0)])
            for jj in range(4):
                j = q * 4 + jj
                pa = self.ps1()
                self.dense(pa, 128, n, sA, jj * 128, OA, kc0=0)
                pb = self.ps1()
                self.dense(pb, 128, n, sA, jj * 128, OB, kc0=8)
                sa, sb = sgs[(0, jj)], sgs[(1, jj)]
                self.tt(sa.t[:, 0:n], pa.t[:, 0:n], sa.t[:, 0:n], ALU.mult, self.B(pa, sa), [sa.b])
                self.tt(sb.t[:, 0:n], pb.t[:, 0:n], sb.t[:, 0:n], ALU.mult, self.B(pb, sb), [sb.b])
                self.tt(self.mgt[:, j, 0:n], sa.t[:, 0:n], sb.t[:, 0:n], ALU.add, self.B(sa, sb), [self.mgb[j]])
        for q in range(4):
            s = self.load_slab([(I["w_o"][:, q * 512:(q + 1) * 512], 0, 0)])
            for jj in range(4):
                j = q * 4 + jj
                ps = self.ps1()
                self.dense(ps, 128, n, s, jj * 128, MG)
                xj = self.X(j)
                self.tt(xj.t[:, 0:n], xj.t[:, 0:n], ps.t[:, 0:n], ALU.add, self.B(xj, ps), [xj.b])
        self.tap("x1", self.xt[:, :, 0:n], self.xb)
        self.rmsnorm_x(n, 1)
        for g in range(4):
            for q in range(4):
                c0 = g * 2048 + q * 512
                s = self.load_slab([(I["w_ff1"][:, c0:c0 + 512], 0, 0)])
                for jj in range(4):
                    jh = q * 4 + jj
                    ps = self.ps1()
                    self.dense(ps, 128, n, s, jj * 128, Hk)
                    r = self.slot(1 + jj % 2)
                    self.act(r.t[:, 0:n], ps.t[:, 0:n], AF.Relu, self.B(ps), [r.b])
                    self.tt(self.mgt[:, jh, 0:n], r.t[:, 0:n], r.t[:, 0:n], ALU.mult, [r.b], [self.mgb[jh]])
            for q in range(4):
                s = self.load_slab([(I["w_ff2"][g * 2048:(g + 1) * 2048, q * 512:(q + 1) * 512], 0, 0)])
                for jj in range(4):
                    j = q * 4 + jj
                    ps = self.ps1()
                    self.dense(ps, 128, n, s, jj * 128, MG)
                    xj = self.X(j)
                    self.tt(xj.t[:, 0:n], xj.t[:, 0:n], ps.t[:, 0:n], ALU.add, self.B(xj, ps), [xj.b])
        self.tap("x2", self.xt[:, :, 0:n], self.xb)
        self.rmsnorm_x(n, 2)
        P2 = [TL(pT.t[:, 0, :], pT.b), TL(pT.t[:, 1, :], pT.b)]
        for q in range(4):
            qc = slice(q * 512, (q + 1) * 512)
            s = self.load_slab([(I["w_ple_gate"][:, qc], 0, 0)])
            s2 = self.load_slab([(I["w_ple"][:, qc], 0, 0)])
            for jj in range(4):
                j = q * 4 + jj
                pg = self.ps1()
                self.dense(pg, 128, n, s, jj * 128, Hk)
                pp = self.ps1()
                self.dense(pp, 128, n, s2, jj * 128, P2)
                sg = self.slot(1 + jj % 2)
                self.act(sg.t[:, 0:n], pg.t[:, 0:n], AF.Sigmoid, self.B(pg), [sg.b])
                self.tt(sg.t[:, 0:n], sg.t[:, 0:n], pp.t[:, 0:n], ALU.mult, self.B(sg, pp), [sg.b])
                xj = self.X(j)
                self.tt(xj.t[:, 0:n], xj.t[:, 0:n], sg.t[:, 0:n], ALU.add, self.B(xj, sg), [xj.b])
        self.rmsnorm_x(n, 3, out_fn=lambda k: self.slot(4 + k))
        for k in range(NKC):
            sk = self.slot(4 + k)
            self.dma("sp", "outq", yT_dst[k * 128:(k + 1) * 128, :], sk.t[:, 0:n], [sk.b], [])

    def tile(self, tag, xT_src, n, C, full, all_xb=False, pT_src=None, yT_dst=None):
        self.tile_tag = tag
        self.dma("sp", "xin", self.xt[:, :, 0:n], xT_src.rearrange("(k p) t -> p k t", p=128), [], self.xb)
        import os
        stage = int(os.environ.get("KSTAGE", "9"))
        self.rmsnorm_x(n, 0)
        self.tap("h", self.ht[:, :, 0:n], self.hb)
        gens = []
        if stage >= 2:
            gens.append(self.gdn_gen(n, C, full, all_xb))
        if stage >= 3:
            gens.append(self.rwkv_gen(n, C, full, all_xb))
        if os.environ.get("KSEQ"):
            for g in gens:
                for _ in g:
                    pass
            gens = []
        stride = int(os.environ.get("KSTRIDE", "1"))
        gid = {id(g): k for k, g in enumerate(gens)}
        while gens:
            for g in list(gens):
                self.psg = gid[id(g)] if len(gid) > 1 else None
                try:
                    for _ in range(stride):
                        next(g)
                except StopIteration:
                    gens.remove(g)
        self.psg = None
        if full and stage >= 4:
            self.tail(n, pT_src, yT_dst)

    def build(self):
        NT = self.NT
        self.declare_io()
        self.init_psum()
        self.init_slabs(3)
        self.setup_consts()
        self.alloc_state()
        self.alloc_work()
        self.zero_state()
        I = self.I
        for t in range(self.n_pre):
            ts_ = slice(t * NT, (t + 1) * NT)
            self.tile(f"pre{t}", I["xT_pre"][:, ts_], NT, 64, False, all_xb=(t == self.n_pre - 1))
        for t in range(self.n_main):
            ts_ = slice(t * NT, (t + 1) * NT)
            self.tile(f"main{t}", I["xT_main"][:, ts_], NT, 64, True, pT_src=I["pT_main"][:, ts_],
                      yT_dst=I["yT_main"][:, ts_])
        self.store_state("p")
        if self.with_sample:
            self.load_state()
            self.tile("samp", I["xT_s"], 32, 32, True, pT_src=I["pT_s"], yT_dst=I["yT_s"])
            self.store_state("s")
        self.S.emit(final_streams=["outq"])
        return self.nc


def _blk(v):
    return np.ascontiguousarray(v.reshape(-1, 128).T)


def _shift_layout(v):
    out = np.zeros((128, 28), np.float32)
    out[:, 0:24] = v[0:3072].reshape(24, 128).T
    out[0:96, 24] = v[3072:3168]
    out[0:96, 25] = v[3168:3264]
    out[:, 26] = v[3264:3392]
    out[:, 27] = v[3392:3520]
    return out


def _shift_unlayout(a):
    v = np.zeros(3520, np.float32)
    v[0:3072] = a[:, 0:24].T.reshape(-1)
    v[3072:3168] = a[0:96, 24]
    v[3168:3264] = a[0:96, 25]
    v[3264:3392] = a[:, 26]
    v[3392:3520] = a[:, 27]
    return v


def prep_weights(w):
    f = np.float32
    w_in = w["w_in"]
    o = {}
    g = np.empty((D, 8, 512), f)
    for h in range(8):
        g[:, h, 0:128] = w_in[:, h * 128:(h + 1) * 128]
        g[:, h, 128:256] = w_in[:, 1024 + h * 128:1024 + (h + 1) * 128]
        g[:, h, 256:384] = w_in[:, 2048 + h * 128:2048 + (h + 1) * 128]
        g[:, h, 384:512] = w_in[:, 3088 + h * 128:3088 + (h + 1) * 128]
    o["w_gdn"] = g
    o["w_ab"] = np.ascontiguousarray(w_in[:, 3072:3088])
    r = np.empty((D, 8, 384), f)
    for i in range(8):
        r[:, i, 0:128] = w_in[:, 4112 + i * 128:4112 + (i + 1) * 128]
        r[:, i, 128:256] = w_in[:, 5136 + i * 128:5136 + (i + 1) * 128]
        r[:, i, 256:384] = w_in[:, 6160 + i * 128:6160 + (i + 1) * 128]
    o["w_rw"] = r
    o["w_lora"] = np.ascontiguousarray(w_in[:, 7184:7632])
    o["w_gm"] = np.ascontiguousarray(w_in[:, 7632:11728].reshape(D, 2, 2048))
    for k in ("w_up_a", "w_up_b", "w_o", "w_ff1", "w_ff2", "w_ple_gate", "w_ple", "w2", "a2", "g2"):
        o[k] = np.ascontiguousarray(w[k], dtype=f)
    gv = np.empty((128, 4, 16), f)
    for i, k in enumerate(("g_mix", "g_mlp", "g_ple", "g_final")):
        gv[:, i, :] = _blk(w[k])
    o["gvec"] = gv
    cw = w["conv_w"]
    o["convw"] = np.ascontiguousarray(cw.reshape(4, 24, 128).transpose(2, 1, 0))
    o["mu"] = _shift_layout(w["mu_shift"])
    rp = np.empty((128, 7, 8), f)
    for i, k in enumerate(("w0", "a0", "k_k", "k_a", "r_k", "lnx_g", "lnx_b")):
        rp[:, i, :] = _blk(w[k].reshape(-1))
    o["rwp"] = rp
    o["gng"] = np.ascontiguousarray(w["gdn_norm_g"].reshape(128, 1))
    o["adt"] = np.ascontiguousarray(np.stack([w["a_log"], w["dt_bias"]], axis=1))
    return o


def state_to_dev(sg, conv, sr, sh):
    return {
        "sg_in": np.ascontiguousarray(sg.transpose(1, 0, 2)),
        "conv_in": np.ascontiguousarray(conv.reshape(3, 24, 128).transpose(2, 1, 0)),
        "sr_in": np.ascontiguousarray(sr.reshape(8, 2, 64, 64).transpose(1, 3, 0, 2).reshape(128, 8, 64)),
        "sh_in": _shift_layout(sh),
    }


def state_from_dev(sg, conv, sr, sh):
    return (np.ascontiguousarray(sg.transpose(1, 0, 2)),
            np.ascontiguousarray(conv.transpose(2, 1, 0).reshape(3, 3072)),
            np.ascontiguousarray(sr.reshape(2, 64, 8, 64).transpose(2, 0, 3, 1).reshape(16, 64, 64)),
            _shift_unlayout(sh))


_NC_CACHE = {}


def get_program(NT, n_pre, n_main, with_sample, debug=()):
    key = (NT, n_pre, n_main, with_sample, tuple(debug))
    if key not in _NC_CACHE:
        b = Builder(NT, n_pre, n_main, with_sample, debug)
        b.build()
        _NC_CACHE[key] = b
    return _NC_CACHE[key]


def kernel(**inputs):
    NT = 256
    f = np.float32
    x_prompt = np.asarray(inputs["x_prompt"], f)
    x_sample = np.asarray(inputs["x_sample"], f)
    p_prompt = np.asarray(inputs["p_prompt"], f)[0]
    p_sample = np.asarray(inputs["p_sample"], f)[0]
    B, T, _ = x_prompt.shape
    half = T // 2
    n_main = half // NT
    w = {k: np.asarray(v, f)[0] for k, v in inputs.items()
         if k not in ("x_prompt", "x_sample", "p_prompt", "p_sample", "state_gdn", "cache_gdn_conv",
                      "state_rwkv", "cache_rwkv_shift", "g_final")}
    w["g_final"] = np.asarray(inputs["g_final"], f)
    wd = prep_weights(w)
    bld = get_program(NT, n_main, n_main, True)
    in_maps = []
    zeros_pre = np.zeros((D, half), f)
    for c in range(8):
        b, hf = c // 2, c % 2
        m = dict(wd)
        xT = np.ascontiguousarray(x_prompt[b].T)
        m["xT_pre"] = np.ascontiguousarray(xT[:, 0:half]) if hf == 1 else zeros_pre
        m["xT_main"] = np.ascontiguousarray(xT[:, hf * half:(hf + 1) * half])
        m["pT_main"] = np.ascontiguousarray(p_prompt[b, hf * half:(hf + 1) * half].T)
        m["xT_s"] = np.ascontiguousarray(x_sample[c].T)
        m["pT_s"] = np.ascontiguousarray(p_sample[c].T)
        m.update(state_to_dev(np.asarray(inputs["state_gdn"], f)[0, c], np.asarray(inputs["cache_gdn_conv"], f)[0, c],
                              np.asarray(inputs["state_rwkv"], f)[0, c], np.asarray(inputs["cache_rwkv_shift"], f)[0, c]))
        in_maps.append(m)
    res = run_bass_kernel_spmd(bld.nc, in_maps, core_ids=list(range(8)))
    R = res.results
    y_prompt = np.empty((B, T, D), f)
    y_sample = np.empty((8, 32, D), f)
    sg_p = np.empty((1, B, 8, 128, 128), f)
    conv_p = np.empty((1, B, 3, 3072), f)
    sr_p = np.empty((1, B, 16, 64, 64), f)
    sh_p = np.empty((1, B, 3520), f)
    sg_s = np.empty((1, 8, 8, 128, 128), f)
    conv_s = np.empty((1, 8, 3, 3072), f)
    sr_s = np.empty((1, 8, 16, 64, 64), f)
    sh_s = np.empty((1, 8, 3520), f)
    for c in range(8):
        b, hf = c // 2, c % 2
        r = R[c]
        y_prompt[b, hf * half:(hf + 1) * half] = r["yT_main"].T
        y_sample[c] = r["yT_s"].T
        if hf == 1:
            sg_p[0, b], conv_p[0, b], sr_p[0, b], sh_p[0, b] = state_from_dev(r["sg_p"], r["conv_p"], r["sr_p"], r["sh_p"])
        sg_s[0, c], conv_s[0, c], sr_s[0, c], sh_s[0, c] = state_from_dev(r["sg_s"], r["conv_s"], r["sr_s"], r["sh_s"])
    return (y_prompt, y_sample, sg_p, conv_p, sr_p, sh_p, sg_s, conv_s, sr_s, sh_s)
```

```python
import contextlib
import numpy as np
import concourse.bass as bass
import concourse.mybir as mybir
from concourse.alu_op_type import AluOpType as ALU
from concourse.bass_utils import run_bass_kernel_spmd

F32 = mybir.dt.float32
BF16 = mybir.dt.bfloat16
AF = mybir.ActivationFunctionType

D = 2048
NKC = 16
H_A = 8
H_B = 16
NB_B = 8
CONV_CH = 3072
SHIFT_W = 3520
D_FF = 8192
EPS = 1e-6
LNX_EPS = 64e-5
SEM_LIM = 12000
DMA_LIM = 1500


class Buf:
    __slots__ = ("w", "r", "excl")

    def __init__(self, init=None, excl=False):
        self.w = dict(init) if init else {}
        self.r = {}
        self.excl = excl


class Op:
    __slots__ = ("eng", "stream", "fn", "waits", "idx", "signal", "is_dma", "vc", "batch")


ENGS = ("pe", "act", "dve", "pool", "sp")


class Sched:
    def __init__(self, nc):
        self.nc = nc
        self.ops = {e: [] for e in ENGS}
        self.stream_ops = {}
        self.stack = contextlib.ExitStack()
        self.n_t = 0
        self.clock = {e: {} for e in ENGS}
        self.batches = {}
        self.batch_closed = {}
        self.fence_next = False
        self.pe_kind = None

    def sbuf(self, shape, dtype, name=None):
        self.n_t += 1
        return self.stack.enter_context(self.nc.sbuf_tensor("sb_" + (name or f"t{self.n_t}"), list(shape), dtype))

    def psum(self, shape, dtype, name=None):
        self.n_t += 1
        return self.stack.enter_context(self.nc.psum_tensor(name or f"p{self.n_t}", list(shape), dtype))

    def barrier_state(self, exclude=()):
        return {s: len(sl) - 1 for s, sl in self.stream_ops.items() if sl and s not in exclude}

    @staticmethod
    def _flat(bs):
        out = []
        for b in bs:
            if isinstance(b, (list, tuple)):
                out.extend(Sched._flat(b))
            else:
                out.append(b)
        return out

    def op(self, eng, fn, reads=(), writes=(), dma=None, fence=False):
        reads = self._flat(reads)
        writes = self._flat(writes)
        o = Op()
        o.eng = eng
        o.is_dma = dma is not None
        o.stream = dma if o.is_dma else eng
        o.fn = fn
        o.signal = o.is_dma
        sl = self.stream_ops.setdefault(o.stream, [])
        o.idx = len(sl)
        need = {}
        same = o.stream
        isd = o.is_dma

        def req(s, i):
            if s in self.batches:
                bl = self.batches[s]
                bi = self.stream_ops[s][i].batch
                if bi == len(bl) - 1:
                    self.batch_closed[s] = True
                    i = len(self.stream_ops[s]) - 1
                else:
                    i = bl[bi + 1] - 1
            if need.get(s, -1) < i:
                need[s] = i

        if isd:
            bl = self.batches.setdefault(o.stream, [])
            if not bl or self.batch_closed.get(o.stream, False):
                if bl:
                    req(o.stream, o.idx - 1)
                bl.append(o.idx)
                self.batch_closed[o.stream] = False
            o.batch = len(bl) - 1
        if eng == "pe" and not isd:
            if fence and sl:
                req("pe", len(sl) - 1)
        for b in reads:
            for s, i in b.w.items():
                req(s, i)
            if b.excl:
                for s, i in b.r.items():
                    if s != same:
                        req(s, i)
        pe_self = (same == "pe")
        for b in writes:
            for s, i in b.w.items():
                if not (pe_self and s == same):
                    req(s, i)
            for s, i in b.r.items():
                req(s, i)
        clk = self.clock[eng]
        waits = []
        for s, i in need.items():
            if clk.get(s, -1) >= i:
                continue
            waits.append((s, i))
            p = self.stream_ops[s][i]
            p.signal = True
            for s2, i2 in p.vc.items():
                if clk.get(s2, -1) < i2:
                    clk[s2] = i2
            if clk.get(s, -1) < i:
                clk[s] = i
        o.waits = waits
        vc = dict(clk)
        vc[o.stream] = o.idx
        o.vc = vc
        sl.append(o)
        self.ops[eng].append(o)
        for b in reads:
            b.r[o.stream] = o.idx
        for b in writes:
            b.w = {o.stream: o.idx}
            b.r = {}
        return o

    def emit(self, final_streams=()):
        nc = self.nc
        sig = {}
        n_ep = {}
        for s, sl in self.stream_ops.items():
            if not sl:
                continue
            if sl[0].is_dma:
                bl = self.batches[s] + [len(sl)]
                ep, cnt = 0, 0
                for bi in range(len(bl) - 1):
                    size = bl[bi + 1] - bl[bi]
                    if cnt + size > DMA_LIM:
                        ep += 1
                        cnt = 0
                    for i in range(bl[bi], bl[bi + 1]):
                        cnt += 1
                        sig[(s, i)] = (ep, cnt * 16)
                n_ep[s] = ep + 1
            else:
                n = 0
                for o in sl:
                    if o.signal:
                        ep, v = divmod(n, SEM_LIM)
                        sig[(s, o.idx)] = (ep, v + 1)
                        n += 1
                n_ep[s] = (n + SEM_LIM - 1) // SEM_LIM
        sems = {}
        for s, k in n_ep.items():
            sems[s] = [self.stack.enter_context(nc.semaphore(f"s_{s}_{e}")) for e in range(k)]
        self.n_sems = sum(n_ep.values())
        block = self.stack.enter_context(nc.Block())

        def run_engine(eng_name):
            def body(e):
                for o in self.ops[eng_name]:
                    for (s, i) in o.waits:
                        ep, v = sig[(s, i)]
                        e.wait_ge(sems[s][ep], v)
                    ins = o.fn(e)
                    if o.signal:
                        ep, v = sig[(o.stream, o.idx)]
                        ins.then_inc(sems[o.stream][ep], 16 if o.is_dma else 1)
                for s in final_streams:
                    sl = self.stream_ops.get(s)
                    if sl and sl[0].eng == eng_name:
                        ep, v = sig[(s, len(sl) - 1)]
                        e.wait_ge(sems[s][ep], v)
            return body

        block.tensor(run_engine("pe"))
        block.scalar(run_engine("act"))
        block.vector(run_engine("dve"))
        block.gpsimd(run_engine("pool"))
        block.sync(run_engine("sp"))
        self.stack.close()


class TL:
    __slots__ = ("t", "b")

    def __init__(self, t, b=None):
        self.t = t
        self.b = b if b is not None else Buf()


class Builder:
    def __init__(self, NT, n_pre, n_main, with_sample=True, debug=()):
        self.NT = NT
        self.n_pre = n_pre
        self.n_main = n_main
        self.with_sample = with_sample
        self.debug = set(debug)
        self.nc = bass.Bass("TRN2", target_bir_lowering=False)
        self.S = Sched(self.nc)
        self.dbg_out = {}
        self.arena_gen = None

    def tt(self, out, in0, in1, op, R, W, eng="dve"):
        return self.S.op(eng, lambda e: e.tensor_tensor(out=out, in0=in0, in1=in1, op=op), reads=R, writes=W)

    def ts(self, out, in0, s1, s2, op0, op1, R, W, eng="dve"):
        if op1 is None:
            return self.S.op(eng, lambda e: e.tensor_scalar(out=out, in0=in0, scalar1=s1, scalar2=None, op0=op0),
                             reads=R, writes=W)
        return self.S.op(eng, lambda e: e.tensor_scalar(out=out, in0=in0, scalar1=s1, scalar2=s2, op0=op0, op1=op1),
                         reads=R, writes=W)

    def stt(self, out, in0, scalar, in1, op0, op1, R, W):
        return self.S.op("dve", lambda e: e.scalar_tensor_tensor(out=out, in0=in0, scalar=scalar, in1=in1,
                                                                op0=op0, op1=op1), reads=R, writes=W)

    def act(self, out, in_, func, R, W, bias=None, scale=None):
        kw = {}
        if bias is not None:
            kw["bias"] = bias
        if scale is not None:
            kw["scale"] = scale
        return self.S.op("act", lambda e: e.activation(out=out, in_=in_, func=func, **kw), reads=R, writes=W)

    def cp(self, eng, out, in_, R, W):
        if eng == "act":
            return self.S.op("act", lambda e: e.activation(out=out, in_=in_, func=AF.Copy), reads=R, writes=W)
        return self.S.op(eng, lambda e: e.tensor_copy(out=out, in_=in_), reads=R, writes=W)

    def mm(self, out, lhsT, rhs, start, stop, R, W, fence=False):
        kind = "bf" if lhsT.dtype == BF16 else "f32"
        if kind != self.S.pe_kind:
            fence = True
            self.S.pe_kind = kind
        return self.S.op("pe", lambda e: e.matmul(out, lhsT=lhsT, rhs=rhs, start=start, stop=stop), reads=R, writes=W,
                         fence=fence)

    def tr(self, out, in_, R, W):
        n = in_.shape[0]
        idn = self.ident.t[0:n, 0:n]
        fence = self.S.pe_kind != "f32"
        self.S.pe_kind = "f32"
        return self.S.op("pe", lambda e: e.transpose(out, in_, idn), reads=list(R) + [self.ident.b], writes=W,
                         fence=fence)

    def dma(self, eng, stream, out, in_, R, W, slow=False):
        if slow:
            return self.S.op(eng, lambda e: e.dma_start(out=out, in_=in_, allow_slow_non_contiguous=True),
                             reads=R, writes=W, dma=stream)
        return self.S.op(eng, lambda e: e.dma_start(out=out, in_=in_), reads=R, writes=W, dma=stream)

    def memset(self, ap, val, W, eng="pool"):
        return self.S.op(eng, lambda e: e.memset(ap, val), writes=W)

    def rsqrt(self, out, in_, scale, eps, R, W):
        self.act(out, in_, AF.Ln, R, W, bias=float(eps), scale=float(scale))
        self.act(out, out, AF.Exp, W, W, scale=-0.5)

    def init_psum(self):
        self.pdb = [self.S.psum([128, 1024], F32, name=f"psdb{i}") for i in range(4)]
        self.pbuf = [Buf(excl=True) for _ in range(8)]
        self.prr = 0
        self.psg = None
        self.prg = [0, 0]

    def ps1(self):
        g = self.psg
        if g is None:
            i = self.prr % 8
            self.prr += 1
        else:
            i = 4 * g + self.prg[g] % 4
            self.prg[g] += 1
        return TL(self.pdb[i // 2][:, (i % 2) * 512:(i % 2) * 512 + 512], self.pbuf[i])

    def ps2(self):
        g = self.psg
        if g is None:
            if self.prr % 2:
                self.prr += 1
            i = self.prr % 8
            self.prr += 2
        else:
            if self.prg[g] % 2:
                self.prg[g] += 1
            i = 4 * g + self.prg[g] % 4
            self.prg[g] += 2
        return self.pdb[i // 2], [self.pbuf[i], self.pbuf[i + 1]]

    def init_slabs(self, nbuf=3):
        self.slabs = [TL(self.S.sbuf([128, 16, 512], BF16, name=f"slab{i}")) for i in range(nbuf)]
        self.slab_i = 0

    def load_slab(self, pieces):
        bi = self.slab_i % len(self.slabs)
        sl = self.slabs[bi]
        self.slab_i += 1
        for (src, kc0, c0) in pieces:
            rows, cols = src.shape
            kc = rows // 128
            if rows % 128 == 0:
                self.dma("pool", f"slab{bi}", sl.t[:, kc0:kc0 + kc, c0:c0 + cols],
                         src.rearrange("(k p) c -> p k c", p=128), [], [sl.b])
            else:
                assert rows < 128
                self.dma("pool", f"slab{bi}", sl.t[0:rows, kc0, c0:c0 + cols], src, [], [sl.b])
        return sl

    def declare_io(self):
        nc = self.nc
        NT, n_pre, n_main = self.NT, self.n_pre, self.n_main
        I = {}

        def inp(name, shape):
            I[name] = nc.dram_tensor(name, list(shape), F32, kind="ExternalInput").ap()

        def outp(name, shape):
            I[name] = nc.dram_tensor(name, list(shape), F32, kind="ExternalOutput").ap()

        if n_pre:
            inp("xT_pre", [D, n_pre * NT])
        inp("xT_main", [D, n_main * NT])
        inp("pT_main", [256, n_main * NT])
        outp("yT_main", [D, n_main * NT])
        for g in (["p", "s"] if self.with_sample else ["p"]):
            outp(f"sg_{g}", [128, 8, 128])
            outp(f"conv_{g}", [128, 24, 3])
            outp(f"sr_{g}", [128, 8, 64])
            outp(f"sh_{g}", [128, 28])
        if self.with_sample:
            inp("xT_s", [D, 32])
            inp("pT_s", [256, 32])
            outp("yT_s", [D, 32])
            inp("sg_in", [128, 8, 128])
            inp("conv_in", [128, 24, 3])
            inp("sr_in", [128, 8, 64])
            inp("sh_in", [128, 28])
        inp("w_gdn", [D, 8, 512])
        inp("w_ab", [D, 16])
        inp("w_rw", [D, 8, 384])
        inp("w_lora", [D, 448])
        inp("w_gm", [D, 2, 2048])
        inp("w_up_a", [1024, D])
        inp("w_up_b", [1024, D])
        inp("w_o", [D, D])
        inp("w_ff1", [D, D_FF])
        inp("w_ff2", [D_FF, D])
        inp("w_ple_gate", [D, D])
        inp("w_ple", [256, D])
        inp("w2", [96, 1024])
        inp("a2", [96, 1024])
        inp("g2", [256, 1024])
        inp("gvec", [128, 4, 16])
        inp("convw", [128, 24, 4])
        inp("mu", [128, 28])
        inp("rwp", [128, 7, 8])
        inp("gng", [128, 1])
        inp("adt", [8, 2])
        self.I = I
        for name in self.debug:
            pass

    def setup_consts(self):
        S, I = self.S, self.I
        self.ident = TL(S.sbuf([128, 128], F32, name="ident"))
        self.memset(self.ident.t[:], 0.0, [self.ident.b])
        S.op("pool", lambda e: e.affine_select(out=self.ident.t[:], in_=self.ident.t[:], pattern=[[-1, 128]],
                                               compare_op=ALU.not_equal, fill=1.0, base=0, channel_multiplier=1),
             reads=[self.ident.b], writes=[self.ident.b])
        self.ones = TL(S.sbuf([128, 128], F32, name="ones"))
        self.memset(self.ones.t[:], 1.0, [self.ones.b])
        self.onesbd = TL(S.sbuf([128, 128], F32, name="onesbd"))
        self.memset(self.onesbd.t[:], 0.0, [self.onesbd.b])
        self.memset(self.onesbd.t[0:64, 0:64], 1.0, [self.onesbd.b])
        self.memset(self.onesbd.t[64:128, 64:128], 1.0, [self.onesbd.b])
        self.ones_bf = TL(S.sbuf([128, 128], BF16, name="ones_bf"))
        self.cp("pool", self.ones_bf.t[:], self.ones.t[:], [self.ones.b], [self.ones_bf.b])
        self.onesbd_bf = TL(S.sbuf([128, 128], BF16, name="onesbd_bf"))
        self.cp("pool", self.onesbd_bf.t[:], self.onesbd.t[:], [self.onesbd.b], [self.onesbd_bf.b])
        self.sel8 = TL(S.sbuf([8, 8, 128], F32, name="sel8"))
        self.memset(self.sel8.t[:], 0.0, [self.sel8.b])
        S.op("pool", lambda e: e.affine_select(out=self.sel8.t[:], in_=self.sel8.t[:], pattern=[[-1, 8], [0, 128]],
                                               compare_op=ALU.not_equal, fill=1.0, base=0, channel_multiplier=1),
             reads=[self.sel8.b], writes=[self.sel8.b])
        self.masks = TL(S.sbuf([64, 4, 64], F32, name="masks"))
        mk = self.masks
        self.memset(mk.t[:], 1.0, [mk.b])
        S.op("pool", lambda e: e.affine_select(out=mk.t[:, 0, :], in_=mk.t[:, 0, :], pattern=[[1, 64]],
                                               compare_op=ALU.is_ge, fill=0.0, base=0, channel_multiplier=-1),
             reads=[mk.b], writes=[mk.b])
        S.op("pool", lambda e: e.affine_select(out=mk.t[:, 1, :], in_=mk.t[:, 1, :], pattern=[[1, 64]],
                                               compare_op=ALU.is_gt, fill=0.0, base=0, channel_multiplier=-1),
             reads=[mk.b], writes=[mk.b])
        S.op("pool", lambda e: e.affine_select(out=mk.t[:, 2, :], in_=mk.t[:, 2, :], pattern=[[-1, 64]],
                                               compare_op=ALU.is_gt, fill=0.0, base=0, channel_multiplier=1),
             reads=[mk.b], writes=[mk.b])
        S.op("pool", lambda e: e.tensor_scalar(out=mk.t[:, 3, :], in0=mk.t[:, 1, :], scalar1=-1.0, scalar2=None,
                                               op0=ALU.mult), reads=[mk.b], writes=[mk.b])
        def ld(name, shape):
            t = TL(S.sbuf(shape, F32, name="c_" + name))
            self.dma("sp", "par", t.t[:], I[name], [], [t.b])
            return t
        self.gvec = ld("gvec", [128, 4, 16])
        self.convw = ld("convw", [128, 24, 4])
        self.mu = ld("mu", [128, 28])
        self.rwp = ld("rwp", [128, 7, 8])
        self.gng = ld("gng", [128, 1])
        self.adt = ld("adt", [8, 2])
        self.omu = TL(S.sbuf([128, 28], F32, name="omu"))
        self.ts(self.omu.t[:], self.mu.t[:], -1.0, 1.0, ALU.mult, ALU.add, [self.mu.b], [self.omu.b])
        self.omka = TL(S.sbuf([128, 8], F32, name="omka"))
        self.ts(self.omka.t[:], self.rwp.t[:, 3, :], -1.0, 1.0, ALU.mult, ALU.add, [self.rwp.b], [self.omka.b])
        self.nA = TL(S.sbuf([8, 1], F32, name="nA"))
        self.act(self.nA.t[:], self.adt.t[:, 0:1], AF.Exp, [self.adt.b], [self.nA.b])
        self.ts(self.nA.t[:], self.nA.t[:], -1.0, None, ALU.mult, None, [self.nA.b], [self.nA.b])
        self.w2 = TL(S.sbuf([96, 1024], BF16, name="w2"))
        self.a2 = TL(S.sbuf([96, 1024], BF16, name="a2"))
        self.g2 = TL(S.sbuf([128, 2, 1024], BF16, name="g2"))
        self.dma("pool", "lora", self.w2.t[:], I["w2"], [], [self.w2.b])
        self.dma("pool", "lora", self.a2.t[:], I["a2"], [], [self.a2.b])
        self.dma("pool", "lora", self.g2.t[:], I["g2"].rearrange("(k p) c -> p k c", p=128), [], [self.g2.b])

    def alloc_state(self):
        S = self.S
        self.Sg = [TL(S.sbuf([128, 128], F32, name=f"Sg{h}")) for h in range(8)]
        self.Pr = [TL(S.sbuf([128, 128], F32, name=f"Pr{i}")) for i in range(8)]
        self.halo_c = TL(S.sbuf([128, 24, 3], F32, name="halo_c"))
        self.halo_x = TL(S.sbuf([128, 28], F32, name="halo_x"))

    def zero_state(self):
        for t in self.Sg + self.Pr:
            self.memset(t.t[:], 0.0, [t.b])
        self.memset(self.halo_c.t[:], 0.0, [self.halo_c.b])
        self.memset(self.halo_x.t[:], 0.0, [self.halo_x.b])

    def load_state(self):
        I = self.I
        for h in range(8):
            self.dma("sp", "stin", self.Sg[h].t[:], I["sg_in"][:, h, :], [], [self.Sg[h].b])
            self.memset(self.Pr[h].t[:], 0.0, [self.Pr[h].b])
            self.dma("sp", "stin", self.Pr[h].t[0:64, 0:64], I["sr_in"][0:64, h, :], [], [self.Pr[h].b])
            self.dma("sp", "stin", self.Pr[h].t[64:128, 64:128], I["sr_in"][64:128, h, :], [], [self.Pr[h].b])
        self.dma("sp", "stin", self.halo_c.t[:], I["conv_in"], [], [self.halo_c.b])
        self.dma("sp", "stin", self.halo_x.t[:], I["sh_in"], [], [self.halo_x.b])

    def store_state(self, g):
        I = self.I
        for h in range(8):
            self.dma("sp", "outq", I[f"sg_{g}"][:, h, :], self.Sg[h].t[:], [self.Sg[h].b], [])
            self.dma("sp", "outq", I[f"sr_{g}"][0:64, h, :], self.Pr[h].t[0:64, 0:64], [self.Pr[h].b], [])
            self.dma("sp", "outq", I[f"sr_{g}"][64:128, h, :], self.Pr[h].t[64:128, 64:128], [self.Pr[h].b], [])
        self.dma("sp", "outq", I[f"conv_{g}"], self.halo_c.t[:], [self.halo_c.b], [])
        self.dma("sp", "outq", I[f"sh_{g}"], self.halo_x.t[:], [self.halo_x.b], [])


    def alloc_work(self):
        S, NT = self.S, self.NT
        self.SW = NT + 4
        self.xt = S.sbuf([128, NKC, NT], F32, name="xt")
        self.xb = [Buf() for _ in range(NKC)]
        self.ht = S.sbuf([128, NKC, NT], BF16, name="ht")
        self.hb = [Buf() for _ in range(NKC)]
        self.oat = S.sbuf([128, 8, NT], BF16, name="oat")
        self.oab = [Buf() for _ in range(8)]
        self.obt = S.sbuf([128, 8, NT], BF16, name="obt")
        self.obb = [Buf() for _ in range(8)]
        self.mgt = S.sbuf([128, NKC, NT], BF16, name="mgt")
        self.mgb = [Buf() for _ in range(NKC)]
        self.pT = TL(S.sbuf([128, 2, NT], BF16, name="pT"))
        self.rs_t = TL(S.sbuf([128, NT], F32, name="rs_t"))
        import os
        self.NA = 78 if os.environ.get("KNOALIAS") else 71
        self.NSLOT = 78
        self.arena = S.sbuf([128, self.NA, self.SW], F32, name="arena")
        self.slot_bufs = [Buf() for _ in range(self.NA)]
        self.mg32 = self.mgt[:].rearrange("p k t -> p (k t)").bitcast(F32)
        assert (self.NSLOT - self.NA) * self.SW <= NKC * NT // 2
        order = list(range(0, 28)) + list(range(32, 48)) + [48, 49, 50, 51]
        self.RMAP = {o: 30 + r for r, o in enumerate(order)}
        self.RMAP[29], self.RMAP[30], self.RMAP[31] = self.RMAP[1], self.RMAP[2], self.RMAP[3]
        self.rmask = {}
        for C in (64, 32):
            t = TL(S.sbuf([128, NT], F32, name=f"rmask{C}"))
            self.memset(t.t[:], 1.0, [t.b])
            self.memset(t.t[:].rearrange("p (c j) -> p c j", j=C)[:, :, 0:1], 0.0, [t.b])
            self.rmask[C] = t
        self.g8 = TL(S.sbuf([8, NT], F32, name="g8"))
        self.G8 = TL(S.sbuf([8, NT], F32, name="G8"))
        self.b8 = TL(S.sbuf([8, NT], F32, name="b8"))
        mc = NT // 32 * 8
        self.Gcol = TL(S.sbuf([64, mc], F32, name="Gcol"))
        self.bcol = TL(S.sbuf([64, mc], F32, name="bcol"))
        self.nbcol = TL(S.sbuf([64, mc], F32, name="nbcol"))
        self.nbeg = TL(S.sbuf([64, mc], F32, name="nbeg"))
        self.kdc = [TL(S.sbuf([64, NT // 32], F32, name=f"kdc{i}")) for i in range(2)]
        self.rv = [TL(S.sbuf([64, 128], BF16, name=f"rv{i}")) for i in range(2)]
        self.knqb = TL(S.sbuf([128, 2, NT], BF16, name="knqb"))
        self.arb = TL(S.sbuf([128, 2 * NT], BF16, name="arb"))
        self.btb = TL(S.sbuf([128, NT], BF16, name="btb"))
        self.ktb = TL(S.sbuf([128, NT], BF16, name="ktb"))
        self.vn = [TL(S.sbuf([64, 128], BF16, name=f"vn{i}")) for i in range(2)]
        self.Sgb = [TL(S.sbuf([128, 128], BF16, name=f"Sgb{i}")) for i in range(2)]
        self.Prb = [TL(S.sbuf([128, 128], BF16, name=f"Prb{i}")) for i in range(2)]
        self.txw = TL(S.sbuf([96, NT], BF16, name="txw"))
        self.xab = TL(S.sbuf([96, NT], BF16, name="xab"))
        self.sxg = TL(S.sbuf([128, 2, NT], BF16, name="sxg"))
        self.WC = TL(S.sbuf([128, NT // 32], F32, name="WC"))
        self.Xs = [TL(S.sbuf([64, 128], BF16, name=f"Xs{i}")) for i in range(2)]
        self.Us = [TL(S.sbuf([64, 128], BF16, name=f"Us{i}")) for i in range(2)]
        self.Ys = [TL(S.sbuf([64, 128], F32, name=f"Ys{i}")) for i in range(2)]

    def X(self, k):
        return TL(self.xt[:, k, :], self.xb[k])

    def Hh(self, k):
        return TL(self.ht[:, k, :], self.hb[k])

    def slot(self, i, nslots=1):
        if i >= self.NA:
            assert nslots == 1
            j = i - self.NA
            c0, c1 = j * self.SW, (j + 1) * self.SW
            per = self.NT // 2
            return TL(self.mg32[:, c0:c1], [self.mgb[k] for k in range(c0 // per, (c1 - 1) // per + 1)])
        assert i + nslots <= self.NA
        if nslots == 1:
            return TL(self.arena[:, i, :], self.slot_bufs[i])
        return TL(self.arena[:, i:i + nslots, :].rearrange("p s w -> p (s w)"), self.slot_bufs[i:i + nslots])

    @staticmethod
    def B(*tls):
        out = []
        for t in tls:
            b = t.b if isinstance(t, TL) else t
            if isinstance(b, (list, tuple)):
                out.extend(b)
            else:
                out.append(b)
        return out

    def rmsnorm_x(self, n, which, out_fn=None):
        ps = self.ps1()
        sq = self.slot(0)
        for k in range(NKC):
            xk = self.X(k)
            sqb = sq.t.bitcast(BF16)
            self.act(sqb[:, 0:n], xk.t[:, 0:n], AF.Square, [xk.b], [sq.b])
            self.mm(ps.t[:, 0:n], self.ones_bf.t[:], sqb[:, 0:n], k == 0, k == NKC - 1, [self.ones_bf.b, sq.b], [ps.b])
        self.rsqrt(self.rs_t.t[:, 0:n], ps.t[:, 0:n], 1.0 / D, EPS, [ps.b], [self.rs_t.b])
        for k in range(NKC):
            xk = self.X(k)
            o = out_fn(k) if out_fn else self.Hh(k)
            self.stt(o.t[:, 0:n], xk.t[:, 0:n], self.gvec.t[:, which, k:k + 1], self.rs_t.t[:, 0:n],
                     ALU.mult, ALU.mult, [xk.b, self.gvec.b, self.rs_t.b], self.B(o))

    def dense(self, ps, M, n, slab, col0, rhs_tiles, kc0=0, kp=128):
        nk = len(rhs_tiles)
        for j, r in enumerate(rhs_tiles):
            self.mm(ps.t[0:M, 0:n], slab.t[0:kp, kc0 + j, col0:col0 + M], r.t[0:kp, 0:n], j == 0, j == nk - 1,
                    self.B(slab, r), self.B(ps))

    def neumann(self, P, L, nch, C, pslots, lslots, rslots):
        return self.neumann_multi([(P, L, pslots, lslots, rslots)], nch, C)[0]

    def neumann_multi(self, pairs, nch, C, sl=None):
        sl = sl or self.slot

        def bt_(t):
            return t.t.bitcast(BF16)

        def v3(t):
            return bt_(t)[0:C, 0:nch * C].rearrange("p (c j) -> p c j", j=C)
        idb = self.ident.t[0:C, 0:C].unsqueeze(1).to_broadcast([C, nch, C])
        st = []
        for (P, L, pslots, lslots, rslots) in pairs:
            R = sl(rslots[0])
            self.tt(v3(R), v3(P), idb, ALU.add, self.B(P, self.ident), self.B(R))
            st.append([P, L, R, pslots, lslots, rslots])
        nlev = {64: 5, 32: 4}[C]
        for k in range(1, nlev + 1):
            tmp = []
            for (P, L, R, pslots, lslots, rslots) in st:
                Pb, Lb = bt_(P), bt_(L)
                psL = self.ps1()
                for c in range(nch):
                    cs = slice(c * C, (c + 1) * C)
                    self.mm(psL.t[0:C, cs], Pb[0:C, cs], Lb[0:C, cs], True, True, self.B(P, L), self.B(psL))
                psP = None
                if k < nlev:
                    psP = self.ps1()
                    for c in range(nch):
                        cs = slice(c * C, (c + 1) * C)
                        self.mm(psP.t[0:C, cs], Lb[0:C, cs], Pb[0:C, cs], True, True, self.B(P, L), self.B(psP))
                tmp.append((psL, psP))
            tmp2 = []
            for e, (psL, psP) in zip(st, tmp):
                P, L, R, pslots, lslots, rslots = e
                Lk = sl(lslots[k % 2])
                self.cp("act", bt_(Lk)[0:C, 0:nch * C], psL.t[0:C, 0:nch * C], self.B(psL), self.B(Lk))
                Pk = P
                if k < nlev:
                    Pk = sl(pslots[k % 2])
                    self.cp("dve", bt_(Pk)[0:C, 0:nch * C], psP.t[0:C, 0:nch * C], self.B(psP), self.B(Pk))
                psR = self.ps1()
                for c in range(nch):
                    cs = slice(c * C, (c + 1) * C)
                    self.mm(psR.t[0:C, cs], bt_(Lk)[0:C, cs], bt_(R)[0:C, cs], True, True, self.B(Lk, R), self.B(psR))
                tmp2.append((Lk, Pk, psR))
            for e, (Lk, Pk, psR) in zip(st, tmp2):
                R, rslots = e[2], e[5]
                Rn = sl(rslots[k % 2])
                self.tt(bt_(Rn)[0:C, 0:nch * C], bt_(R)[0:C, 0:nch * C], psR.t[0:C, 0:nch * C], ALU.add,
                        self.B(R, psR), self.B(Rn))
                e[0], e[1], e[2] = Pk, Lk, Rn
        return [e[2] for e in st]

    def gdn_gen(self, n, C, full, all_xb=False):
        I = self.I
        nch = n // C
        Hk = [self.Hh(k) for k in range(NKC)]
        slab = self.load_slab([(I["w_ab"], 0, 0)])
        psA = self.ps1()
        self.dense(psA, 8, n, slab, 0, Hk)
        psB = self.ps1()
        self.dense(psB, 8, n, slab, 8, Hk)
        g8, G8, b8 = self.g8, self.G8, self.b8
        self.act(g8.t[:, 0:n], psA.t[0:8, 0:n], AF.Exp, self.B(psA, self.adt), [g8.b], bias=self.adt.t[:, 1:2])
        self.act(g8.t[:, 0:n], g8.t[:, 0:n], AF.Ln, [g8.b], [g8.b], bias=1.0)
        self.ts(g8.t[:, 0:n], g8.t[:, 0:n], self.nA.t[:, 0:1], None, ALU.mult, None, self.B(g8, self.nA), [g8.b])
        rm = self.rmask[C]
        self.S.op("dve", lambda e: e.tensor_tensor_scan(out=G8.t[:, 0:n], data0=rm.t[0:8, 0:n], data1=g8.t[:, 0:n],
                                                       initial=0.0, op0=ALU.mult, op1=ALU.add),
                  reads=self.B(g8, rm), writes=[G8.b])
        self.act(b8.t[:, 0:n], psB.t[0:8, 0:n], AF.Sigmoid, self.B(psB), [b8.b])
        Gcol, bcol, nbcol, nbeg = self.Gcol, self.bcol, self.nbcol, self.nbeg
        m8 = nch * 8
        for (src, dst) in ((G8, Gcol), (b8, bcol)):
            psT = self.ps1()
            for c in range(nch):
                self.tr(psT.t[0:C, c * 8:(c + 1) * 8], src.t[0:8, c * C:(c + 1) * C], self.B(src), self.B(psT))
            self.cp("dve", dst.t[0:C, 0:m8], psT.t[0:C, 0:m8], self.B(psT), [dst.b])
        self.ts(nbcol.t[0:C, 0:m8], bcol.t[0:C, 0:m8], -1.0, None, ALU.mult, None, [bcol.b], [nbcol.b])
        self.act(nbeg.t[0:C, 0:m8], Gcol.t[0:C, 0:m8], AF.Exp, [Gcol.b], [nbeg.b])
        self.tt(nbeg.t[0:C, 0:m8], nbeg.t[0:C, 0:m8], nbcol.t[0:C, 0:m8], ALU.mult, [nbeg.b, nbcol.b], [nbeg.b])
        yield
        Gcol3 = Gcol.t[0:C, 0:m8].rearrange("p (c h) -> p c h", h=8)
        bcol3 = bcol.t[0:C, 0:m8].rearrange("p (c h) -> p c h", h=8)
        nbcol3 = nbcol.t[0:C, 0:m8].rearrange("p (c h) -> p c h", h=8)
        nbeg3 = nbeg.t[0:C, 0:m8].rearrange("p (c h) -> p c h", h=8)
        mk = self.masks

        def mask(i):
            return mk.t[0:C, i:i + 1, 0:C].to_broadcast([C, nch, C])

        def v3(t):
            return t.t[0:C, 0:nch * C].rearrange("p (c j) -> p c j", j=C)

        def v3b(t):
            return t.t.bitcast(BF16)[0:C, 0:nch * C].rearrange("p (c j) -> p c j", j=C)

        for h in range(8):
            slab = self.load_slab([(I["w_gdn"][:, h, :], 0, 0)])
            kinds = (0, 1, 2) if (full or all_xb) else (1, 2)
            co = {}
            pss = {}
            for j in kinds + ((3,) if full else ()):
                pss[j] = self.ps1()
                self.dense(pss[j], 128, n, slab, j * 128, Hk)
            yield
            for j in kinds:
                blk = j * 8 + h
                ps = pss[j]
                raw = self.slot(1 + j)
                self.cp("dve", raw.t[:, 0:3], self.halo_c.t[:, blk, :], [self.halo_c.b], [raw.b])
                self.cp("act", raw.t[:, 3:3 + n], ps.t[:, 0:n], self.B(ps), [raw.b])
                self.cp("dve", self.halo_c.t[:, blk, :], raw.t[:, n:n + 3], [raw.b], [self.halo_c.b])
                if j == 0 and not full:
                    continue
                o = self.slot(4 + j)
                cw = self.convw
                self.ts(o.t[:, 0:n], raw.t[:, 0:n], cw.t[:, blk, 0:1], None, ALU.mult, None, self.B(raw, cw), [o.b])
                for tap in range(1, 4):
                    self.stt(o.t[:, 0:n], raw.t[:, tap:tap + n], cw.t[:, blk, tap:tap + 1], o.t[:, 0:n],
                             ALU.mult, ALU.add, self.B(raw, cw, o), [o.b])
                self.act(o.t[:, 0:n], o.t[:, 0:n], AF.Silu, [o.b], [o.b])
                co[j] = o
                yield
            if full:
                ps = pss[3]
                sgate = self.slot(7)
                self.act(sgate.t[:, 0:n], ps.t[:, 0:n], AF.Silu, self.B(ps), [sgate.b])
            for j in kinds:
                if j == 2 or j not in co:
                    continue
                o = co[j]
                sq = self.slot(0)
                sqb = sq.t.bitcast(BF16)
                self.act(sqb[:, 0:n], o.t[:, 0:n], AF.Square, [o.b], [sq.b])
                psn = self.ps1()
                self.mm(psn.t[:, 0:n], self.ones_bf.t[:], sqb[:, 0:n], True, True, self.B(self.ones_bf, sq), self.B(psn))
                rs = self.slot(29)
                self.rsqrt(rs.t[:, 0:n], psn.t[:, 0:n], 1.0, EPS, self.B(psn), [rs.b])
                if j == 0:
                    self.stt(o.t[:, 0:n], o.t[:, 0:n], float(128 ** -0.5), rs.t[:, 0:n], ALU.mult, ALU.mult,
                             self.B(o, rs), [o.b])
                else:
                    self.tt(o.t[:, 0:n], o.t[:, 0:n], rs.t[:, 0:n], ALU.mult, self.B(o, rs), [o.b])
            kn, vv = co[1], co[2]
            yield
            qn = co.get(0)
            knqb = self.knqb
            self.cp("act", knqb.t[:, 0, 0:n], kn.t[:, 0:n], [kn.b], [knqb.b])
            if full:
                self.cp("act", knqb.t[:, 1, 0:n], qn.t[:, 0:n], [qn.b], [knqb.b])
            psG = self.ps1()
            self.mm(psG.t[:, 0:n], self.sel8.t[:, h, :], G8.t[:, 0:n], True, True, self.B(self.sel8, G8), self.B(psG))
            Grow = self.slot(8)
            self.cp("dve", Grow.t[:, 0:n], psG.t[:, 0:n], self.B(psG), [Grow.b])
            eG = self.slot(10)
            self.act(eG.t[:, 0:n], psG.t[:, 0:n], AF.Exp, self.B(psG), [eG.b])
            psBt = self.ps1()
            self.mm(psBt.t[:, 0:n], self.sel8.t[:, h, :], b8.t[:, 0:n], True, True, self.B(self.sel8, b8), self.B(psBt))
            brow = self.slot(9)
            self.cp("act", brow.t[:, 0:n], psBt.t[:, 0:n], self.B(psBt), [brow.b])
            if full:
                qdec = self.slot(11)
                qdb = qdec.t.bitcast(BF16)
                self.tt(qdb[:, 0:n], qn.t[:, 0:n], eG.t[:, 0:n], ALU.mult, self.B(qn, eG), [qdec.b])
            yield
            psKK = self.ps1()
            for c in range(nch):
                cs = slice(c * C, (c + 1) * C)
                self.mm(psKK.t[0:C, cs], knqb.t[:, 0, cs], knqb.t[:, 0, cs], True, True, [knqb.b], self.B(psKK))
            if full:
                psQK = self.ps1()
                for c in range(nch):
                    cs = slice(c * C, (c + 1) * C)
                    self.mm(psQK.t[0:C, cs], knqb.t[:, 0, cs], knqb.t[:, 1, cs], True, True, [knqb.b], self.B(psQK))
            Gc_h = Gcol3[:, :, h:h + 1].to_broadcast([C, nch, C])
            t1 = self.slot(12)
            self.tt(v3(t1), v3(Grow), Gc_h, ALU.subtract, self.B(Grow, Gcol), [t1.b])
            tU = self.slot(13)
            self.tt(v3(tU), v3(t1), mask(0), ALU.mult, self.B(t1, mk), [tU.b])
            self.act(tU.t[0:C, 0:nch * C], tU.t[0:C, 0:nch * C], AF.Exp, [tU.b], [tU.b])
            self.tt(v3(tU), v3(tU), mask(0), ALU.mult, self.B(tU, mk), [tU.b])
            if full:
                QKT = self.slot(14)
                self.tt(v3b(QKT), v3(psQK), v3(tU), ALU.mult, self.B(psQK, tU), [QKT.b])
            mbs = self.slot(15)
            self.tt(v3(mbs), v3(brow), mask(3), ALU.mult, self.B(brow, mk), [mbs.b])
            self.tt(v3(mbs), v3(mbs), v3(tU), ALU.mult, self.B(mbs, tU), [mbs.b])
            P0 = self.slot(16)
            self.tt(v3b(P0), v3(psKK), v3(mbs), ALU.mult, self.B(psKK, mbs), [P0.b])
            yield
            tL = self.slot(18)
            self.tt(v3(tL), v3(t1), mask(2), ALU.mult, self.B(t1, mk), [tL.b])
            self.act(tL.t[0:C, 0:nch * C], tL.t[0:C, 0:nch * C], AF.Exp, [tL.b], [tL.b], scale=-1.0)
            mbL = self.slot(19)
            self.tt(v3(mbL), mask(2), nbcol3[:, :, h:h + 1].to_broadcast([C, nch, C]), ALU.mult,
                    self.B(mk, nbcol), [mbL.b])
            self.tt(v3(mbL), v3(mbL), v3(tL), ALU.mult, self.B(mbL, tL), [mbL.b])
            L0 = self.slot(17)
            self.tt(v3b(L0), v3(psKK), v3(mbL), ALU.mult, self.B(psKK, mbL), [L0.b])
            yield
            R = self.neumann(P0, L0, nch, C, (16, 20), (17, 21), (22, 23))
            yield
            R3 = v3b(R)
            kdc = self.kdc[h % 2]
            glast = Grow.t[0:C, 0:n].rearrange("p (c j) -> p c j", j=C)[:, :, C - 1:C]
            self.tt(kdc.t[0:C, 0:nch].unsqueeze(2), glast, Gcol3[:, :, h:h + 1], ALU.subtract,
                    self.B(Grow, Gcol), [kdc.b])
            self.act(kdc.t[0:C, 0:nch], kdc.t[0:C, 0:nch], AF.Exp, [kdc.b], [kdc.b])
            kdec = self.slot(24, 2)
            bv = self.slot(26, 2)
            for (src, dst, colv) in ((kn, kdec, kdc.t[0:C, 0:nch].unsqueeze(2).to_broadcast([C, nch, 128])),
                                     (vv, bv, bcol3[:, :, h:h + 1].to_broadcast([C, nch, 128]))):
                if nch * 128 <= 512:
                    pt = self.ps1()
                    ptt, ptb = pt.t, self.B(pt)
                else:
                    ptt, ptb = self.ps2()
                for c in range(nch):
                    self.tr(ptt[0:C, c * 128:(c + 1) * 128], src.t[:, c * C:(c + 1) * C], self.B(src), ptb)
                dview = dst.t.bitcast(BF16) if dst is kdec else dst.t
                self.tt(dview[0:C, 0:nch * 128].rearrange("p (c d) -> p c d", d=128),
                        ptt[0:C, 0:nch * 128].rearrange("p (c d) -> p c d", d=128), colv, ALU.mult,
                        ptb + self.B(kdc, bcol), self.B(dst))
            kdec3 = kdec.t.bitcast(BF16)[0:C, 0:nch * 128].rearrange("p (c d) -> p c d", d=128)
            bv3 = bv.t[0:C, 0:nch * 128].rearrange("p (c d) -> p c d", d=128)
            if full:
                QKT3 = v3b(QKT)
                o_h = self.slot(28)
            Sg = self.Sg[h]
            Sb = self.Sgb[h % 2]
            self.cp("act", Sb.t[:, :], Sg.t[:, :], [Sg.b], [Sb.b])
            for c in range(nch):
                cs = slice(c * C, (c + 1) * C)
                psk = self.ps1()
                self.mm(psk.t[0:C, 0:128], knqb.t[:, 0, cs], Sb.t[:, :], True, True, self.B(knqb, Sb), self.B(psk))
                rv = self.rv[c % 2]
                self.stt(rv.t[0:C, :], psk.t[0:C, 0:128], nbeg3[:, c, h:h + 1], bv3[:, c, :], ALU.mult, ALU.add,
                         self.B(psk, nbeg, bv), [rv.b])
                psv = self.ps1()
                self.mm(psv.t[0:C, 0:128], R3[:, c, :], rv.t[0:C, :], True, True, self.B(R, rv), self.B(psv))
                yield
                vn = self.vn[c % 2]
                self.cp("act", vn.t[0:C, :], psv.t[0:C, 0:128], self.B(psv), [vn.b])
                yield
                if full:
                    pso = self.ps1()
                    self.mm(pso.t[:, 0:C], Sb.t[:, :], qdb[:, cs], True, False, self.B(Sb, qdec), self.B(pso))
                    self.mm(pso.t[:, 0:C], vn.t[0:C, :], QKT3[:, c, :], False, True, self.B(vn, QKT), self.B(pso))
                    self.cp("act", o_h.t[:, cs], pso.t[:, 0:C], self.B(pso), [o_h.b])
                pss = self.ps1()
                self.mm(pss.t[:, 0:128], kdec3[:, c, :], vn.t[0:C, :], True, True, self.B(kdec, vn), self.B(pss))
                self.stt(Sg.t[:, :], Sg.t[:, :], eG.t[:, c * C + C - 1:c * C + C], pss.t[:, 0:128], ALU.mult, ALU.add,
                         self.B(Sg, eG, pss), [Sg.b])
                if c < nch - 1:
                    self.cp("act", Sb.t[:, :], Sg.t[:, :], [Sg.b], [Sb.b])
                yield
            if full:
                sq = self.slot(0)
                sqb = sq.t.bitcast(BF16)
                self.act(sqb[:, 0:n], o_h.t[:, 0:n], AF.Square, [o_h.b], [sq.b])
                psn = self.ps1()
                self.mm(psn.t[:, 0:n], self.ones_bf.t[:], sqb[:, 0:n], True, True, self.B(self.ones_bf, sq), self.B(psn))
                rs = self.slot(29)
                self.rsqrt(rs.t[:, 0:n], psn.t[:, 0:n], 1.0 / 128, EPS, self.B(psn), [rs.b])
                self.stt(o_h.t[:, 0:n], o_h.t[:, 0:n], self.gng.t[:, 0:1], rs.t[:, 0:n], ALU.mult, ALU.mult,
                         self.B(o_h, self.gng, rs), [o_h.b])
                self.tt(self.oat[:, h, 0:n], o_h.t[:, 0:n], sgate.t[:, 0:n], ALU.mult, self.B(o_h, sgate),
                        [self.oab[h]])
                self.tap(f"oa{h}", self.oat[:, h, 0:n], [self.oab[h]], n)

    def tap(self, name, ap, bufs, n=None):
        full = f"{self.tile_tag}_{name}"
        if name not in self.debug and full not in self.debug:
            return
        shape = list(ap.shape)
        d = self.nc.dram_tensor("dbg_" + full, shape, ap.dtype, kind="ExternalOutput").ap()
        self.dbg_out["dbg_" + full] = shape
        self.dma("sp", "outq", d, ap, bufs, [])

    def rwkv_gen(self, n, C, full, all_xb):
        I = self.I
        nch = n // C
        RMAP = self.RMAP

        def rsl(i, nslots=1):
            return self.slot(RMAP[i], nslots)
        Hk = [self.Hh(k) for k in range(NKC)]
        mu, omu, hx, rwp = self.mu, self.omu, self.halo_x, self.rwp
        mk = self.masks

        def mask(i):
            return mk.t[0:C, i:i + 1, 0:C].to_broadcast([C, nch, C])

        def v3(t, w=None):
            w = w or C
            return t.t[0:C, 0:nch * w].rearrange("p (c j) -> p c j", j=w)

        def f3(t):
            return t.t[:, 0:n].rearrange("p (c j) -> p c j", j=C)

        def v3b(t):
            return t.t.bitcast(BF16)[0:C, 0:nch * C].rearrange("p (c j) -> p c j", j=C)

        def shift_mix(ps, M, blkid, rawslot, xmslot, need_xm=True):
            raw = rsl(rawslot)
            self.cp("dve", raw.t[0:M, 0:1], hx.t[0:M, blkid:blkid + 1], [hx.b], [raw.b])
            self.cp("act", raw.t[0:M, 1:1 + n], ps.t[0:M, 0:n], self.B(ps), [raw.b])
            self.cp("dve", hx.t[0:M, blkid:blkid + 1], raw.t[0:M, n:n + 1], [raw.b], [hx.b])
            if not need_xm:
                return None
            tmp = rsl(13)
            self.ts(tmp.t[0:M, 0:n], raw.t[0:M, 0:n], mu.t[0:M, blkid:blkid + 1], None, ALU.mult, None,
                    self.B(raw, mu), [tmp.b])
            xm = rsl(xmslot)
            self.stt(xm.t[0:M, 0:n], raw.t[0:M, 1:1 + n], omu.t[0:M, blkid:blkid + 1], tmp.t[0:M, 0:n],
                     ALU.mult, ALU.add, self.B(raw, omu, tmp), [xm.b])
            return xm

        slab = self.load_slab([(I["w_lora"], 0, 0)])
        lor = []
        for (blkid, col0, M, kind) in ((24, 0, 96, "xw"), (25, 96, 96, "xa"), (26, 192, 128, "xg0"), (27, 320, 128, "xg1")):
            isg = kind.startswith("xg")
            if isg and not (full or all_xb):
                continue
            ps = self.ps1()
            self.dense(ps, M, n, slab, col0, Hk)
            lor.append((blkid, M, kind, isg, ps))
        yield
        for (blkid, M, kind, isg, ps) in lor:
            xm = shift_mix(ps, M, blkid, 1, 4, need_xm=(full or not isg))
            if xm is None:
                continue
            if kind == "xw":
                self.act(self.txw.t[0:96, 0:n], xm.t[0:96, 0:n], AF.Tanh, [xm.b], [self.txw.b])
            elif kind == "xa":
                self.cp("dve", self.xab.t[0:96, 0:n], xm.t[0:96, 0:n], [xm.b], [self.xab.b])
            else:
                gi = 0 if kind == "xg0" else 1
                self.act(self.sxg.t[:, gi, 0:n], xm.t[:, 0:n], AF.Sigmoid, [xm.b], [self.sxg.b])
            yield
        import os
        ksub = int(os.environ.get("KSUB", "9"))
        if ksub < 2:
            return
        for i in range(8):
            bc = slice(i * 128, (i + 1) * 128)
            slab = self.load_slab([(I["w_rw"][:, i, :], 0, 0)])
            xm = {}
            for j in (0, 1, 2):
                if j == 0 and not (full or all_xb):
                    continue
                ps = self.ps1()
                self.dense(ps, 128, n, slab, j * 128, Hk)
                xm[j] = shift_mix(ps, 128, j * 8 + i, 1 + j, 4 + j, need_xm=(full or j != 0))
            r, k, v = xm.get(0), xm[1], xm[2]
            yield
            if ksub < 3:
                continue
            psw = self.ps1()
            self.mm(psw.t[:, 0:n], self.w2.t[0:96, bc], self.txw.t[0:96, 0:n], True, True,
                    self.B(self.w2, self.txw), self.B(psw))
            logw = rsl(7)
            self.act(logw.t[:, 0:n], psw.t[:, 0:n], AF.Sigmoid, self.B(psw, rwp), [logw.b], bias=rwp.t[:, 0, i:i + 1])
            self.ts(logw.t[:, 0:n], logw.t[:, 0:n], float(-np.exp(-0.5)), None, ALU.mult, None, [logw.b], [logw.b])
            Lc = rsl(8)
            rm = self.rmask[C]
            self.S.op("dve", lambda e, Lc=Lc, logw=logw, rm=rm: e.tensor_tensor_scan(
                out=Lc.t[:, 0:n], data0=rm.t[:, 0:n], data1=logw.t[:, 0:n], initial=0.0, op0=ALU.mult, op1=ALU.add),
                reads=self.B(logw, rm), writes=[Lc.b])
            psa = self.ps1()
            self.mm(psa.t[:, 0:n], self.a2.t[0:96, bc], self.xab.t[0:96, 0:n], True, True,
                    self.B(self.a2, self.xab), self.B(psa))
            a = rsl(9)
            self.act(a.t[:, 0:n], psa.t[:, 0:n], AF.Sigmoid, self.B(psa, rwp), [a.b], bias=rwp.t[:, 1, i:i + 1])
            if full:
                psg = self.ps1()
                self.mm(psg.t[:, 0:n], self.g2.t[:, 0, bc], self.sxg.t[:, 0, 0:n], True, False,
                        self.B(self.g2, self.sxg), self.B(psg))
                self.mm(psg.t[:, 0:n], self.g2.t[:, 1, bc], self.sxg.t[:, 1, 0:n], False, True,
                        self.B(self.g2, self.sxg), self.B(psg))
                gate = rsl(22)
                self.cp("act", gate.t[:, 0:n], psg.t[:, 0:n], self.B(psg), [gate.b])
            yield
            kk = rsl(10)
            self.ts(kk.t[:, 0:n], k.t[:, 0:n], rwp.t[:, 2, i:i + 1], None, ALU.mult, None, self.B(k, rwp), [kk.b])
            sq = rsl(0)
            sqb = sq.t.bitcast(BF16)
            self.act(sqb[:, 0:n], kk.t[:, 0:n], AF.Square, [kk.b], [sq.b])
            psn = self.ps1()
            self.mm(psn.t[:, 0:n], self.onesbd_bf.t[:], sqb[:, 0:n], True, True, self.B(self.onesbd_bf, sq), self.B(psn))
            rs = rsl(29)
            self.rsqrt(rs.t[:, 0:n], psn.t[:, 0:n], 1.0, EPS, self.B(psn), [rs.b])
            self.tt(kk.t[:, 0:n], kk.t[:, 0:n], rs.t[:, 0:n], ALU.mult, self.B(kk, rs), [kk.b])
            kh = rsl(11)
            self.ts(kh.t[:, 0:n], a.t[:, 0:n], rwp.t[:, 3, i:i + 1], self.omka.t[:, i:i + 1], ALU.mult, ALU.add,
                    self.B(a, rwp, self.omka), [kh.b])
            self.tt(kh.t[:, 0:n], kh.t[:, 0:n], k.t[:, 0:n], ALU.mult, self.B(kh, k), [kh.b])
            kka = rsl(12)
            self.tt(kka.t[:, 0:n], kk.t[:, 0:n], a.t[:, 0:n], ALU.mult, self.B(kk, a), [kka.b])
            yield
            Winv = rsl(14)
            self.act(Winv.t[:, 0:n], Lc.t[:, 0:n], AF.Exp, [Lc.b], [Winv.b], scale=-1.0)
            Wprev = rsl(15)
            self.tt(Wprev.t[:, 0:n], Lc.t[:, 0:n], logw.t[:, 0:n], ALU.subtract, self.B(Lc, logw), [Wprev.b])
            self.act(Wprev.t[:, 0:n], Wprev.t[:, 0:n], AF.Exp, [Wprev.b], [Wprev.b])
            AR = rsl(16, 2)
            AR4 = AR.t[:, 0:2 * n].rearrange("p (c two j) -> p c two j", two=2, j=C)
            self.tt(AR4[:, :, 0, :], f3(kk), f3(Wprev), ALU.mult, self.B(kk, Wprev), self.B(AR))
            if full:
                Wt = rsl(13)
                self.act(Wt.t[:, 0:n], Lc.t[:, 0:n], AF.Exp, [Lc.b], [Wt.b])
                self.tt(AR4[:, :, 1, :], f3(r), f3(Wt), ALU.mult, self.B(r, Wt), self.B(AR))
            btb, ktb, arb = self.btb, self.ktb, self.arb
            self.stt(btb.t[:, 0:n], kka.t[:, 0:n], -1.0, Winv.t[:, 0:n], ALU.mult, ALU.mult, self.B(kka, Winv), [btb.b])
            self.tt(ktb.t[:, 0:n], kh.t[:, 0:n], Winv.t[:, 0:n], ALU.mult, self.B(kh, Winv), [ktb.b])
            if full:
                self.cp("act", arb.t[:, 0:2 * n], AR.t[:, 0:2 * n], self.B(AR), [arb.b])
            else:
                arb4 = arb.t[:, 0:2 * n].rearrange("p (c two j) -> p c two j", two=2, j=C)
                self.cp("act", arb4[:, :, 0, :], AR4[:, :, 0, :], self.B(AR), [arb.b])
            edl = rsl(31)
            Lc3 = f3(Lc)
            self.tt(f3(edl), Lc3[:, :, C - 1:C].to_broadcast([128, nch, C]), Lc3, ALU.subtract, [Lc.b], [edl.b])
            self.act(edl.t[:, 0:n], edl.t[:, 0:n], AF.Exp, [edl.b], [edl.b])
            bdec = rsl(20)
            self.stt(bdec.t[:, 0:n], kka.t[:, 0:n], -1.0, edl.t[:, 0:n], ALU.mult, ALU.mult, self.B(kka, edl), [bdec.b])
            kdec = rsl(21)
            self.tt(kdec.t[:, 0:n], kh.t[:, 0:n], edl.t[:, 0:n], ALU.mult, self.B(kh, edl), [kdec.b])
            WC = self.WC
            self.act(WC.t[:, 0:nch].unsqueeze(2), Lc3[:, :, C - 1:C], AF.Exp, [Lc.b], [WC.b])
            if full:
                rk = rsl(0)
                rkb = rk.t.bitcast(BF16)
                self.stt(rkb[:, 0:n], r.t[:, 0:n], rwp.t[:, 4, i:i + 1], kh.t[:, 0:n], ALU.mult, ALU.mult,
                         self.B(r, rwp, kh), [rk.b])
                psb = self.ps1()
                self.mm(psb.t[:, 0:n], self.onesbd_bf.t[:], rkb[:, 0:n], True, True, self.B(self.onesbd_bf, rk), self.B(psb))
                bvv = rsl(30)
                self.tt(bvv.t[:, 0:n], psb.t[:, 0:n], v.t[:, 0:n], ALU.mult, self.B(psb, v), [bvv.b])
            if ksub < 4:
                continue
            yield
            tok = []
            for (src, s0) in ((v, 32), (bdec, 34), (kdec, 36)):
                dst = rsl(s0, 2)
                if nch * 128 <= 512:
                    pt = self.ps1()
                    ptt, ptb = pt.t, self.B(pt)
                else:
                    ptt, ptb = self.ps2()
                for c in range(nch):
                    self.tr(ptt[0:C, c * 128:(c + 1) * 128], src.t[:, c * C:(c + 1) * C], self.B(src), ptb)
                self.cp("act" if s0 == 34 else "dve", dst.t.bitcast(BF16)[0:C, 0:nch * 128], ptt[0:C, 0:nch * 128], ptb,
                        self.B(dst))
                tok.append(dst)
            Vt, Bt, Kt = tok
            Vt3, Bt3, Kt3 = (t.t.bitcast(BF16)[0:C, 0:nch * 128].rearrange("p (c d) -> p c d", d=128) for t in tok)
            if ksub < 5:
                continue
            Rr, RabT, AakT, RakT = [], [], [], []
            W2 = 2 * C if full else C
            pairs = []
            for hp in (0, 1):
                pr = slice(hp * 64, hp * 64 + 64)
                outs = []
                for gi_, (lt, slots_) in enumerate(((btb, (26 if hp == 0 else 48, 38 + 3 * hp)), (ktb, (39 + 3 * hp, 40 + 3 * hp)))):
                    if nch * W2 <= 512:
                        pg = self.ps1()
                        pgt, pgb = pg.t, self.B(pg)
                    else:
                        pgt, pgb = self.ps2()
                    for c in range(nch):
                        self.mm(pgt[0:C, c * W2:(c + 1) * W2], lt.t[pr, c * C:(c + 1) * C],
                                arb.t[pr, c * 2 * C:c * 2 * C + W2], True, True, self.B(lt, arb), pgb)
                    pg3 = pgt[0:C, 0:nch * W2].rearrange("p (c w) -> p c w", w=W2)
                    oA = rsl(slots_[0])
                    self.tt(v3b(oA), pg3[:, :, 0:C], mask(1), ALU.mult, pgb + [mk.b], [oA.b])
                    outs.append(oA)
                    if full:
                        oR = rsl(slots_[1])
                        self.tt(v3b(oR), pg3[:, :, C:2 * C], mask(0), ALU.mult, pgb + [mk.b], [oR.b])
                        outs.append(oR)
                    else:
                        outs.append(None)
                AabT, RabT_h, AakT_h, RakT_h = outs
                pg = self.ps1()
                for c in range(nch):
                    self.mm(pg.t[0:C, c * C:(c + 1) * C], arb.t[pr, c * 2 * C:c * 2 * C + C], btb.t[pr, c * C:(c + 1) * C],
                            True, True, self.B(arb, btb), self.B(pg))
                Aab = rsl(24 if hp == 0 else 50)
                self.tt(v3b(Aab), v3(pg), mask(2), ALU.mult, self.B(pg, mk), [Aab.b])
                pairs.append((AabT, Aab, (26, 27) if hp == 0 else (48, 49), (24, 25) if hp == 0 else (50, 51),
                              (44 + 2 * hp, 45 + 2 * hp)))
                RabT.append(RabT_h)
                AakT.append(AakT_h)
                RakT.append(RakT_h)
                yield
            Rr = self.neumann_multi(pairs, nch, C, sl=rsl)
            yield
            Pr = self.Pr[i]
            Pb = self.Prb[i % 2]
            self.cp("act", Pb.t[:, :], Pr.t[:, :], [Pr.b], [Pb.b])
            yfm = rsl(23)
            if ksub < 6:
                continue
            for c in range(nch):
                cs = slice(c * C, (c + 1) * C)
                psX = self.ps1()
                self.mm(psX.t[0:C, 0:128], arb.t[:, c * 2 * C:c * 2 * C + C], Pb.t[:, :], True, False,
                        self.B(arb, Pb), self.B(psX))
                for hp in (0, 1):
                    hc = slice(hp * 64, hp * 64 + 64)
                    self.mm(psX.t[0:C, hc], v3b(AakT[hp])[:, c, :], Vt3[:, c, hc], False, hp == 1,
                            self.B(AakT[hp], Vt), self.B(psX))
                Xs = self.Xs[c % 2]
                self.cp("act", Xs.t[0:C, :], psX.t[0:C, 0:128], self.B(psX), [Xs.b])
                yield
                psU = self.ps1()
                for hp in (0, 1):
                    hc = slice(hp * 64, hp * 64 + 64)
                    self.mm(psU.t[0:C, hc], v3b(Rr[hp])[:, c, :], Xs.t[0:C, hc], True, True, self.B(Rr[hp], Xs), self.B(psU))
                Us = self.Us[c % 2]
                self.cp("dve", Us.t[0:C, :], psU.t[0:C, 0:128], self.B(psU), [Us.b])
                yield
                if full:
                    psY = self.ps1()
                    self.mm(psY.t[0:C, 0:128], arb.t[:, c * 2 * C + C:(c + 1) * 2 * C], Pb.t[:, :], True, False,
                            self.B(arb, Pb), self.B(psY))
                    for hp in (0, 1):
                        hc = slice(hp * 64, hp * 64 + 64)
                        self.mm(psY.t[0:C, hc], v3b(RabT[hp])[:, c, :], Us.t[0:C, hc], False, False,
                                self.B(RabT[hp], Us), self.B(psY))
                        self.mm(psY.t[0:C, hc], v3b(RakT[hp])[:, c, :], Vt3[:, c, hc], False, hp == 1,
                                self.B(RakT[hp], Vt), self.B(psY))
                    Ys = self.Ys[c % 2]
                    self.cp("dve", Ys.t[0:C, :], psY.t[0:C, 0:128], self.B(psY), [Ys.b])
                    psT = self.ps1()
                    self.tr(psT.t[:, 0:C], Ys.t[0:C, :], [Ys.b], self.B(psT))
                    self.cp("act", yfm.t[:, cs], psT.t[:, 0:C], self.B(psT), [yfm.b])
                psP = self.ps1()
                self.mm(psP.t[:, 0:128], Bt3[:, c, :], Us.t[0:C, :], True, False, self.B(Bt, Us), self.B(psP))
                self.mm(psP.t[:, 0:128], Kt3[:, c, :], Vt3[:, c, :], False, True, self.B(Kt, Vt), self.B(psP))
                for hp in (0, 1):
                    pr = slice(hp * 64, hp * 64 + 64)
                    self.stt(Pr.t[pr, pr], Pr.t[pr, pr], WC.t[pr, c:c + 1], psP.t[pr, pr],
                             ALU.mult, ALU.add, self.B(Pr, WC, psP), [Pr.b])
                if c < nch - 1:
                    self.cp("act", Pb.t[:, :], Pr.t[:, :], [Pr.b], [Pb.b])
                yield
            if full and ksub >= 7:
                psm = self.ps1()
                self.mm(psm.t[:, 0:n], self.onesbd.t[:], yfm.t[:, 0:n], True, True, self.B(self.onesbd, yfm), self.B(psm))
                self.stt(yfm.t[:, 0:n], psm.t[:, 0:n], -1.0 / 64, yfm.t[:, 0:n], ALU.mult, ALU.add,
                         self.B(psm, yfm), [yfm.b])
                sq = rsl(0)
                sqb = sq.t.bitcast(BF16)
                self.act(sqb[:, 0:n], yfm.t[:, 0:n], AF.Square, [yfm.b], [sq.b])
                psv = self.ps1()
                self.mm(psv.t[:, 0:n], self.onesbd_bf.t[:], sqb[:, 0:n], True, True, self.B(self.onesbd_bf, sq), self.B(psv))
                rs = rsl(29)
                self.rsqrt(rs.t[:, 0:n], psv.t[:, 0:n], 1.0 / 64, LNX_EPS, self.B(psv), [rs.b])
                self.tt(yfm.t[:, 0:n], yfm.t[:, 0:n], rs.t[:, 0:n], ALU.mult, self.B(yfm, rs), [yfm.b])
                self.ts(yfm.t[:, 0:n], yfm.t[:, 0:n], rwp.t[:, 5, i:i + 1], rwp.t[:, 6, i:i + 1], ALU.mult, ALU.add,
                        self.B(yfm, rwp), [yfm.b])
                self.tt(yfm.t[:, 0:n], yfm.t[:, 0:n], bvv.t[:, 0:n], ALU.add, self.B(yfm, bvv), [yfm.b])
                self.tt(self.obt[:, i, 0:n], yfm.t[:, 0:n], gate.t[:, 0:n], ALU.mult, self.B(yfm, gate), [self.obb[i]])
                self.tap(f"ob{i}", self.obt[:, i, 0:n], [self.obb[i]], n)

    def tail(self, n, pT_src, yT_dst):
        I = self.I
        Hk = [self.Hh(k) for k in range(NKC)]
        OA = [TL(self.oat[:, k, :], self.oab[k]) for k in range(8)]
        OB = [TL(self.obt[:, k, :], self.obb[k]) for k in range(8)]
        MG = [TL(self.mgt[:, k, :], self.mgb[k]) for k in range(NKC)]
        pT = self.pT
        self.dma("pool", "pin", pT.t[:, :, 0:n], pT_src.rearrange("(k p) t -> p k t", p=128), [], [pT.b])
        for q in range(4):
            qc = slice(q * 512, (q + 1) * 512)
            sgs = {}
            for gi in (0, 1):
                sG = self.load_slab([(I["w_gm"][:, gi, qc], 0, 0)])
                for jj in range(4):
                    pg = self.ps1()
                    self.dense(pg, 128, n, sG, jj * 128, Hk)
                    sg = self.slot(20 + gi * 4 + jj)
                    self.act(sg.t[:, 0:n], pg.t[:, 0:n], AF.Sigmoid, self.B(pg), [sg.b])
                    sgs[(gi, jj)] = sg
            sA = self.load_slab([(I["w_up_a"][:, qc], 0, 0), (I["w_up_b"][:, qc], 8, 0)])
            for jj in range(4):
                j = q * 4 + jj
                pa = self.ps1()
                self.dense(pa, 128, n, sA, jj * 128, OA, kc0=0)
                pb = self.ps1()
                self.dense(pb, 128, n, sA, jj * 128, OB, kc0=8)
                sa, sb = sgs[(0, jj)], sgs[(1, jj)]
                self.tt(sa.t[:, 0:n], pa.t[:, 0:n], sa.t[:, 0:n], ALU.mult, self.B(pa, sa), [sa.b])
                self.tt(sb.t[:, 0:n], pb.t[:, 0:n], sb.t[:, 0:n], ALU.mult, self.B(pb, sb), [sb.b])
                self.tt(self.mgt[:, j, 0:n], sa.t[:, 0:n], sb.t[:, 0:n], ALU.add, self.B(sa, sb), [self.mgb[j]])
        for q in range(4):
            s = self.load_slab([(I["w_o"][:, q * 512:(q + 1) * 512], 0, 0)])
            for jj in range(4):
                j = q * 4 + jj
                ps = self.ps1()
                self.dense(ps, 128, n, s, jj * 128, MG)
                xj = self.X(j)
                self.tt(xj.t[:, 0:n], xj.t[:, 0:n], ps.t[:, 0:n], ALU.add, self.B(xj, ps), [xj.b])
        self.tap("x1", self.xt[:, :, 0:n], self.xb)
        self.rmsnorm_x(n, 1)
        for g in range(4):
            for q in range(4):
                c0 = g * 2048 + q * 512
                s = self.load_slab([(I["w_ff1"][:, c0:c0 + 512], 0, 0)])
                for jj in range(4):
                    jh = q * 4 + jj
                    ps = self.ps1()
                    self.dense(ps, 128, n, s, jj * 128, Hk)
                    r = self.slot(1 + jj % 2)
                    self.act(r.t[:, 0:n], ps.t[:, 0:n], AF.Relu, self.B(ps), [r.b])
                    self.tt(self.mgt[:, jh, 0:n], r.t[:, 0:n], r.t[:, 0:n], ALU.mult, [r.b], [self.mgb[jh]])
            for q in range(4):
                s = self.load_slab([(I["w_ff2"][g * 2048:(g + 1) * 2048, q * 512:(q + 1) * 512], 0, 0)])
                for jj in range(4):
                    j = q * 4 + jj
                    ps = self.ps1()
                    self.dense(ps, 128, n, s, jj * 128, MG)
                    xj = self.X(j)
                    self.tt(xj.t[:, 0:n], xj.t[:, 0:n], ps.t[:, 0:n], ALU.add, self.B(xj, ps), [xj.b])
        self.tap("x2", self.xt[:, :, 0:n], self.xb)
        self.rmsnorm_x(n, 2)
        P2 = [TL(pT.t[:, 0, :], pT.b), TL(pT.t[:, 1, :], pT.b)]
        for q in range(4):
            qc = slice(q * 512, (q + 1) * 512)
            s = self.load_slab([(I["w_ple_gate"][:, qc], 0, 0)])
            s2 = self.load_slab([(I["w_ple"][:, qc], 0, 0)])
            for jj in range(4):
                j = q * 4 + jj
                pg = self.ps1()
                self.dense(pg, 128, n, s, jj * 128, Hk)
                pp = self.ps1()
                self.dense(pp, 128, n, s2, jj * 128, P2)
                sg = self.slot(1 + jj % 2)
                self.act(sg.t[:, 0:n], pg.t[:, 0:n], AF.Sigmoid, self.B(pg), [sg.b])
                self.tt(sg.t[:, 0:n], sg.t[:, 0:n], pp.t[:, 0:n], ALU.mult, self.B(sg, pp), [sg.b])
                xj = self.X(j)
                self.tt(xj.t[:, 0:n], xj.t[:, 0:n], sg.t[:, 0:n], ALU.add, self.B(xj, sg), [xj.b])
        self.rmsnorm_x(n, 3, out_fn=lambda k: self.slot(4 + k))
        for k in range(NKC):
            sk = self.slot(4 + k)
            self.dma("sp", "outq", yT_dst[k * 128:(k + 1) * 128, :], sk.t[:, 0:n], [sk.b], [])

    def tile(self, tag, xT_src, n, C, full, all_xb=False, pT_src=None, yT_dst=None):
        self.tile_tag = tag
        self.dma("sp", "xin", self.xt[:, :, 0:n], xT_src.rearrange("(k p) t -> p k t", p=128), [], self.xb)
        import os
        stage = int(os.environ.get("KSTAGE", "9"))
        self.rmsnorm_x(n, 0)
        self.tap("h", self.ht[:, :, 0:n], self.hb)
        gens = []
        if stage >= 2:
            gens.append(self.gdn_gen(n, C, full, all_xb))
        if stage >= 3:
            gens.append(self.rwkv_gen(n, C, full, all_xb))
        if os.environ.get("KSEQ"):
            for g in gens:
                for _ in g:
                    pass
            gens = []
        stride = int(os.environ.get("KSTRIDE", "1"))
        gid = {id(g): k for k, g in enumerate(gens)}
        while gens:
            for g in list(gens):
                self.psg = gid[id(g)] if len(gid) > 1 else None
                try:
                    for _ in range(stride):
                        next(g)
                except StopIteration:
                    gens.remove(g)
        self.psg = None
        if full and stage >= 4:
            self.tail(n, pT_src, yT_dst)

    def build(self):
        NT = self.NT
        self.declare_io()
        self.init_psum()
        self.init_slabs(3)
        self.setup_consts()
        self.alloc_state()
        self.alloc_work()
        self.zero_state()
        I = self.I
        for t in range(self.n_pre):
            ts_ = slice(t * NT, (t + 1) * NT)
            self.tile(f"pre{t}", I["xT_pre"][:, ts_], NT, 64, False, all_xb=(t == self.n_pre - 1))
        for t in range(self.n_main):
            ts_ = slice(t * NT, (t + 1) * NT)
            self.tile(f"main{t}", I["xT_main"][:, ts_], NT, 64, True, pT_src=I["pT_main"][:, ts_],
                      yT_dst=I["yT_main"][:, ts_])
        self.store_state("p")
        if self.with_sample:
            self.load_state()
            self.tile("samp", I["xT_s"], 32, 32, True, pT_src=I["pT_s"], yT_dst=I["yT_s"])
            self.store_state("s")
        self.S.emit(final_streams=["outq"])
        return self.nc


def _blk(v):
    return np.ascontiguousarray(v.reshape(-1, 128).T)


def _shift_layout(v):
    out = np.zeros((128, 28), np.float32)
    out[:, 0:24] = v[0:3072].reshape(24, 128).T
    out[0:96, 24] = v[3072:3168]
    out[0:96, 25] = v[3168:3264]
    out[:, 26] = v[3264:3392]
    out[:, 27] = v[3392:3520]
    return out


def _shift_unlayout(a):
    v = np.zeros(3520, np.float32)
    v[0:3072] = a[:, 0:24].T.reshape(-1)
    v[3072:3168] = a[0:96, 24]
    v[3168:3264] = a[0:96, 25]
    v[3264:3392] = a[:, 26]
    v[3392:3520] = a[:, 27]
    return v


def prep_weights(w):
    f = np.float32
    w_in = w["w_in"]
    o = {}
    g = np.empty((D, 8, 512), f)
    for h in range(8):
        g[:, h, 0:128] = w_in[:, h * 128:(h + 1) * 128]
        g[:, h, 128:256] = w_in[:, 1024 + h * 128:1024 + (h + 1) * 128]
        g[:, h, 256:384] = w_in[:, 2048 + h * 128:2048 + (h + 1) * 128]
        g[:, h, 384:512] = w_in[:, 3088 + h * 128:3088 + (h + 1) * 128]
    o["w_gdn"] = g
    o["w_ab"] = np.ascontiguousarray(w_in[:, 3072:3088])
    r = np.empty((D, 8, 384), f)
    for i in range(8):
        r[:, i, 0:128] = w_in[:, 4112 + i * 128:4112 + (i + 1) * 128]
        r[:, i, 128:256] = w_in[:, 5136 + i * 128:5136 + (i + 1) * 128]
        r[:, i, 256:384] = w_in[:, 6160 + i * 128:6160 + (i + 1) * 128]
    o["w_rw"] = r
    o["w_lora"] = np.ascontiguousarray(w_in[:, 7184:7632])
    o["w_gm"] = np.ascontiguousarray(w_in[:, 7632:11728].reshape(D, 2, 2048))
    for k in ("w_up_a", "w_up_b", "w_o", "w_ff1", "w_ff2", "w_ple_gate", "w_ple", "w2", "a2", "g2"):
        o[k] = np.ascontiguousarray(w[k], dtype=f)
    gv = np.empty((128, 4, 16), f)
    for i, k in enumerate(("g_mix", "g_mlp", "g_ple", "g_final")):
        gv[:, i, :] = _blk(w[k])
    o["gvec"] = gv
    cw = w["conv_w"]
    o["convw"] = np.ascontiguousarray(cw.reshape(4, 24, 128).transpose(2, 1, 0))
    o["mu"] = _shift_layout(w["mu_shift"])
    rp = np.empty((128, 7, 8), f)
    for i, k in enumerate(("w0", "a0", "k_k", "k_a", "r_k", "lnx_g", "lnx_b")):
        rp[:, i, :] = _blk(w[k].reshape(-1))
    o["rwp"] = rp
    o["gng"] = np.ascontiguousarray(w["gdn_norm_g"].reshape(128, 1))
    o["adt"] = np.ascontiguousarray(np.stack([w["a_log"], w["dt_bias"]], axis=1))
    return o


def state_to_dev(sg, conv, sr, sh):
    return {
        "sg_in": np.ascontiguousarray(sg.transpose(1, 0, 2)),
        "conv_in": np.ascontiguousarray(conv.reshape(3, 24, 128).transpose(2, 1, 0)),
        "sr_in": np.ascontiguousarray(sr.reshape(8, 2, 64, 64).transpose(1, 3, 0, 2).reshape(128, 8, 64)),
        "sh_in": _shift_layout(sh),
    }


def state_from_dev(sg, conv, sr, sh):
    return (np.ascontiguousarray(sg.transpose(1, 0, 2)),
            np.ascontiguousarray(conv.transpose(2, 1, 0).reshape(3, 3072)),
            np.ascontiguousarray(sr.reshape(2, 64, 8, 64).transpose(2, 0, 3, 1).reshape(16, 64, 64)),
            _shift_unlayout(sh))


_NC_CACHE = {}


def get_program(NT, n_pre, n_main, with_sample, debug=()):
    key = (NT, n_pre, n_main, with_sample, tuple(debug))
    if key not in _NC_CACHE:
        b = Builder(NT, n_pre, n_main, with_sample, debug)
        b.build()
        _NC_CACHE[key] = b
    return _NC_CACHE[key]


def kernel(**inputs):
    NT = 256
    f = np.float32
    x_prompt = np.asarray(inputs["x_prompt"], f)
    x_sample = np.asarray(inputs["x_sample"], f)
    p_prompt = np.asarray(inputs["p_prompt"], f)[0]
    p_sample = np.asarray(inputs["p_sample"], f)[0]
    B, T, _ = x_prompt.shape
    half = T // 2
    n_main = half // NT
    w = {k: np.asarray(v, f)[0] for k, v in inputs.items()
         if k not in ("x_prompt", "x_sample", "p_prompt", "p_sample", "state_gdn", "cache_gdn_conv",
                      "state_rwkv", "cache_rwkv_shift", "g_final")}
    w["g_final"] = np.asarray(inputs["g_final"], f)
    wd = prep_weights(w)
    bld = get_program(NT, n_main, n_main, True)
    in_maps = []
    zeros_pre = np.zeros((D, half), f)
    for c in range(8):
        b, hf = c // 2, c % 2
        m = dict(wd)
        xT = np.ascontiguousarray(x_prompt[b].T)
        m["xT_pre"] = np.ascontiguousarray(xT[:, 0:half]) if hf == 1 else zeros_pre
        m["xT_main"] = np.ascontiguousarray(xT[:, hf * half:(hf + 1) * half])
        m["pT_main"] = np.ascontiguousarray(p_prompt[b, hf * half:(hf + 1) * half].T)
        m["xT_s"] = np.ascontiguousarray(x_sample[c].T)
        m["pT_s"] = np.ascontiguousarray(p_sample[c].T)
        m.update(state_to_dev(np.asarray(inputs["state_gdn"], f)[0, c], np.asarray(inputs["cache_gdn_conv"], f)[0, c],
                              np.asarray(inputs["state_rwkv"], f)[0, c], np.asarray(inputs["cache_rwkv_shift"], f)[0, c]))
        in_maps.append(m)
    res = run_bass_kernel_spmd(bld.nc, in_maps, core_ids=list(range(8)))
    R = res.results
    y_prompt = np.empty((B, T, D), f)
    y_sample = np.empty((8, 32, D), f)
    sg_p = np.empty((1, B, 8, 128, 128), f)
    conv_p = np.empty((1, B, 3, 3072), f)
    sr_p = np.empty((1, B, 16, 64, 64), f)
    sh_p = np.empty((1, B, 3520), f)
    sg_s = np.empty((1, 8, 8, 128, 128), f)
    conv_s = np.empty((1, 8, 3, 3072), f)
    sr_s = np.empty((1, 8, 16, 64, 64), f)
    sh_s = np.empty((1, 8, 3520), f)
    for c in range(8):
        b, hf = c // 2, c % 2
        r = R[c]
        y_prompt[b, hf * half:(hf + 1) * half] = r["yT_main"].T
        y_sample[c] = r["yT_s"].T
        if hf == 1:
            sg_p[0, b], conv_p[0, b], sr_p[0, b], sh_p[0, b] = state_from_dev(r["sg_p"], r["conv_p"], r["sr_p"], r["sh_p"])
        sg_s[0, c], conv_s[0, c], sr_s[0, c], sh_s[0, c] = state_from_dev(r["sg_s"], r["conv_s"], r["sr_s"], r["sh_s"])
    return (y_prompt, y_sample, sg_p, conv_p, sr_p, sh_p, sg_s, conv_s, sr_s, sh_s)
```
